# Optimizing a Trainium2 kernel written in Bass

```python
import math
import jax
import jax.numpy as jnp
from jax import lax
import numpy as np

D_MODEL = 1024
BATCH = 4
SEQ = 4096
DEPTH = 2
DEC_BATCH = 32
DEC_SEQ = 4
PAST_LEN = 8192
PAGE_SIZE = 128

GROUP_W = D_MODEL // 4
LRU_W = GROUP_W
LRU_BLOCKS = 4
LRU_CONV = 4
LRU_C = 8.0
DIFF_HEADS = 4
DIFF_V = GROUP_W // DIFF_HEADS
DIFF_QK = DIFF_V // 2
DSA_HEADS = 4
DSA_HD = GROUP_W // DSA_HEADS
IDX_HEADS = 8
IDX_DIM = 32
TOPK_MAX = 256
GLA_HEADS = 4
GLA_DK = GROUP_W // 2 // GLA_HEADS
GLA_DV = GROUP_W // GLA_HEADS
GLA_RANK = 16
GLA_TAU = 16.0
GLA_CHUNK = 64
D_FF = 2816
FFN_CONV = 3
ROPE_THETA = 10000.0
EPS = 1e-6
Q_BLOCK = 128

_IN_SPLITS = (
    ("lru_x", LRU_W), ("lru_gate", LRU_W),
    ("diff_q", DIFF_HEADS * 2 * DIFF_QK), ("diff_k", DIFF_HEADS * 2 * DIFF_QK), ("diff_v", DIFF_HEADS * DIFF_V),
    ("dsa_q", DSA_HEADS * DSA_HD), ("dsa_k", DSA_HEADS * DSA_HD), ("dsa_v", DSA_HEADS * DSA_HD),
    ("idx_q", IDX_HEADS * IDX_DIM), ("idx_k", IDX_DIM), ("idx_w", IDX_HEADS),
    ("gla_q", GLA_HEADS * GLA_DK), ("gla_k", GLA_HEADS * GLA_DK), ("gla_v", GLA_HEADS * GLA_DV),
    ("gla_g", GLA_HEADS * GLA_DV), ("gla_a", GLA_RANK),
)
D_IN = sum(w for _, w in _IN_SPLITS)
STATE_KEYS = ("diff_k", "diff_v", "dsa_k", "dsa_v", "idx_k", "lru_h", "lru_conv", "gla", "ffn_conv")

kernel_name = "hymba_lru_diff_dsa_gla_decode_step"


def split_cols(proj):
    out = {}
    off = 0
    for name, w in _IN_SPLITS:
        out[name] = proj[..., off:off + w]
        off += w
    return out


def rmsnorm(x, g):
    xf = x.astype(jnp.float32)
    y = xf * lax.rsqrt(jnp.mean(xf * xf, axis=-1, keepdims=True) + EPS)
    return (y * g.astype(jnp.float32)).astype(x.dtype)


def rope(x, pos):
    half = x.shape[-1] // 2
    inv = ROPE_THETA ** (-jnp.arange(half, dtype=jnp.float32) / half)
    ang = pos.astype(jnp.float32)[:, None] * inv[None, :]
    shp = (pos.shape[0],) + (1,) * (x.ndim - 3) + (half,)
    cos = jnp.cos(ang).reshape(shp)
    sin = jnp.sin(ang).reshape(shp)
    xf = x.astype(jnp.float32)
    x1, x2 = xf[..., :half], xf[..., half:]
    return jnp.concatenate([x1 * cos - x2 * sin, x1 * sin + x2 * cos], axis=-1).astype(x.dtype)


def causal_dwconv(x, buf, w, b):
    K = w.shape[0]
    T = x.shape[1]
    xx = jnp.concatenate([buf.astype(x.dtype), x], axis=1)
    y = b + sum(xx[:, j:j + T] * w[j] for j in range(K))
    return y, xx[:, T:]


def _lin_comb(e1, e2):
    a1, b1 = e1
    a2, b2 = e2
    return a1 * a2, a2 * b1 + b2


def rg_lru(x, h0, pos, w_a, b_a, w_x, b_x, lam):
    Bn, T, W = x.shape
    xb = x.reshape(Bn, T, LRU_BLOCKS, W // LRU_BLOCKS)
    gate_a = jax.nn.sigmoid((jnp.einsum("btni,nij->btnj", xb, w_a).reshape(Bn, T, W) + b_a).astype(jnp.float32))
    gate_x = jax.nn.sigmoid((jnp.einsum("btni,nij->btnj", xb, w_x).reshape(Bn, T, W) + b_x).astype(jnp.float32))
    log_a = -LRU_C * gate_a * jax.nn.softplus(-lam.astype(jnp.float32))
    a = jnp.exp(log_a)
    mult = jnp.sqrt(-jnp.expm1(2.0 * log_a))
    mult = jnp.where((pos == 0)[None, :, None], 1.0, mult)
    u = mult * gate_x * x.astype(jnp.float32)
    u = u.at[:, 0].add(a[:, 0] * h0.astype(jnp.float32))
    _, h = lax.associative_scan(_lin_comb, (a, u), axis=1)
    return h, h[:, -1]


def over_query_blocks(fn, q_arrays, q_pos):
    T = q_pos.shape[0]
    nb = T // Q_BLOCK

    def blk(a):
        return jnp.swapaxes(a.reshape(a.shape[0], nb, Q_BLOCK, *a.shape[2:]), 0, 1)

    out = lax.map(lambda args: fn(*args), tuple(blk(a) for a in q_arrays) + (q_pos.reshape(nb, Q_BLOCK),))
    out = jnp.swapaxes(out, 0, 1)
    return out.reshape(out.shape[0], T, *out.shape[3:])


def diff_attend(q, k, v, lam, q_pos, k_pos):
    s = jnp.einsum("bqhmd,bshmd->bhmqs", q, k).astype(jnp.float32) * DIFF_QK ** -0.5
    s = jnp.where(k_pos[None, :] <= q_pos[:, None], s, -jnp.inf)
    p = jax.nn.softmax(s, axis=-1)
    w = p[:, :, 0] - lam * p[:, :, 1]
    return jnp.einsum("bhqs,bshd->bqhd", w.astype(v.dtype), v)


def take_rows(x, idx):
    return jax.vmap(lambda xx, ii: xx[ii])(x, idx)


def gather_pages(pool, page_table):
    g = pool[page_table]
    return g.reshape(g.shape[0], g.shape[1] * g.shape[2], *g.shape[3:])


def dsa_select(iq, iw, ik, q_pos, k_pos, topk):
    sc = jnp.einsum("bqhd,bsd->bqhs", iq, ik).astype(jnp.float32) * IDX_DIM ** -0.5
    score = jnp.einsum("bqh,bqhs->bqs", iw.astype(jnp.float32) * IDX_HEADS ** -0.5, jax.nn.relu(sc))
    score = jnp.where(k_pos[None, :] <= q_pos[:, None], score, -jnp.inf)
    val, idx = lax.top_k(score, topk)
    return idx, jnp.isfinite(val)


def dsa_attend(q, kg, vg, valid):
    s = jnp.einsum("bqhd,bqkhd->bhqk", q, kg).astype(jnp.float32) * DSA_HD ** -0.5
    s = jnp.where(valid[:, None], s, -jnp.inf)
    p = jax.nn.softmax(s, axis=-1)
    return jnp.einsum("bhqk,bqkhd->bqhd", p.astype(vg.dtype), vg)


def gla(q, k, v, log_a, s0):
    Bn, T, H, dk = q.shape
    f32 = jnp.float32
    C = GLA_CHUNK if T % GLA_CHUNK == 0 else T
    n = T // C

    def to_chunks(a):
        return jnp.swapaxes(a.astype(f32).reshape(Bn, n, C, *a.shape[2:]), 0, 1)

    qc, kc, vc, ac = (to_chunks(a) for a in (q * dk ** -0.5, k, v, log_a))
    causal = jnp.tril(jnp.ones((C, C), dtype=bool))[None, :, :, None, None]

    def step(S, inp):
        qi, ki, vi, ai = inp
        b = jnp.cumsum(ai, axis=1)
        o_inter = jnp.einsum("bthk,bhkv->bthv", qi * jnp.exp(b), S)
        decay = jnp.exp(jnp.where(causal, b[:, :, None] - b[:, None, :], -jnp.inf))
        att = jnp.einsum("bthk,bshk,btshk->bhts", qi, ki, decay)
        o_intra = jnp.einsum("bhts,bshv->bthv", att, vi)
        b_end = b[:, -1]
        S = jnp.exp(b_end)[..., None] * S + jnp.einsum("bshk,bshv->bhkv", ki * jnp.exp(b_end[:, None] - b), vi)
        return S, o_inter + o_intra

    s_last, o = lax.scan(step, s0.astype(f32), (qc, kc, vc, ac))
    return jnp.swapaxes(o, 0, 1).reshape(Bn, T, H, v.shape[-1]), s_last


def decoder_layer(x, c, lp, l, pos, past, page_table):
    Bn, T, _ = x.shape
    f32 = jnp.float32
    prompt = past is None
    new = {}
    mod = (jax.nn.silu(c) @ lp["w_ada"] + lp["b_ada"])[:, None, :]
    sh1, sc1, g1, sh2, sc2, g2 = jnp.split(mod, 6, axis=-1)

    h = rmsnorm(x, lp["norm1_g"]) * (1.0 + sc1) + sh1
    p = split_cols(h @ lp["w_in"])

    lru_buf = jnp.zeros((Bn, LRU_CONV - 1, LRU_W), x.dtype) if prompt else past["lru_conv"]
    lru_h0 = jnp.zeros((Bn, LRU_W), x.dtype) if prompt else past["lru_h"]
    xa, new["lru_conv"] = causal_dwconv(p["lru_x"], lru_buf, lp["lru_conv_w"], lp["lru_conv_b"])
    hs, h_last = rg_lru(xa, lru_h0, pos, lp["lru_wa"], lp["lru_ba"], lp["lru_wx"], lp["lru_bx"], lp["lru_lambda"])
    new["lru_h"] = h_last.astype(x.dtype)
    o_a = hs.astype(x.dtype) * jax.nn.gelu(p["lru_gate"])

    q_b = rope(p["diff_q"].reshape(Bn, T, DIFF_HEADS, 2, DIFF_QK), pos)
    k_b = rope(p["diff_k"].reshape(Bn, T, DIFF_HEADS, 2, DIFF_QK), pos)
    v_b = p["diff_v"].reshape(Bn, T, DIFF_HEADS, DIFF_V)
    new["diff_k"], new["diff_v"] = k_b, v_b
    lam_init = 0.8 - 0.6 * math.exp(-0.3 * l)
    lam = (jnp.exp(jnp.sum(lp["diff_lq1"].astype(f32) * lp["diff_lk1"].astype(f32)))
           - jnp.exp(jnp.sum(lp["diff_lq2"].astype(f32) * lp["diff_lk2"].astype(f32))) + lam_init)
    if prompt:
        o_b = over_query_blocks(lambda qb, pb: diff_attend(qb, k_b, v_b, lam, pb, pos), (q_b,), pos)
    else:
        kk = jnp.concatenate([gather_pages(past["diff_k"], page_table), k_b], axis=1)
        vv = jnp.concatenate([gather_pages(past["diff_v"], page_table), v_b], axis=1)
        o_b = diff_attend(q_b, kk, vv, lam, pos, jnp.arange(kk.shape[1], dtype=jnp.int32))
    o_b = (rmsnorm(o_b, lp["diff_subln_g"]) * (1.0 - lam_init)).reshape(Bn, T, GROUP_W)

    q_c = rope(p["dsa_q"].reshape(Bn, T, DSA_HEADS, DSA_HD), pos)
    k_c = rope(p["dsa_k"].reshape(Bn, T, DSA_HEADS, DSA_HD), pos)
    v_c = p["dsa_v"].reshape(Bn, T, DSA_HEADS, DSA_HD)
    iq = rope(p["idx_q"].reshape(Bn, T, IDX_HEADS, IDX_DIM), pos)
    ik = rope(p["idx_k"], pos)
    iw = p["idx_w"]
    new["dsa_k"], new["dsa_v"], new["idx_k"] = k_c, v_c, ik
    if prompt:
        topk = min(TOPK_MAX, T // 4)

        def dsa_block(qb, iqb, iwb, pb):
            idx, valid = dsa_select(iqb, iwb, ik, pb, pos, topk)
            return dsa_attend(qb, take_rows(k_c, idx), take_rows(v_c, idx), valid)

        o_c = over_query_blocks(dsa_block, (q_c, iq, iw), pos)
    else:
        past_len = page_table.shape[1] * PAGE_SIZE
        n_keys = past_len + T
        topk = min(TOPK_MAX, n_keys // 4)
        ik_all = jnp.concatenate([gather_pages(past["idx_k"], page_table), ik], axis=1)
        idx, valid = dsa_select(iq, iw, ik_all, pos, jnp.arange(n_keys, dtype=jnp.int32), topk)
        from_past = (idx < past_len)[..., None, None]
        pidx = jnp.minimum(idx, past_len - 1)
        phys = take_rows(page_table, pidx // PAGE_SIZE)
        off = pidx % PAGE_SIZE
        nidx = jnp.clip(idx - past_len, 0, T - 1)
        kg = jnp.where(from_past, past["dsa_k"][phys, off], take_rows(k_c, nidx))
        vg = jnp.where(from_past, past["dsa_v"][phys, off], take_rows(v_c, nidx))
        o_c = dsa_attend(q_c, kg, vg, valid)
    o_c = o_c.reshape(Bn, T, GROUP_W)

    q_d = p["gla_q"].reshape(Bn, T, GLA_HEADS, GLA_DK)
    k_d = p["gla_k"].reshape(Bn, T, GLA_HEADS, GLA_DK)
    v_d = p["gla_v"].reshape(Bn, T, GLA_HEADS, GLA_DV)
    log_alpha = (jax.nn.log_sigmoid((p["gla_a"] @ lp["gla_wa2"] + lp["gla_ba"]).astype(f32)) / GLA_TAU
                 ).reshape(Bn, T, GLA_HEADS, GLA_DK)
    s0 = jnp.zeros((Bn, GLA_HEADS, GLA_DK, GLA_DV), f32) if prompt else past["gla"]
    o_g, s_last = gla(q_d, k_d, v_d, log_alpha, s0)
    new["gla"] = s_last.astype(x.dtype)
    o_d = (rmsnorm(o_g.astype(x.dtype), lp["gla_norm_g"])
           * jax.nn.silu(p["gla_g"].reshape(Bn, T, GLA_HEADS, GLA_DV))).reshape(Bn, T, GROUP_W)

    mix = jnp.concatenate([o_a, o_b, o_c, o_d], axis=-1) @ lp["w_out"]
    x = x + g1 * mix

    h = rmsnorm(x, lp["norm2_g"]) * (1.0 + sc2) + sh2
    ffn_buf = jnp.zeros((Bn, FFN_CONV - 1, D_FF), x.dtype) if prompt else past["ffn_conv"]
    z, new["ffn_conv"] = causal_dwconv(h @ lp["ffn_w_gate"], ffn_buf, lp["ffn_conv_w"], lp["ffn_conv_b"])
    y = (jax.nn.silu(z) * (h @ lp["ffn_w_up"])) @ lp["ffn_w_down"]
    x = x + g2 * y
    return x, new


def setup_inputs(seed: int = 0) -> dict:
    key = jax.random.key(seed)
    keys = jax.random.split(key, 64)
    counter = [0]

    def nk():
        k = keys[counter[0]]
        counter[0] += 1
        return k

    f32 = jnp.float32

    def nrm(shape, scale=1.0):
        return scale * jax.random.normal(nk(), shape, f32)

    def gain(shape):
        return 1.0 + nrm(shape, 0.02)

    n_pages = PAST_LEN // PAGE_SIZE
    n_pool = (DEC_BATCH * n_pages * 5) // 4
    bw = LRU_W // LRU_BLOCKS

    x_prompt = nrm((BATCH, SEQ, D_MODEL))
    x_sample = nrm((DEC_BATCH, DEC_SEQ, D_MODEL))
    cache_diff_k = nrm((DEPTH, n_pool, PAGE_SIZE, DIFF_HEADS, 2, DIFF_QK))
    cache_diff_v = nrm((DEPTH, n_pool, PAGE_SIZE, DIFF_HEADS, DIFF_V))
    cache_dsa_k = nrm((DEPTH, n_pool, PAGE_SIZE, DSA_HEADS, DSA_HD))
    cache_dsa_v = nrm((DEPTH, n_pool, PAGE_SIZE, DSA_HEADS, DSA_HD))
    cache_idx_k = nrm((DEPTH, n_pool, PAGE_SIZE, IDX_DIM))
    state_lru_h = nrm((DEPTH, DEC_BATCH, LRU_W), 0.5)
    state_lru_conv = nrm((DEPTH, DEC_BATCH, LRU_CONV - 1, LRU_W))
    state_gla = nrm((DEPTH, DEC_BATCH, GLA_HEADS, GLA_DK, GLA_DV), 0.5)
    state_ffn_conv = nrm((DEPTH, DEC_BATCH, FFN_CONV - 1, D_FF))
    page_table = jax.random.permutation(nk(), n_pool)[: DEC_BATCH * n_pages].reshape(DEC_BATCH, n_pages).astype(jnp.int32)
    c_prompt = nrm((BATCH, D_MODEL))
    c_sample = nrm((DEC_BATCH, D_MODEL))

    w_ada = nrm((DEPTH, D_MODEL, 6 * D_MODEL), 0.5 * D_MODEL ** -0.5)
    b_ada = nrm((DEPTH, 6 * D_MODEL), 0.02)
    norm1_g = gain((DEPTH, D_MODEL))
    w_in = nrm((DEPTH, D_MODEL, D_IN), D_MODEL ** -0.5)
    lru_conv_w = nrm((DEPTH, LRU_CONV, LRU_W), LRU_CONV ** -0.5)
    lru_conv_b = nrm((DEPTH, LRU_W), 0.02)
    lru_wa = nrm((DEPTH, LRU_BLOCKS, bw, bw), bw ** -0.5)
    lru_ba = nrm((DEPTH, LRU_W), 0.02)
    lru_wx = nrm((DEPTH, LRU_BLOCKS, bw, bw), bw ** -0.5)
    lru_bx = nrm((DEPTH, LRU_W), 0.02)
    u = jax.random.uniform(nk(), (DEPTH, LRU_W), f32, 0.9, 0.999)
    a_base = u ** (1.0 / LRU_C)
    lru_lambda = jnp.log(a_base) - jnp.log1p(-a_base)
    diff_lq1 = nrm((DEPTH, DIFF_QK), 0.1)
    diff_lk1 = nrm((DEPTH, DIFF_QK), 0.1)
    diff_lq2 = nrm((DEPTH, DIFF_QK), 0.1)
    diff_lk2 = nrm((DEPTH, DIFF_QK), 0.1)
    diff_subln_g = gain((DEPTH, DIFF_V))
    gla_wa2 = nrm((DEPTH, GLA_RANK, GLA_HEADS * GLA_DK), GLA_RANK ** -0.5)
    gla_ba = nrm((DEPTH, GLA_HEADS * GLA_DK), 0.02)
    gla_norm_g = gain((DEPTH, GLA_DV))
    w_out = nrm((DEPTH, D_MODEL, D_MODEL), D_MODEL ** -0.5)
    norm2_g = gain((DEPTH, D_MODEL))
    ffn_w_gate = nrm((DEPTH, D_MODEL, D_FF), D_MODEL ** -0.5)
    ffn_w_up = nrm((DEPTH, D_MODEL, D_FF), D_MODEL ** -0.5)
    ffn_conv_w = nrm((DEPTH, FFN_CONV, D_FF), FFN_CONV ** -0.5)
    ffn_conv_b = nrm((DEPTH, D_FF), 0.02)
    ffn_w_down = nrm((DEPTH, D_FF, D_MODEL), D_FF ** -0.5)
    final_norm_g = gain((D_MODEL,))
    return {
        "x_prompt": x_prompt, "x_sample": x_sample,
        "cache_diff_k": cache_diff_k, "cache_diff_v": cache_diff_v,
        "cache_dsa_k": cache_dsa_k, "cache_dsa_v": cache_dsa_v, "cache_idx_k": cache_idx_k,
        "state_lru_h": state_lru_h, "state_lru_conv": state_lru_conv,
        "state_gla": state_gla, "state_ffn_conv": state_ffn_conv,
        "page_table": page_table, "c_prompt": c_prompt, "c_sample": c_sample,
        "w_ada": w_ada, "b_ada": b_ada, "norm1_g": norm1_g, "w_in": w_in,
        "lru_conv_w": lru_conv_w, "lru_conv_b": lru_conv_b, "lru_wa": lru_wa, "lru_ba": lru_ba,
        "lru_wx": lru_wx, "lru_bx": lru_bx, "lru_lambda": lru_lambda,
        "diff_lq1": diff_lq1, "diff_lk1": diff_lk1, "diff_lq2": diff_lq2, "diff_lk2": diff_lk2,
        "diff_subln_g": diff_subln_g, "gla_wa2": gla_wa2, "gla_ba": gla_ba, "gla_norm_g": gla_norm_g,
        "w_out": w_out, "norm2_g": norm2_g, "ffn_w_gate": ffn_w_gate, "ffn_w_up": ffn_w_up,
        "ffn_conv_w": ffn_conv_w, "ffn_conv_b": ffn_conv_b, "ffn_w_down": ffn_w_down,
        "final_norm_g": final_norm_g,
    }


def reference(x_prompt, x_sample, cache_diff_k, cache_diff_v, cache_dsa_k, cache_dsa_v, cache_idx_k,
              state_lru_h, state_lru_conv, state_gla, state_ffn_conv, page_table, c_prompt, c_sample,
              w_ada, b_ada, norm1_g, w_in, lru_conv_w, lru_conv_b, lru_wa, lru_ba, lru_wx, lru_bx, lru_lambda,
              diff_lq1, diff_lk1, diff_lq2, diff_lk2, diff_subln_g, gla_wa2, gla_ba, gla_norm_g,
              w_out, norm2_g, ffn_w_gate, ffn_w_up, ffn_conv_w, ffn_conv_b, ffn_w_down, final_norm_g):
    pos_p = jnp.arange(x_prompt.shape[1], dtype=jnp.int32)
    past_len = page_table.shape[1] * PAGE_SIZE
    pos_s = past_len + jnp.arange(x_sample.shape[1], dtype=jnp.int32)
    xp, xs = x_prompt, x_sample
    st_p, st_s = [], []
    for l in range(DEPTH):
        lp = dict(
            w_ada=w_ada[l], b_ada=b_ada[l], norm1_g=norm1_g[l], w_in=w_in[l],
            lru_conv_w=lru_conv_w[l], lru_conv_b=lru_conv_b[l], lru_wa=lru_wa[l], lru_ba=lru_ba[l],
            lru_wx=lru_wx[l], lru_bx=lru_bx[l], lru_lambda=lru_lambda[l],
            diff_lq1=diff_lq1[l], diff_lk1=diff_lk1[l], diff_lq2=diff_lq2[l], diff_lk2=diff_lk2[l],
            diff_subln_g=diff_subln_g[l], gla_wa2=gla_wa2[l], gla_ba=gla_ba[l], gla_norm_g=gla_norm_g[l],
            w_out=w_out[l], norm2_g=norm2_g[l], ffn_w_gate=ffn_w_gate[l], ffn_w_up=ffn_w_up[l],
            ffn_conv_w=ffn_conv_w[l], ffn_conv_b=ffn_conv_b[l], ffn_w_down=ffn_w_down[l],
        )
        xp, new_p = decoder_layer(xp, c_prompt, lp, l, pos_p, None, None)
        past = dict(
            diff_k=cache_diff_k[l], diff_v=cache_diff_v[l], dsa_k=cache_dsa_k[l], dsa_v=cache_dsa_v[l],
            idx_k=cache_idx_k[l], lru_h=state_lru_h[l], lru_conv=state_lru_conv[l], gla=state_gla[l],
            ffn_conv=state_ffn_conv[l],
        )
        xs, new_s = decoder_layer(xs, c_sample, lp, l, pos_s, past, page_table)
        st_p.append(new_p)
        st_s.append(new_s)
    y_prompt = rmsnorm(xp, final_norm_g)
    y_sample = rmsnorm(xs, final_norm_g)
    P = {k: jnp.stack([s[k] for s in st_p]) for k in STATE_KEYS}
    S = {k: jnp.stack([s[k] for s in st_s]) for k in STATE_KEYS}
    return (y_prompt, y_sample,
            P["diff_k"], P["diff_v"], P["dsa_k"], P["dsa_v"], P["idx_k"],
            P["lru_h"], P["lru_conv"], P["gla"], P["ffn_conv"],
            S["diff_k"], S["diff_v"], S["dsa_k"], S["dsa_v"], S["idx_k"],
            S["lru_h"], S["lru_conv"], S["gla"], S["ffn_conv"])
```

```python
import math
import numpy as np
import concourse.bass as bass
import concourse.mybir as mybir
from contextlib import ExitStack
F32 = mybir.dt.float32; BF16 = mybir.dt.bfloat16; I32 = mybir.dt.int32
AF = mybir.ActivationFunctionType; ALU = mybir.AluOpType
ENGS = ['pe', 'act', 'dve', 'pool', 'sp']
EP = 60000
ND = 20

class Prog:
    def __init__(self, nc, stack):
        self.nc = nc; self.stack = stack
        self.q = {e: [] for e in ENGS}
        self.n = {e: 0 for e in ENGS}
        self.esem = {}
        self.seen = {e: {} for e in ENGS}
        self.res = {}
        self.dsem = [stack.enter_context(nc.semaphore(f"d{i}")) for i in range(2 * ND)]
        self.dcount = [0] * (2 * ND)
        self.dnext = {'sp': 0, 'pool': 0}
        self.nbank = 0

    def _sem(self, k):
        if k[0] == 'd':
            return self.dsem[k[1]]
        if k not in self.esem:
            self.esem[k] = self.stack.enter_context(self.nc.semaphore(f"s_{k[1]}_{k[2]}"))
        return self.esem[k]

    def _deps(self, eng, reads, writes):
        writes = tuple(writes) + tuple(r for r in reads if r.startswith("ps"))
        reads = tuple(r for r in reads if not r.startswith("ps"))
        deps = []
        for r in reads:
            rr = self.res.get(r)
            if rr and rr['w']: deps.append(rr['w'])
        for w in writes:
            rr = self.res.get(w)
            if rr:
                if rr['w']: deps.append(rr['w'])
                deps.extend(rr['r'].values())
        waits = []
        for (k, v) in deps:
            if eng == 'pe' and k[0] == 'e' and k[1] == 'pe': continue
            if self.seen[eng].get(k, 0) >= v: continue
            self.seen[eng][k] = v; waits.append((k, v))
        return waits

    def _commit(self, tok, reads, writes):
        writes = tuple(writes) + tuple(r for r in reads if r.startswith("ps"))
        reads = tuple(r for r in reads if not r.startswith("ps"))
        for r in reads:
            rr = self.res.setdefault(r, {'w': None, 'r': {}})
            rr['r'][tok[0]] = tok
        for w in writes:
            self.res[w] = {'w': tok, 'r': {}}

    def op(self, eng, fn, reads=(), writes=()):
        waits = self._deps(eng, reads, writes)
        self.n[eng] += 1; idx = self.n[eng]; ep = (idx - 1) // EP
        tok = (('e', eng, ep), idx - ep * EP)
        self._sem(tok[0])
        self._commit(tok, reads, writes)
        self.q[eng].append((waits, fn, tok[0], 1))

    def dma(self, eng, fn, reads=(), writes=()):
        waits = self._deps(eng, reads, writes)
        i = (self.dnext[eng] % ND) + (ND if eng == 'pool' else 0); self.dnext[eng] += 1
        k = ('d', i); prev = self.dcount[i]
        if prev > 0 and self.seen[eng].get(k, 0) < prev:
            self.seen[eng][k] = prev; waits.append((k, prev))
        self.dcount[i] = prev + 16
        tok = (k, prev + 16)
        self._commit(tok, reads, writes)
        self.q[eng].append((waits, fn, k, 16))

    def emit(self):
        nc = self.nc
        fin = [(('d', i), c) for i, c in enumerate(self.dcount) if c > 0]
        q = self.q
        esem = self._sem
        def run(e, name):
            for (waits, fn, k, inc) in q[name]:
                for (wk, wv) in waits:
                    e.wait_ge(esem(wk), wv)
                ins = fn(e)
                ins.then_inc(esem(k), inc)
        with nc.Block() as block:
            @block.tensor
            def _(e): run(e, 'pe')
            @block.scalar
            def _(e): run(e, 'act')
            @block.vector
            def _(e): run(e, 'dve')
            @block.gpsimd
            def _(e): run(e, 'pool')
            @block.sync
            def _(e):
                run(e, 'sp')
                for (k, c) in fin:
                    e.wait_ge(esem(k), c)
                for en in ENGS:
                    if self.n[en] > 0:
                        idx = self.n[en]; ep = (idx - 1) // EP
                        e.wait_ge(esem(('e', en, ep)), idx - ep * EP)

from concourse.bass_utils import run_bass_kernel_spmd
import math
D = 1024; KC = 8; DFF = 2816; FC = 22
NFM = 29 * 128; NTM = 1160
V_N1 = 0; V_N2 = 8; V_CW = 16; V_CB = 24; V_BA = 26; V_BX = 28; V_LAM = 30; V_FCW = 32; V_FCB = 32 + 66; V_BADA = 98 + 22
NV = V_BADA + 48
EPS = 1e-6
NBIS = 26
BLK = 256
QPB = BLK // 128
GC = 1.5957691216057308

class _Stop(Exception):
    pass
def ckpt(n):
    return None

def build_program(nc, T, L, lam_inits, topk, scfg=None):
    try:
        return _build_program(nc, T, L, lam_inits, topk, scfg)
    except _Stop:
        pass
    _build_program.P.emit(); _build_program.stack.close()
    return nc

def _build_program(nc, T, L, lam_inits, topk, scfg):
    stack = ExitStack()
    P = Prog(nc, stack)
    _build_program.P = P; _build_program.stack = stack
    NB = T // BLK; NT = T // 128
    dt = nc.dram_tensor
    xT_in = dt("xT", [D, T], F32, kind="ExternalInput").ap()
    cT_in = dt("cT", [128, KC, 1 + 4 * (scfg.get("ng", 1) if scfg else 1)], F32, kind="ExternalInput").ap()
    wada = dt("wada", [L, D, 6 * D], F32, kind="ExternalInput").ap()
    winfm = dt("winfm", [L, D, NFM], F32, kind="ExternalInput").ap()
    wintm = dt("wintm", [L, D, NTM], F32, kind="ExternalInput").ap()
    wout = dt("wout", [L, D, D], F32, kind="ExternalInput").ap()
    wg = dt("wg", [L, D, DFF], F32, kind="ExternalInput").ap()
    wu = dt("wu", [L, D, DFF], F32, kind="ExternalInput").ap()
    wd = dt("wd", [L, DFF, D], F32, kind="ExternalInput").ap()
    vecs_in = dt("vecs", [L, 128, NV], F32, kind="ExternalInput").ap()
    lruw_in = dt("lruw", [L, 128, 4, 128], F32, kind="ExternalInput").ap()
    glaw_in = dt("glaw", [L, 17, 128], F32, kind="ExternalInput").ap()
    bc_in = dt("bc", [L, 1, 256], F32, kind="ExternalInput").ap()
    fng_in = dt("fng", [128, KC], F32, kind="ExternalInput").ap()
    rope_in = dt("rope", [4, 128, T], F32, kind="ExternalInput").ap()
    consts_in = dt("consts", [128, 6, 128], F32, kind="ExternalInput").ap()
    yT = dt("yT", [D, T], F32, kind="ExternalOutput").ap()
    o_kd = dt("o_kd", [L, 256, T], F32, kind="ExternalOutput").ap()
    o_vd = dt("o_vd", [L, T, 256], F32, kind="ExternalOutput").ap()
    o_kc = dt("o_kc", [L, 256, T], F32, kind="ExternalOutput").ap()
    o_vc = dt("o_vc", [L, T, 256], F32, kind="ExternalOutput").ap()
    o_ik = dt("o_ik", [L, 32, T], F32, kind="ExternalOutput").ap()
    o_lh = dt("o_lh", [L, 128, 2], F32, kind="ExternalOutput").ap()
    o_lc = dt("o_lc", [L, 128, 2, 3], F32, kind="ExternalOutput").ap()
    o_gs = dt("o_gs", [L, 128, 64], F32, kind="ExternalOutput").ap()
    o_fc = dt("o_fc", [L, 128, FC, 2], F32, kind="ExternalOutput").ap()
    xs = [dt(f"xs{i}", [D, T], F32, kind="Internal").ap() for i in range(max(1, L - 1))]
    SAMPLE = scfg is not None
    NBS = 4; NS = 16
    NG = scfg.get('ng', 1) if SAMPLE else 1
    NBC = 1 + 4 * NG
    if SAMPLE:
        NPG, NPOOL, topk_s = scfg["npg"], scfg["npool"], scfg["topk"]
        PPR = NPG // 8; W = PPR * 128
        NG_ = scfg.get('ng', 1)
        xsT_in = dt("xsT", [NG_] + [128, KC, NS], F32, kind="ExternalInput").ap()
        ropeS_in = dt("ropeS", [4, 128, NS], F32, kind="ExternalInput").ap()
        pt_in = dt("pt", [NG_, 1, NBS * NPG], I32, kind="ExternalInput").ap()
        pdk = dt("pdk", [L * NPOOL * 128, 256], F32, kind="ExternalInput").ap()
        pdv = dt("pdv", [L * NPOOL * 128, 256], F32, kind="ExternalInput").ap()
        pck = dt("pck", [L * NPOOL * 128, 256], F32, kind="ExternalInput").ap()
        pcv = dt("pcv", [L * NPOOL * 128, 256], F32, kind="ExternalInput").ap()
        pik = dt("pik", [L * NPOOL * 32, 128], F32, kind="ExternalInput").ap()
        st_lh_in = dt("st_lh", [NG_] + [L, 128, 2, NBS], F32, kind="ExternalInput").ap()
        st_lc_in = dt("st_lc", [NG_] + [L, 128, 2, NBS, 3], F32, kind="ExternalInput").ap()
        st_gs_in = dt("st_gs", [NG_] + [L, 128, NBS, 64], F32, kind="ExternalInput").ap()
        st_fc_in = dt("st_fc", [NG_] + [L, 128, FC, NBS, 2], F32, kind="ExternalInput").ap()
        cS_in = dt("constS", [128, 144], F32, kind="ExternalInput").ap()
        ysT = dt("ysT", [NG_] + [128, KC, NS], F32, kind="ExternalOutput").ap()
        s_kd = dt("s_kd", [NG_] + [L, 256, NS], F32, kind="ExternalOutput").ap()
        s_vd = dt("s_vd", [NG_] + [L, NBS, 4, 256], F32, kind="ExternalOutput").ap()
        s_kc = dt("s_kc", [NG_] + [L, 256, NS], F32, kind="ExternalOutput").ap()
        s_vc = dt("s_vc", [NG_] + [L, NBS, 4, 256], F32, kind="ExternalOutput").ap()
        s_ik = dt("s_ik", [NG_] + [L, 32, NS], F32, kind="ExternalOutput").ap()
        s_lh = dt("s_lh", [NG_] + [L, 128, 2, NBS], F32, kind="ExternalOutput").ap()
        s_lc = dt("s_lc", [NG_] + [L, 128, 2, NBS, 3], F32, kind="ExternalOutput").ap()
        s_gs = dt("s_gs", [NG_] + [L, 128, NBS, 64], F32, kind="ExternalOutput").ap()
        s_fc = dt("s_fc", [NG_] + [L, 128, FC, NBS, 2], F32, kind="ExternalOutput").ap()

    def sb(name, shape, dtype=F32):
        n = 1
        for d_ in shape[1:]: n *= d_
        _build_program.sbtot = getattr(_build_program, "sbtot", 0) + n * (2 if dtype == BF16 else 4)
        return stack.enter_context(nc.sbuf_tensor("s_" + name, shape, dtype))
    def pst(name):
        return stack.enter_context(nc.psum_tensor(name, [128, 512], F32))
    banks = [pst(f"bank{i}") for i in range(8)]
    rot = [0]
    def bank(lo=0, hi=6):
        i = lo + rot[0] % (hi - lo); rot[0] += 1
        return i

    consts = sb("consts", [128, 6, 128])
    identb = sb("identb", [128, 128], BF16)
    onesb = sb("onesb", [128, 128], BF16)
    ones1 = sb("ones1", [1, 128], BF16)
    Ubf = sb("Ubf", [128, 128], BF16)
    vecs = sb("vecs", [128, NV])
    lruw = sb("lruw", [128, 4, 128], BF16)
    glaw = sb("glaw", [17, 128], BF16)
    glab = sb("glab", [1, 128], BF16)
    bcv = sb("bcv", [128, 256])
    fng = sb("fng", [128, KC])
    csil = sb("csil", [128, KC, NBC], BF16)
    cTt = sb("cTt", [128, KC, NBC])
    mod = sb("mod", [128, 48, NBC])
    A1 = sb("A1", [128, KC, NBC]); A2 = sb("A2", [128, KC, NBC])
    lamt = sb("lamt", [128, 4]); cl = sb("cl", [128, 2]); cl2 = sb("cl2", [128, 2])
    KdT = sb("KdT", [128, 2, T], BF16); KcT = sb("KcT", [128, 2, T], BF16); ikT = sb("ikT", [128, T], BF16)
    Vd = sb("Vd", [128, NT, 4, 65], BF16); Vc = sb("Vc", [128, NT, 4, 65], BF16)
    Sg = sb("Sg", [128, 64]); Sgb = sb("Sgb", [128, 64], BF16)
    hlast = sb("hlast", [128, 2])
    fcs = sb("fcs", [128, FC, 2])
    xt = sb("xt", [128, KC, BLK]); hT = sb("hT", [128, KC, BLK], BF16); catT = sb("catT", [128, KC, BLK], BF16)
    sqb = sb("sqb", [128, BLK], BF16)
    rstd = sb("rstd", [128, BLK])
    tA = sb("tA", [128, 512]); tB = sb("tB", [128, BLK]); tC = sb("tC", [128, BLK]); tD = sb("tD", [128, BLK]); tE = sb("tE", [128, BLK])
    ropet = sb("ropet", [128, 4, BLK])
    xpad = sb("xpad", [128, 2, BLK + 3])
    xab = sb("xab", [128, BLK], BF16)
    QdT = sb("QdT", [128, 4, 2, BLK], BF16); QcT = sb("QcT", [128, 2, 2, BLK], BF16); iqm = sb("iqm", [128, 8, BLK], BF16)
    gqT = sb("gqT", [128, BLK]); gkT = sb("gkT", [128, BLK]); gaT = sb("gaT", [16, BLK], BF16)
    kdf = sb("kdf", [128, BLK]);
    gv = sb("gv", [128, QPB, 256], BF16); gg = sb("gg", [128, QPB, 256]); gk = sb("gk", [128, QPB, 128]); iw = sb("iw", [128, QPB, 8])
    vout = sb("vout", [128, 1, 256])
    NWB = 2
    wbuf = [sb(f"wb{i}", [128, 4096], BF16) for i in range(NWB)]
    wrot = [0]
    big = sb("big", [128, max(22 * BLK * 2, 6 * T, 4300 * 4) // 4])
    act = big[:].bitcast(BF16).rearrange("p (c n) -> p c n", n=512) if False else None
    PT = sb("PT", [128, 1024], BF16)
    tmf = sb("tmf", [128, 8, 65])
    tms = sb("tms", [128, 32])
    otm = sb("otm", [128, 256]); otb = sb("otb", [128, 256], BF16)
    g_la = sb("g_la", [128, 128]); g_e1 = sb("g_e1", [128, 128]); g_e2 = sb("g_e2", [128, 128]); g_kh = sb("g_kh", [128, 128], BF16)
    g_qt = sb("g_qt", [128, 128]); g_qm = sb("g_qm", [128, 4, 128], BF16); g_kt = sb("g_kt", [128, 128], BF16); g_at = sb("g_at", [128, 4, 128], BF16)
    bis = sb("bis", [128, 8])
    scr = big[:, 0:T]
    mbt = big[:, T:T + T // 2].bitcast(BF16)
    actt = big[:, 0:22 * BLK // 2].bitcast(BF16).rearrange("p (c n) -> p c n", n=BLK)

    if SAMPLE:
        cS = sb("cS", [128, 144])
        Gm = cS[:, 0:128]; negS = cS[:, 128:132]; iotaP = cS[:, 132:133]; iotaP32 = cS[:, 133:134]
        xsSg = [sb(f"xsS{g_}", [128, KC, NS]) for g_ in range(NG)]; hS = sb("hS", [128, KC, NS], BF16); catS = sb("catS", [128, KC, NS], BF16)
        ropeS = sb("ropeS", [128, 4, NS])
        sqS = sb("sqS", [128, NS], BF16); rstdS = sb("rstdS", [128, NS])
        uA = sb("uA", [128, NS]); uB = sb("uB", [128, NS]); uC = sb("uC", [128, NS]); uD = sb("uD", [128, NS]); uE = sb("uE", [128, NS]); uF = sb("uF", [128, NS])
        xabS = sb("xabS", [128, NS], BF16)
        xpS = sb("xpS", [128, 2, NBS, 7]); h0S = sb("h0S", [128, 2, NBS]); hnS = sb("hnS", [128, 2, NBS])
        QdS = sb("QdS", [128, 4, 2, NS], BF16); QcS = sb("QcS", [128, 2, 2, NS], BF16); iqmS = sb("iqmS", [128, 8, NS], BF16)
        KdN = sb("KdN", [128, 2, NS], BF16); KcN = sb("KcN", [128, 2, NS], BF16); ikN = sb("ikN", [128, NS], BF16)
        gqS = sb("gqS", [128, NS]); gkS_f = sb("gkS_f", [128, NS]); gaS = sb("gaS", [16, NS], BF16)
        voS = sb("voS", [4, 2, 256]); vdN = sb("vdN", [4, NBS, 4, 65], BF16); vcN = sb("vcN", [4, NBS, 4, 65], BF16)
        gvS = sb("gvS", [4, NBS, 256], BF16); ggS = sb("ggS", [4, NBS, 256], BF16); gkS = sb("gkS", [4, NBS, 128])
        iw16 = sb("iw16", [16, 8]); iwR = sb("iwR", [128, 8]); iwB = sb("iwB", [128, NBS, 8])
        SgS = sb("SgS", [128, NBS, 64]); SgSb = sb("SgSb", [128, NBS, 64], BF16)
        fcS = sb("fcS", [128, FC, NBS, 2]); fcSn = sb("fcSn", [128, FC, NBS, 2]); fpad = sb("fpad", [128, NBS, 6])
        actS = sb("actS", [128, FC, NS], BF16)
        ptb = sb("ptb", [128, NBS * NPG], I32); ptf1 = sb("ptf1", [128, NBS * NPG])
        ixK = sb("ixK", [128, NBS * NPG], I32); ixI = sb("ixI", [128, NBS * NPG], I32)
        vpb = [sb(f"vpb{i}", [128, 4, 65], BF16) for i in range(2)]
        PTs = sb("PTs", [128, 32], BF16); PTn = sb("PTn", [4, 32], BF16)
        bisS = sb("bisS", [128, 8])
    tAf = tA
    if SAMPLE:
        assert W + 4 <= 1040 and (W + 5) // 2 <= 560
        scrS = big[:, 0:W + 4]
        mbS = big[:, 1040:1040 + (W + 5) // 2].bitcast(BF16)[:, 0:W + 4]
        ZP = big[:, 1600:1600 + 960].bitcast(BF16).rearrange("p (h n) -> p h n", n=240)
        kpf = [big[:, 2600:2856], big[:, 2860:3116]]
        vpf = [big[:, 3120:3376], big[:, 3380:3636]]
        ipf = [big[:, 3640:3768], big[:, 3770:3898]]
        kpb = [big[:, 3900:4028].bitcast(BF16).rearrange("p (c k) -> p c k", k=128), big[:, 4030:4158].bitcast(BF16).rearrange("p (c k) -> p c k", k=128)]
        ipb = [big[:, 4160:4224].bitcast(BF16), big[:, 4230:4294].bitcast(BF16)]
        FENCE = ("scr", "actt", "mbt", "scrS", "mbS", "ZP", "kpf0", "kpf1", "vpf0", "vpf1", "ipf0", "ipf1", "kpb0", "kpb1", "ipb0", "ipb1")
    I_ = consts[:, 0, :]; U_ = consts[:, 1, :]; UT_ = consts[:, 2, :]; L_ = consts[:, 3, :]; NEG_ = consts[:, 4, :]
    HM = lambda g: consts[:, 5, g:g + 1]

    def dma_sp(out, in_, reads, writes):
        P.dma('sp', lambda e: e.dma_start(out=out, in_=in_), reads, writes)
    def dma_pool(out, in_, reads, writes):
        P.dma('pool', lambda e: e.dma_start(out=out, in_=in_), reads, writes)
    def mm(out, lhsT, rhs, start, stop, reads, writes):
        P.op('pe', lambda e: e.matmul(out, lhsT=lhsT, rhs=rhs, start=start, stop=stop, skip_group_check=True), reads, writes)
    def tr(out, in_, ident, reads, writes):
        P.op('pe', lambda e: e.transpose(out, in_, ident), reads, writes)
    def actf(out, in_, func, reads, writes, bias=None, scale=None, accum=None):
        kw = {}
        if bias is not None: kw['bias'] = bias
        if scale is not None: kw['scale'] = scale
        if accum is not None: kw['accum_out'] = accum
        P.op('act', lambda e: e.activation(out=out, in_=in_, func=func, **kw), reads, writes)
    def ts(eng, out, in0, s1, s2, op0, op1, reads, writes, accum=None):
        kw = {}
        if accum is not None: kw['accum_out'] = accum
        if op1 is None:
            P.op(eng, lambda e: e.tensor_scalar(out=out, in0=in0, scalar1=s1, scalar2=None, op0=op0, **kw), reads, writes)
        else:
            P.op(eng, lambda e: e.tensor_scalar(out=out, in0=in0, scalar1=s1, scalar2=s2, op0=op0, op1=op1, **kw), reads, writes)
    def tt(eng, out, in0, in1, op, reads, writes):
        P.op(eng, lambda e: e.tensor_tensor(out=out, in0=in0, in1=in1, op=op), reads, writes)
    def stt(out, in0, s, in1, op0, op1, reads, writes):
        P.op('dve', lambda e: e.scalar_tensor_tensor(out=out, in0=in0, scalar=s, in1=in1, op0=op0, op1=op1), reads, writes)
    def cp(eng, out, in_, reads, writes):
        if eng == 'act':
            P.op('act', lambda e: e.copy(out=out, in_=in_), reads, writes)
        else:
            P.op(eng, lambda e: e.tensor_copy(out=out, in_=in_), reads, writes)
    def memset(eng, ap, val, writes):
        P.op(eng, lambda e: e.memset(ap, val), (), writes)
    def recip(out, in_, reads, writes):
        P.op('dve', lambda e: e.reciprocal(out=out, in_=in_), reads, writes)

    wscr = {}
    def load_w(key, src2d, krows, ncols):
        i = wrot[0] % NWB; wrot[0] += 1
        kc = krows // 128
        assert kc * ncols <= 4096
        flat = wbuf[i][:, 0:kc * ncols]
        dst = flat.rearrange("p (k n) -> p k n", n=ncols)
        if key not in wscr:
            nm = "ws_" + "_".join(str(k) for k in key)
            wscr[key] = (nc.dram_tensor(nm, [128, kc * ncols], BF16, kind="Internal").ap(), nm)
            dma_pool(dst, src2d.rearrange("(k p) n -> p k n", p=128), (), (f"wb{i}",))
            dma_sp(wscr[key][0], flat, (f"wb{i}",), (wscr[key][1],))
        else:
            dma_sp(flat, wscr[key][0], (wscr[key][1],), (f"wb{i}",))
        return dst, f"wb{i}"

    def gather(dst, pool_ap, idx_col, reads, writes):
        P.dma('pool', lambda e: e.indirect_dma_start(out=dst, out_offset=None, in_=pool_ap,
                                                     in_offset=bass.IndirectOffsetOnAxis(ap=idx_col, axis=0)), reads, writes)

    def finalize_tm(PP, kind, lam_init, cat_dst_fn):
        otm3 = otm[:PP].rearrange("p (h d) -> p h d", d=64)
        otb3 = otb[:PP].rearrange("p (h d) -> p h d", d=64)
        if kind == 'diff':
            for m in range(2):
                cp('act', tmf[:PP, m * 4:(m + 1) * 4, :], banks[6 + m][:PP, 0:260].rearrange("p (h d) -> p h d", d=65), (f"ps{6 + m}",), ("tmf",))
            recip(tms[:PP, 0:8], tmf[:PP, :, 64], ("tmf",), ("tms",))
            tt('dve', tms[:PP, 4:8], tms[:PP, 4:8], lamt[:PP, 3:4].to_broadcast([PP, 4]), ALU.mult, ("tms", "lamt"), ("tms",))
            tt('dve', tmf[:PP, 0:4, 0:64], tmf[:PP, 0:4, 0:64], tms[:PP, 0:4].unsqueeze(2).to_broadcast([PP, 4, 64]), ALU.mult, ("tmf", "tms"), ("tmf",))
            tt('dve', tmf[:PP, 4:8, 0:64], tmf[:PP, 4:8, 0:64], tms[:PP, 4:8].unsqueeze(2).to_broadcast([PP, 4, 64]), ALU.mult, ("tmf", "tms"), ("tmf",))
            tt('dve', otm3, tmf[:PP, 0:4, 0:64], tmf[:PP, 4:8, 0:64], ALU.add, ("tmf",), ("otm",))
            tt('dve', tmf[:PP, 0:4, 0:64], otm3, otm3, ALU.mult, ("otm",), ("tmf",))
            P.op('dve', lambda e: e.tensor_reduce(out=tms[:PP, 8:12], in_=tmf[:PP, 0:4, 0:64], axis=mybir.AxisListType.X, op=ALU.add), ("tmf",), ("tms",))
            actf(tms[:PP, 8:12], tms[:PP, 8:12], AF.Sqrt, ("tms",), ("tms",), scale=1.0 / 64, bias=EPS)
            recip(tms[:PP, 8:12], tms[:PP, 8:12], ("tms",), ("tms",))
            tt('dve', otm3, otm3, tms[:PP, 8:12].unsqueeze(2).to_broadcast([PP, 4, 64]), ALU.mult, ("otm", "tms"), ("otm",))
            stt(otb3, otm3, 1.0 - lam_init, bcv[:PP, 0:64].unsqueeze(1).to_broadcast([PP, 4, 64]), ALU.mult, ALU.mult, ("otm", "bcv"), ("otb",))
        else:
            cp('act', tmf[:PP, 0:4, :], banks[6][:PP, 0:260].rearrange("p (h d) -> p h d", d=65), ("ps6",), ("tmf",))
            recip(tms[:PP, 0:4], tmf[:PP, 0:4, 64], ("tmf",), ("tms",))
            tt('dve', otb3, tmf[:PP, 0:4, 0:64], tms[:PP, 0:4].unsqueeze(2).to_broadcast([PP, 4, 64]), ALU.mult, ("tmf", "tms"), ("otb",))
        bt = bank()
        for c in range(2):
            mm(banks[bt][:, c * 128:c * 128 + PP], otb[:PP, c * 128:(c + 1) * 128], identb[:PP, :PP], c == 0, True, ("otb", "identb"), (f"ps{bt}",))
        for c in range(2):
            cat_dst_fn(c, bt)

    def sample_setup():
        dma_sp(cS[:], cS_in, (), ("cS",))
        for g_ in range(NG):
            dma_sp(xsSg[g_][:], xsT_in[g_], (), (f"xsS{g_}",))
        dma_sp(ropeS[:], ropeS_in.rearrange("r p t -> p r t"), (), ("ropeS",))
        memset('pool', vdN[:, :, :, 64:65], 1.0, ("vdN",)); memset('pool', vcN[:, :, :, 64:65], 1.0, ("vcN",))
        for i in range(2):
            memset('pool', vpb[i][:, :, 64:65], 1.0, (f"vpb{i}",))

    def sample_layer(l, lam_init, grp):
        xsS = xsSg[grp]; ptf = ptf1
        XS = f"xsS{grp}"; PTF = "ptf1"
        st_lh_g, st_lc_g, st_gs_g, st_fc_g = st_lh_in[grp], st_lc_in[grp], st_gs_in[grp], st_fc_in[grp]
        ysT_g = ysT[grp]; s_kd_g = s_kd[grp]; s_vd_g = s_vd[grp]; s_kc_g = s_kc[grp]; s_vc_g = s_vc[grp]; s_ik_g = s_ik[grp]
        s_lh_g = s_lh[grp]; s_lc_g = s_lc[grp]; s_gs_g = s_gs[grp]; s_fc_g = s_fc[grp]
        P.op('dve', lambda e: e.memset(bisS[:, 7:8], 0.0), (), FENCE + ("bisS",))
        memset('pool', ZP, 0.0, ("ZP",))
        dma_sp(h0S[:], st_lh_g[l], (), ("h0S",))
        dma_sp(xpS[:, :, :, 0:3], st_lc_g[l], (), ("xpS",))
        dma_sp(SgS[:], st_gs_g[l], (), ("SgS",))
        cp('act', SgSb[:], SgS[:], ("SgS",), ("SgSb",))
        dma_sp(fcS[:], st_fc_g[l], (), ("fcS",))
        dma_sp(ptb[:], pt_in[grp].partition_broadcast(128), (), ("ptb",))
        cp('dve', ptf[:], ptb[:], ("ptb",), (PTF,))
        ts('dve', uA[:, 0:1], iotaP, float(l * NPOOL * 128), None, ALU.add, None, ("cS",), ("uA",))
        ts('dve', ixK[:], ptf[:], 128.0, uA[:, 0:1], ALU.mult, ALU.add, (PTF, "uA"), ("ixK",))
        ts('dve', uA[:, 1:2], iotaP32, float(l * NPOOL * 32), None, ALU.add, None, ("cS", "ixK"), ("uA",))
        ts('dve', ixI[:], ptf[:], 32.0, uA[:, 1:2], ALU.mult, ALU.add, (PTF, "uA"), ("ixI",))

        def norm_mod_s(Acoef, shoff):
            b = bank()
            for kc in range(KC):
                actf(sqS[:], xsS[:, kc, :], AF.Square, (XS,), ("sqS",))
                mm(banks[b][:, :NS], onesb[:], sqS[:], kc == 0, kc == KC - 1, ("sqS", "onesb"), (f"ps{b}",))
            actf(rstdS[:], banks[b][:, :NS], AF.Sqrt, (f"ps{b}",), ("rstdS",), scale=1.0 / D, bias=EPS)
            recip(rstdS[:], rstdS[:], ("rstdS",), ("rstdS",))
            for kc in range(KC):
                tt('dve', uA[:], xsS[:, kc, :], rstdS[:], ALU.mult, (XS, "rstdS"), ("uA",))
                for bb in range(NBS):
                    actf(hS[:, kc, 4 * bb:4 * bb + 4], uA[:, 4 * bb:4 * bb + 4], AF.Identity, ("uA", "mod", Acoef[1]), ("hS",),
                         scale=Acoef[0][:, kc, 1 + 4 * grp + bb:2 + 4 * grp + bb], bias=mod[:, shoff + kc, 1 + 4 * grp + bb:2 + 4 * grp + bb])
        norm_mod_s((A1, "A1"), 0)

        def proj_s(key, col0, ncols, chunk_ms):
            wt, wr = load_w(key, winfm[l][:, col0:col0 + ncols], D, ncols)
            out = []
            for ii, M_ in enumerate(chunk_ms):
                b = bank(); out.append(b)
                for kc in range(KC):
                    mm(banks[b][:M_, :NS], wt[:, kc, ii * 128:ii * 128 + M_], hS[:, kc, :], kc == 0, kc == KC - 1, (wr, "hS"), (f"ps{b}",))
            return out
        def rope_s(out, outres, bp, bs, tab):
            tt('dve', uB[:], banks[bp][:, :NS], ropeS[:, tab, :], ALU.mult, (f"ps{bp}", "ropeS"), ("uB",))
            tt('dve', uC[:], banks[bs][:, :NS], ropeS[:, tab + 1, :], ALU.mult, (f"ps{bs}", "ropeS"), ("uC",))
            tt('pool', out, uB[:], uC[:], ALU.add, ("uB", "uC"), (outres,))

        bl = proj_s(("fm", l, 0), 0, 512, [128] * 4)
        for c in range(2):
            cp('act', xpS[:, c, :, 3:7], banks[bl[c]][:, :NS].rearrange("p (b i) -> p b i", i=4), (f"ps{bl[c]}",), ("xpS",))
        for c in range(2):
            b = bl[2 + c]
            cp('act', uD[:], banks[b][:, :NS], (f"ps{b}",), ("uD",))
            tt('dve', uB[:], uD[:], uD[:], ALU.mult, ("uD",), ("uB",))
            ts('dve', uB[:], uB[:], 0.044715, 1.0, ALU.mult, ALU.add, ("uB",), ("uB",))
            tt('dve', uB[:], uB[:], uD[:], ALU.mult, ("uB", "uD"), ("uB",))
            actf(uB[:], uB[:], AF.Sigmoid, ("uB",), ("uB",), scale=GC)
            tt('dve', uD[:], uD[:], uB[:], ALU.mult, ("uD", "uB"), ("uD",))
            cw = lambda k: vecs[:, V_CW + c * 4 + k:V_CW + c * 4 + k + 1]
            uA3 = uA[:].rearrange("p (b i) -> p b i", i=4)
            ts('dve', uA3, xpS[:, c, :, 3:7], cw(3), vecs[:, V_CB + c:V_CB + c + 1], ALU.mult, ALU.add, ("xpS", "vecs"), ("uA",))
            for k in range(3):
                stt(uA3, xpS[:, c, :, k:k + 4], cw(k), uA3, ALU.mult, ALU.add, ("xpS", "vecs", "uA"), ("uA",))
            dma_sp(s_lc_g[l][:, c, :, :], xpS[:, c, :, 4:7], ("xpS",), ("s_lc",))
            cp('act', xabS[:], uA[:], ("uA",), ("xabS",))
            ba_ = bank(); bx_ = bank()
            mm(banks[ba_][:, :NS], lruw[:, c, :], xabS[:], True, True, ("lruw", "xabS"), (f"ps{ba_}",))
            mm(banks[bx_][:, :NS], lruw[:, 2 + c, :], xabS[:], True, True, ("lruw", "xabS"), (f"ps{bx_}",))
            actf(uB[:], banks[ba_][:, :NS], AF.Sigmoid, (f"ps{ba_}", "vecs"), ("uB",), bias=vecs[:, V_BA + c:V_BA + c + 1])
            actf(uC[:], banks[bx_][:, :NS], AF.Sigmoid, (f"ps{bx_}", "vecs"), ("uC",), bias=vecs[:, V_BX + c:V_BX + c + 1])
            actf(uE[:], uB[:], AF.Exp, ("uB", "cl2"), ("uE",), scale=cl2[:, c:c + 1])
            actf(uB[:], uB[:], AF.Exp, ("uB", "cl"), ("uB",), scale=cl[:, c:c + 1])
            actf(uE[:], uE[:], AF.Sqrt, ("uE",), ("uE",), scale=-1.0, bias=1.0)
            tt('dve', uC[:], uC[:], uA[:], ALU.mult, ("uC", "uA"), ("uC",))
            tt('dve', uC[:], uC[:], uE[:], ALU.mult, ("uC", "uE"), ("uC",))
            for bb in range(NBS):
                sl = slice(4 * bb, 4 * bb + 4)
                P.op('dve', lambda e, sl=sl, bb=bb, c=c: e.tensor_tensor_scan(out=uF[:, sl], data0=uB[:, sl], data1=uC[:, sl], initial=h0S[:, c, bb:bb + 1],
                                                                       op0=ALU.mult, op1=ALU.add), ("uB", "uC", "h0S"), ("uF",))
            cp('dve', hnS[:, c, :], uF[:].rearrange("p (b i) -> p b i", i=4)[:, :, 3], ("uF",), ("hnS",))
            tt('dve', catS[:, c, :], uF[:], uD[:], ALU.mult, ("uF", "uD"), ("catS",))
        dma_sp(s_lh_g[l], hnS[:], ("hnS",), ("s_lh",))

        def rope_group(cbase, tab, fn):
            bs = proj_s(("fm", l, cbase), cbase * 128, 512, [128] * 4)
            for c in range(2):
                fn(c, bs[c], bs[2 + c], tab)
        def s_dq(c, bp, bsw, tab):
            rope_s(uE[:], "uE", bp, bsw, tab)
            for g in range(4):
                ts('dve', QdS[:, g, c, :], uE[:], HM(g), None, ALU.mult, None, ("uE", "consts"), ("QdS",))
        def s_dk(c, bp, bsw, tab):
            rope_s(uE[:], "uE", bp, bsw, tab)
            cp('act', KdN[:, c, :], uE[:], ("uE",), ("KdN",))
            dma_sp(s_kd_g[l][c * 128:(c + 1) * 128, :], uE[:], ("uE",), ("s_kd",))
        def s_cq(c, bp, bsw, tab):
            rope_s(uE[:], "uE", bp, bsw, tab)
            ts('dve', QcS[:, 0, c, :], uE[:], HM(6), None, ALU.mult, None, ("uE", "consts"), ("QcS",))
            ts('dve', QcS[:, 1, c, :], uE[:], HM(7), None, ALU.mult, None, ("uE", "consts"), ("QcS",))
        def s_ck(c, bp, bsw, tab):
            rope_s(uE[:], "uE", bp, bsw, tab)
            cp('act', KcN[:, c, :], uE[:], ("uE",), ("KcN",))
            dma_sp(s_kc_g[l][c * 128:(c + 1) * 128, :], uE[:], ("uE",), ("s_kc",))
        def s_iq(c, bp, bsw, tab):
            rope_s(uE[:], "uE", bp, bsw, tab)
            for g in range(4):
                ts('dve', iqmS[:, c * 4 + g, :], uE[:], HM(g), None, ALU.mult, None, ("uE", "consts"), ("iqmS",))
        rope_group(4, 0, s_dq); rope_group(8, 0, s_dk); rope_group(12, 2, s_cq); rope_group(16, 2, s_ck); rope_group(20, 0, s_iq)
        bs = proj_s(("fm", l, 24), 24 * 128, 512, [128] * 4)
        cp('act', gqS[:], banks[bs[0]][:, :NS], (f"ps{bs[0]}",), ("gqS",))
        cp('act', gkS_f[:], banks[bs[1]][:, :NS], (f"ps{bs[1]}",), ("gkS_f",))
        rope_s(uE[:], "uE", bs[2], bs[3], 0)
        cp('act', ikN[:], uE[:], ("uE",), ("ikN",))
        dma_sp(s_ik_g[l], uE[0:32, :], ("uE",), ("s_ik",))
        bs = proj_s(("fm", l, 28), 28 * 128, 128, [16])
        cp('act', gaS[:], banks[bs[0]][:16, :NS], (f"ps{bs[0]}",), ("gaS",))

        for pi, (c0, c1) in enumerate(((0, 512), (512, 1024), (1024, NTM))):
            wt, wr = load_w(("tm", l, c0), wintm[l][:, c0:c1], D, c1 - c0)
            for bb in range(NBS):
                b = bank()
                for kc in range(KC):
                    mm(banks[b][:4, :c1 - c0], hS[:, kc, 4 * bb:4 * bb + 4], wt[:, kc, :], kc == 0, kc == KC - 1, (wr, "hS"), (f"ps{b}",))
                if pi == 0:
                    cp('act', voS[:, :, :], banks[b][:4, 0:512].rearrange("p (v d) -> p v d", d=256), (f"ps{b}",), ("voS",))
                    cp('dve', vdN[:, bb, :, 0:64], banks[b][:4, 0:256].rearrange("p (h d) -> p h d", d=64), (f"ps{b}",), ("vdN",))
                    cp('dve', vcN[:, bb, :, 0:64], banks[b][:4, 256:512].rearrange("p (h d) -> p h d", d=64), (f"ps{b}",), ("vcN",))
                    dma_sp(s_vd_g[l][bb], voS[:, 0, :], ("voS",), ("s_vd",))
                    dma_sp(s_vc_g[l][bb], voS[:, 1, :], ("voS",), ("s_vc",))
                elif pi == 1:
                    cp('act', gvS[:, bb, :], banks[b][:4, 0:256], (f"ps{b}",), ("gvS",))
                    actf(ggS[:, bb, :], banks[b][:4, 256:512], AF.Silu, (f"ps{b}",), ("ggS",))
                else:
                    cp('act', gkS[:, bb, :], banks[b][:4, 0:128], (f"ps{b}",), ("gkS",))
            if pi == 2:
                b = bank()
                for kc in range(KC):
                    mm(banks[b][:16, 0:8], hS[:, kc, :], wt[:, kc, 128:136], kc == 0, kc == KC - 1, (wr, "hS"), (f"ps{b}",))
                cp('act', iw16[:], banks[b][:16, 0:8], (f"ps{b}",), ("iw16",))
        for r in range(8):
            dma_sp(iwR[16 * r:16 * r + 16, :], iw16[:], ("iw16",), ("iwR",))
        for bb in range(NBS):
            ts('dve', iwB[:, bb, :], iwR[:], cS[:, 136 + bb:137 + bb], None, ALU.mult, None, ("iwR", "cS"), ("iwB",))

        for bb in range(NBS):
            sl = slice(4 * bb, 4 * bb + 4)
            b = bank()
            mm(banks[b][:4, 0:128], gaS[:, sl], glaw[0:16, :], True, False, ("gaS", "glaw"), (f"ps{b}",))
            mm(banks[b][:4, 0:128], ones1[:, 0:4], glab[:, :], False, True, ("ones1", "glab"), (f"ps{b}",))
            actf(g_la[:4, :], banks[b][:4, 0:128], AF.Exp, (f"ps{b}",), ("g_la",), scale=-1.0)
            actf(g_la[:4, :], g_la[:4, :], AF.Ln, ("g_la",), ("g_la",), bias=1.0)
            ts('dve', g_la[:4, :], g_la[:4, :], -1.0 / 16.0, None, ALU.mult, None, ("g_la",), ("g_la",))
            bb_ = bank()
            mm(banks[bb_][:, 0:4], g_la[:4, :], consts[0:4, 1, 0:4], True, True, ("g_la", "consts"), (f"ps{bb_}",))
            mm(banks[bb_][:4, 128:256], consts[0:4, 3, 0:4], g_la[:4, :], False, True, ("g_la", "consts"), (f"ps{bb_}",))
            actf(g_e1[:, 0:4], banks[bb_][:, 0:4], AF.Exp, (f"ps{bb_}",), ("g_e1",))
            actf(g_e2[:, 0:4], banks[bb_][:, 0:4], AF.Exp, (f"ps{bb_}",), ("g_e2",), scale=-1.0)
            stt(g_qt[:, 0:4], gqS[:, sl], 32.0 ** -0.5, g_e1[:, 0:4], ALU.mult, ALU.mult, ("gqS", "g_e1"), ("g_qt",))
            for h in range(4):
                ts('dve', g_qm[:, h, 0:4], g_qt[:, 0:4], HM(h), None, ALU.mult, None, ("g_qt", "consts"), ("g_qm",))
            tt('dve', g_kt[:, 0:4], gkS_f[:, sl], g_e2[:, 0:4], ALU.mult, ("gkS_f", "g_e2"), ("g_kt",))
            actf(g_e2[:4, :], banks[bb_][:4, 128:256], AF.Exp, (f"ps{bb_}", "g_kt"), ("g_e2",))
            tt('dve', g_kh[:4, :], gkS[:, bb, :], g_e2[:4, :], ALU.mult, ("gkS", "g_e2"), ("g_kh",))
            ba_ = bank()
            for h in range(4):
                mm(banks[ba_][:4, h * 4:h * 4 + 4], g_kt[:, 0:4], g_qm[:, h, 0:4], h == 0, True, ("g_kt", "g_qm"), (f"ps{ba_}",))
            tt('dve', g_at[:4, :, 0:4], banks[ba_][:4, 0:16].rearrange("p (h t) -> p h t", t=4), Ubf[0:4, 0:4].unsqueeze(1).to_broadcast([4, 4, 4]), ALU.mult,
               (f"ps{ba_}", "Ubf"), ("g_at",))
            bo = bank()
            for h in range(4):
                mm(banks[bo][:4, h * 64:(h + 1) * 64], g_at[:4, h, 0:4], gvS[:, bb, h * 64:(h + 1) * 64], h == 0, False, ("g_at", "gvS"), (f"ps{bo}",))
                mm(banks[bo][:4, h * 64:(h + 1) * 64], g_qm[:, h, 0:4], SgSb[:, bb, :], False, True, ("g_qm", "SgSb"), (f"ps{bo}",))
            bs_ = bank()
            mm(banks[bs_][:, 0:256], g_kh[:4, :], gvS[:, bb, :], True, True, ("g_kh", "gvS"), (f"ps{bs_}",))
            ts('dve', SgS[:, bb, :], SgS[:, bb, :], g_e1[:, 3:4], None, ALU.mult, None, ("SgS", "g_e1"), ("SgS",))
            for h in range(4):
                stt(SgS[:, bb, :], banks[bs_][:, h * 64:(h + 1) * 64], HM(h), SgS[:, bb, :], ALU.mult, ALU.add, ("SgS", "consts", f"ps{bs_}"), ("SgS",))
            cp('act', otm[:4, :], banks[bo][:4, 0:256], (f"ps{bo}",), ("otm",))
            otm3 = otm[:4].rearrange("p (h d) -> p h d", d=64)
            tt('dve', tmf[:4, 0:4, 0:64], otm3, otm3, ALU.mult, ("otm",), ("tmf",))
            P.op('dve', lambda e: e.tensor_reduce(out=tms[:4, 0:4], in_=tmf[:4, 0:4, 0:64], axis=mybir.AxisListType.X, op=ALU.add), ("tmf",), ("tms",))
            actf(tms[:4, 0:4], tms[:4, 0:4], AF.Sqrt, ("tms",), ("tms",), scale=1.0 / 64, bias=EPS)
            recip(tms[:4, 0:4], tms[:4, 0:4], ("tms",), ("tms",))
            tt('dve', otm3, otm3, tms[:4, 0:4].unsqueeze(2).to_broadcast([4, 4, 64]), ALU.mult, ("otm", "tms"), ("otm",))
            tt('dve', otm3, otm3, bcv[:4, 64:128].unsqueeze(1).to_broadcast([4, 4, 64]), ALU.mult, ("otm", "bcv"), ("otm",))
            tt('dve', otb[:4, :], otm[:4, :], ggS[:, bb, :], ALU.mult, ("otm", "ggS"), ("otb",))
            bt = bank()
            for c in range(2):
                mm(banks[bt][:, c * 128:c * 128 + 4], otb[:4, c * 128:(c + 1) * 128], identb[:4, :4], c == 0, True, ("otb", "identb"), (f"ps{bt}",))
            for c in range(2):
                cp('act', catS[:, 6 + c, sl], banks[bt][:, c * 128:c * 128 + 4], (f"ps{bt}",), ("catS",))
        dma_sp(s_gs_g[l], SgS[:], ("SgS",), ("s_gs",))

        pgc = [0]
        def load_kv(kpool, vpool, col):
            i = pgc[0] % 2; pgc[0] += 1
            gather(kpf[i][:], kpool, ixK[:, col:col + 1], ("ixK",), (f"kpf{i}",))
            gather(vpf[i][:], vpool, ixK[:, col:col + 1], ("ixK",), (f"vpf{i}",))
            bT = bank()
            for c in range(2):
                tr(banks[bT][:, c * 128:(c + 1) * 128], kpf[i][:, c * 128:(c + 1) * 128], I_, (f"kpf{i}", "consts"), (f"ps{bT}",))
            cp('dve', kpb[i][:], banks[bT][:, 0:256].rearrange("p (c k) -> p c k", k=128), (f"ps{bT}",), (f"kpb{i}",))
            cp('act', vpb[i][:, :, 0:64], vpf[i][:].rearrange("p (h d) -> p h d", d=64), (f"vpf{i}",), (f"vpb{i}",))
            return kpb[i], f"kpb{i}", vpb[i], f"vpb{i}"

        for bb in range(NBS):
            sl = slice(4 * bb, 4 * bb + 4)
            for j in range(NPG):
                kb_, kr, vb_, vr = load_kv(pdk, pdv, bb * NPG + j)
                bS = bank()
                for m in range(2):
                    for h in range(4):
                        c = h // 2; g = (h % 2) * 2 + m; o = (m * 4 + h) * 4
                        mm(banks[bS][:, o:o + 4], kb_[:, c, :], QdS[:, g, c, sl], (m == 0 and h == 0), True, (kr, "QdS"), (f"ps{bS}",))
                actf(PTs[:, 0:32], banks[bS][:, 0:32], AF.Exp, (f"ps{bS}",), ("PTs",), scale=32.0 ** -0.5)
                for m in range(2):
                    for h in range(4):
                        o = (m * 4 + h) * 4
                        mm(banks[6 + m][:4, h * 65:(h + 1) * 65], PTs[:, o:o + 4], vb_[:, h, :], (j == 0 and h == 0), False, ("PTs", vr), (f"ps{6 + m}",))
            bS = bank()
            for m in range(2):
                for h in range(4):
                    c = h // 2; g = (h % 2) * 2 + m; o = (m * 4 + h) * 4
                    mm(banks[bS][:4, o:o + 4], KdN[:, c, sl], QdS[:, g, c, sl], (m == 0 and h == 0), True, ("KdN", "QdS"), (f"ps{bS}",))
            actf(PTn[:, 0:32], banks[bS][:4, 0:32], AF.Exp, (f"ps{bS}",), ("PTn",), scale=32.0 ** -0.5)
            tt('dve', PTn[:, 0:32].rearrange("p (g t) -> p g t", t=4), PTn[:, 0:32].rearrange("p (g t) -> p g t", t=4),
               Ubf[0:4, 0:4].unsqueeze(1).to_broadcast([4, 8, 4]), ALU.mult, ("PTn", "Ubf"), ("PTn",))
            for m in range(2):
                for h in range(4):
                    o = (m * 4 + h) * 4
                    mm(banks[6 + m][:4, h * 65:(h + 1) * 65], PTn[:, o:o + 4], vdN[:, bb, h, :], False, True, ("PTn", "vdN"), (f"ps{6 + m}",))
            finalize_tm(4, 'diff', lam_init, lambda c, bt, sl=sl: cp('act', catS[:, 2 + c, sl], banks[bt][:, c * 128:c * 128 + 4], (f"ps{bt}",), ("catS",)))

        memset('dve', scrS[:], 0.0, ("scrS",))
        for h in range(8):
            cp('dve', ZP[:, h, 112:128], iqmS[:, h, :], ("iqmS",), ("ZP",))
        ipc = [0]
        for bb in range(NBS):
            for j in range(NPG):
                r = j // PPR; col0 = (j % PPR) * 128
                i = ipc[0] % 2; ipc[0] += 1
                gather(ipf[i][:], pik, ixI[:, bb * NPG + j:bb * NPG + j + 1], ("ixI",), (f"ipf{i}",))
                cp('act', ipb[i][:], ipf[i][:], (f"ipf{i}",), (f"ipb{i}",))
                for hg in range(2):
                    bI = bank()
                    for hh in range(4):
                        h = hg * 4 + hh
                        mm(banks[bI][:, hh * 128:(hh + 1) * 128], ZP[:, h, 112 - 16 * r:240 - 16 * r], ipb[i][:], hh == 0, True, ("ZP", f"ipb{i}"), (f"ps{bI}",))
                    actf(tA[:, 0:512], banks[bI][:, :], AF.Relu, (f"ps{bI}",), ("tA",))
                    for hh in range(4):
                        h = hg * 4 + hh
                        stt(scrS[:, col0:col0 + 128], tA[:, hh * 128:(hh + 1) * 128], iwB[:, bb, h:h + 1], scrS[:, col0:col0 + 128], ALU.mult, ALU.add,
                            ("tA", "iwB", "scrS"), ("scrS",))
            bI = bank()
            for h in range(8):
                mm(banks[bI][:, h * 4:h * 4 + 4], ZP[:, h, 112:240], ikN[:, 4 * bb:4 * bb + 4], h == 0, True, ("ZP", "ikN"), (f"ps{bI}",))
            actf(tA[:, 0:32], banks[bI][:, 0:32], AF.Relu, (f"ps{bI}",), ("tA",))
            for h in range(8):
                stt(scrS[:, W:W + 4], tA[:, h * 4:h * 4 + 4], iwB[:, bb, h:h + 1], scrS[:, W:W + 4], ALU.mult, ALU.add, ("tA", "iwB", "scrS"), ("scrS",))
        P.op('dve', lambda e: e.tensor_reduce(out=bisS[:, 6:7], in_=scrS[:, 0:W + 4], axis=mybir.AxisListType.X, op=ALU.max, apply_absolute_value=True), ("scrS",), ("bisS",))
        bG = bank()
        mm(banks[bG][:, 0:1], Gm, bisS[:, 6:7], True, True, ("cS", "bisS"), (f"ps{bG}",))
        cp('dve', bisS[:, 1:2], banks[bG][:, 0:1], (f"ps{bG}",), ("bisS",))
        ts('dve', bisS[:, 0:1], bisS[:, 1:2], -1.0, None, ALU.mult, None, ("bisS",), ("bisS",))
        tt('dve', scrS[:, W:W + 4], scrS[:, W:W + 4], negS, ALU.add, ("scrS", "cS"), ("scrS",))
        for it in range(NBIS + 3):
            tt('dve', bisS[:, 2:3], bisS[:, 0:1], bisS[:, 1:2], ALU.add, ("bisS",), ("bisS",))
            ts('dve', bisS[:, 2:3], bisS[:, 2:3], 0.5, None, ALU.mult, None, ("bisS",), ("bisS",))
            P.op('dve', lambda e: e.tensor_scalar(out=mbS[:, :], in0=scrS[:, :], scalar1=bisS[:, 2:3], scalar2=0.0, op0=ALU.is_ge, op1=ALU.add,
                                                  accum_out=bisS[:, 3:4]), ("scrS", "bisS"), ("mbS", "bisS"))
            bG = bank()
            mm(banks[bG][:, 0:1], Gm, bisS[:, 3:4], True, True, ("cS", "bisS"), (f"ps{bG}",))
            ts('dve', bisS[:, 4:5], banks[bG][:, 0:1], float(topk_s) - 0.5, None, ALU.is_gt, None, (f"ps{bG}",), ("bisS",))
            tt('dve', bisS[:, 5:6], bisS[:, 2:3], bisS[:, 0:1], ALU.subtract, ("bisS",), ("bisS",))
            stt(bisS[:, 0:1], bisS[:, 5:6], bisS[:, 4:5], bisS[:, 0:1], ALU.mult, ALU.add, ("bisS",), ("bisS",))
            tt('dve', bisS[:, 5:6], bisS[:, 1:2], bisS[:, 2:3], ALU.subtract, ("bisS",), ("bisS",))
            stt(bisS[:, 1:2], bisS[:, 5:6], bisS[:, 4:5], bisS[:, 2:3], ALU.mult, ALU.add, ("bisS",), ("bisS",))
        P.op('dve', lambda e: e.tensor_scalar(out=mbS[:, :], in0=scrS[:, :], scalar1=bisS[:, 0:1], scalar2=-30000.0, op0=ALU.is_lt, op1=ALU.mult),
             ("scrS", "bisS"), ("mbS",))
        for bb in range(NBS):
            sl = slice(4 * bb, 4 * bb + 4)
            for j in range(NPG):
                r = j // PPR; col0 = (j % PPR) * 128
                kb_, kr, vb_, vr = load_kv(pck, pcv, bb * NPG + j)
                bS = bank()
                for h in range(4):
                    c = h // 2
                    mm(banks[bS][:, h * 4:h * 4 + 4], kb_[:, c, :], QcS[:, h % 2, c, sl], h == 0, False, (kr, "QcS"), (f"ps{bS}",))
                    mm(banks[bS][:, h * 4:h * 4 + 4], mbS[:, col0:col0 + 128], identb[:, 16 * r + 4 * bb:16 * r + 4 * bb + 4], False, True,
                       ("mbS", "identb"), (f"ps{bS}",))
                actf(PTs[:, 0:16], banks[bS][:, 0:16], AF.Exp, (f"ps{bS}",), ("PTs",), scale=0.125)
                for h in range(4):
                    mm(banks[6][:4, h * 65:(h + 1) * 65], PTs[:, h * 4:h * 4 + 4], vb_[:, h, :], (j == 0 and h == 0), False, ("PTs", vr), ("ps6",))
            bS = bank()
            for h in range(4):
                c = h // 2
                mm(banks[bS][:4, h * 4:h * 4 + 4], KcN[:, c, sl], QcS[:, h % 2, c, sl], h == 0, False, ("KcN", "QcS"), (f"ps{bS}",))
                mm(banks[bS][:4, h * 4:h * 4 + 4], mbS[:, W:W + 4], identb[:, 4 * bb:4 * bb + 4], False, True, ("mbS", "identb"), (f"ps{bS}",))
            actf(PTn[:, 0:16], banks[bS][:4, 0:16], AF.Exp, (f"ps{bS}",), ("PTn",), scale=0.125)
            for h in range(4):
                mm(banks[6][:4, h * 65:(h + 1) * 65], PTn[:, h * 4:h * 4 + 4], vcN[:, bb, h, :], False, True, ("PTn", "vcN"), ("ps6",))
            finalize_tm(4, 'dsa', lam_init, lambda c, bt, sl=sl: cp('act', catS[:, 4 + c, sl], banks[bt][:, c * 128:c * 128 + 4], (f"ps{bt}",), ("catS",)))

        for half in range(2):
            wt, wr = load_w(("out", l, half), wout[l][:, half * 512:(half + 1) * 512], D, 512)
            for oc in range(4):
                o = half * 4 + oc
                b = bank()
                for kc in range(KC):
                    mm(banks[b][:, :NS], wt[:, kc, oc * 128:(oc + 1) * 128], catS[:, kc, :], kc == 0, kc == KC - 1, (wr, "catS"), (f"ps{b}",))
                for bb in range(NBS):
                    sl = slice(4 * bb, 4 * bb + 4)
                    stt(xsS[:, o, sl], banks[b][:, sl], mod[:, 16 + o, 1 + 4 * grp + bb:2 + 4 * grp + bb], xsS[:, o, sl], ALU.mult, ALU.add, (f"ps{b}", "mod", XS), (XS,))
        norm_mod_s((A2, "A2"), 24)
        for c0 in range(0, FC, 4):
            ncg = min(4, FC - c0)
            wtg, wrg = load_w(("g", l, c0), wg[l][:, c0 * 128:(c0 + ncg) * 128], D, ncg * 128)
            wtu, wru = load_w(("u", l, c0), wu[l][:, c0 * 128:(c0 + ncg) * 128], D, ncg * 128)
            for ii in range(ncg):
                c = c0 + ii
                bg = bank(); bu = bank()
                for kc in range(KC):
                    mm(banks[bg][:, :NS], wtg[:, kc, ii * 128:(ii + 1) * 128], hS[:, kc, :], kc == 0, kc == KC - 1, (wrg, "hS"), (f"ps{bg}",))
                for kc in range(KC):
                    mm(banks[bu][:, :NS], wtu[:, kc, ii * 128:(ii + 1) * 128], hS[:, kc, :], kc == 0, kc == KC - 1, (wru, "hS"), (f"ps{bu}",))
                fw = lambda k: vecs[:, V_FCW + c * 3 + k:V_FCW + c * 3 + k + 1]
                cp('dve', fpad[:, :, 0:2], fcS[:, c, :, :], ("fcS",), ("fpad",))
                cp('act', fpad[:, :, 2:6], banks[bg][:, :NS].rearrange("p (b i) -> p b i", i=4), (f"ps{bg}",), ("fpad",))
                uA3 = uA[:].rearrange("p (b i) -> p b i", i=4)
                ts('dve', uA3, fpad[:, :, 2:6], fw(2), vecs[:, V_FCB + c:V_FCB + c + 1], ALU.mult, ALU.add, ("fpad", "vecs"), ("uA",))
                stt(uA3, fpad[:, :, 1:5], fw(1), uA3, ALU.mult, ALU.add, ("fpad", "vecs", "uA"), ("uA",))
                stt(uA3, fpad[:, :, 0:4], fw(0), uA3, ALU.mult, ALU.add, ("fpad", "vecs", "uA"), ("uA",))
                cp('dve', fcSn[:, c, :, :], fpad[:, :, 4:6], ("fpad",), ("fcSn",))
                actf(uA[:], uA[:], AF.Silu, ("uA",), ("uA",))
                tt('dve', actS[:, c, :], uA[:], banks[bu][:, :NS], ALU.mult, ("uA", f"ps{bu}"), ("actS",))
        dma_sp(s_fc_g[l], fcSn[:], ("fcSn",), ("s_fc",))
        for o in range(KC):
            wt, wr = load_w(("d", l, o), wd[l][:, o * 128:(o + 1) * 128], DFF, 128)
            b = bank()
            for kc in range(FC):
                mm(banks[b][:, :NS], wt[:, kc, :], actS[:, kc, :], kc == 0, kc == FC - 1, (wr, "actS"), (f"ps{b}",))
            for bb in range(NBS):
                sl = slice(4 * bb, 4 * bb + 4)
                stt(xsS[:, o, sl], banks[b][:, sl], mod[:, 40 + o, 1 + 4 * grp + bb:2 + 4 * grp + bb], xsS[:, o, sl], ALU.mult, ALU.add, (f"ps{b}", "mod", XS), (XS,))
        if l == L - 1:
            b = bank()
            for kc in range(KC):
                actf(sqS[:], xsS[:, kc, :], AF.Square, (XS,), ("sqS",))
                mm(banks[b][:, :NS], onesb[:], sqS[:], kc == 0, kc == KC - 1, ("sqS", "onesb"), (f"ps{b}",))
            actf(rstdS[:], banks[b][:, :NS], AF.Sqrt, (f"ps{b}",), ("rstdS",), scale=1.0 / D, bias=EPS)
            recip(rstdS[:], rstdS[:], ("rstdS",), ("rstdS",))
            for kc in range(KC):
                stt(xsS[:, kc, :], xsS[:, kc, :], fng[:, kc:kc + 1], rstdS[:], ALU.mult, ALU.mult, (XS, "fng", "rstdS"), (XS,))
            dma_sp(ysT_g, xsS[:], (XS,), ("ysT",))
        P.op('dve', lambda e: e.memset(bisS[:, 7:8], 0.0), (), FENCE + ("bisS",))

    dma_sp(consts[:], consts_in, (), ("consts",))
    cp('dve', identb[:], I_, ("consts",), ("identb",))
    cp('dve', Ubf[:], U_, ("consts",), ("Ubf",))
    memset('pool', onesb[:], 1.0, ("onesb",))
    memset('pool', ones1[:], 1.0, ("ones1",))
    dma_sp(fng[:], fng_in, (), ("fng",))
    dma_sp(cTt[:], cT_in, (), ("cTt",))
    actf(csil[:], cTt[:], AF.Silu, ("cTt",), ("csil",))
    memset('pool', Vd[:, :, :, 64:65], 1.0, ("Vd",))
    memset('pool', Vc[:, :, :, 64:65], 1.0, ("Vc",))

    if SAMPLE:
        sample_setup()
    ckpt(1)
    for l in range(L):
        lam_init = lam_inits[l]
        dma_sp(vecs[:], vecs_in[l], (), ("vecs",))
        dma_pool(lruw[:], lruw_in[l], (), ("lruw",))
        dma_pool(glaw[:], glaw_in[l], (), ("glaw",))
        dma_pool(glab[:], glaw_in[l, 16:17, :], (), ("glab",))
        dma_sp(bcv[:], bc_in[l].partition_broadcast(128), (), ("bcv",))
        for pc in range(12):
            wt, wr = load_w(("ada", l, pc), wada[l][:, pc * 512:(pc + 1) * 512], D, 512)
            b = bank()
            for j in range(4):
                for kc in range(KC):
                    mm(banks[b][:, j * 16:j * 16 + NBC], wt[:, kc, j * 128:(j + 1) * 128], csil[:, kc, :], kc == 0, kc == KC - 1,
                       (wr, "csil"), (f"ps{b}",))
            for j in range(4):
                ch = pc * 4 + j
                actf(mod[:, ch, :], banks[b][:, j * 16:j * 16 + NBC], AF.Identity, (f"ps{b}", "vecs"), ("mod",),
                     bias=vecs[:, V_BADA + ch:V_BADA + ch + 1])
        ckpt(2)
        stt(A1[:], mod[:, 8:16, :], 1.0, vecs[:, V_N1:V_N1 + 8].unsqueeze(2).to_broadcast([128, 8, NBC]), ALU.add, ALU.mult, ("mod", "vecs"), ("A1",))
        stt(A2[:], mod[:, 32:40, :], 1.0, vecs[:, V_N2:V_N2 + 8].unsqueeze(2).to_broadcast([128, 8, NBC]), ALU.add, ALU.mult, ("mod", "vecs"), ("A2",))
        actf(cl[:], vecs[:, V_LAM:V_LAM + 2], AF.Exp, ("vecs",), ("cl",), scale=-1.0)
        actf(cl[:], cl[:], AF.Ln, ("cl",), ("cl",), bias=1.0)
        ts('dve', cl2[:], cl[:], -16.0, None, ALU.mult, None, ("cl",), ("cl2",))
        ts('dve', cl[:], cl[:], -8.0, None, ALU.mult, None, ("cl", "cl2"), ("cl",))
        tt('dve', tms[:, 0:32], bcv[:, 128:160], bcv[:, 160:192], ALU.mult, ("bcv",), ("tms",))
        P.op('dve', lambda e: e.reduce_sum(out=lamt[:, 0:1], in_=tms[:, 0:32], axis=mybir.AxisListType.X), ("tms",), ("lamt",))
        tt('dve', tms[:, 0:32], bcv[:, 192:224], bcv[:, 224:256], ALU.mult, ("bcv", "lamt"), ("tms",))
        P.op('dve', lambda e: e.reduce_sum(out=lamt[:, 1:2], in_=tms[:, 0:32], axis=mybir.AxisListType.X), ("tms",), ("lamt",))
        actf(lamt[:, 0:2], lamt[:, 0:2], AF.Exp, ("lamt",), ("lamt",))
        tt('dve', lamt[:, 2:3], lamt[:, 0:1], lamt[:, 1:2], ALU.subtract, ("lamt",), ("lamt",))
        ts('dve', lamt[:, 3:4], lamt[:, 2:3], lam_init, -1.0, ALU.add, ALU.mult, ("lamt",), ("lamt",))
        memset('pool', Sg[:], 0.0, ("Sg",)); memset('pool', Sgb[:], 0.0, ("Sgb",))
        memset('pool', xpad[:, :, 0:3], 0.0, ("xpad",))
        memset('pool', fcs[:], 0.0, ("fcs",))
        memset('pool', hlast[:], 0.0, ("hlast",))

        ckpt(3)
        xsrc = xT_in if l == 0 else xs[l - 1]
        xdst = None if l == L - 1 else xs[l]
        for j in range(NB):
            t0 = j * BLK
            N = BLK
            dma_sp(xt[:], xsrc.rearrange("(k p) t -> p k t", p=128)[:, :, t0:t0 + N], (f"xs{l-1}",) if l > 0 else (), ("xt",))
            dma_sp(ropet[:], rope_in.rearrange("r p t -> p r t")[:, :, t0:t0 + N], (), ("ropet",))

            def norm_mod(Acoef, shoff):
                b = bank()
                for kc in range(KC):
                    actf(sqb[:], xt[:, kc, :], AF.Square, ("xt",), ("sqb",))
                    mm(banks[b][:, :N], onesb[:], sqb[:], kc == 0, kc == KC - 1, ("sqb", "onesb"), (f"ps{b}",))
                actf(rstd[:], banks[b][:, :N], AF.Sqrt, (f"ps{b}",), ("rstd",), scale=1.0 / D, bias=EPS)
                recip(rstd[:], rstd[:], ("rstd",), ("rstd",))
                for kc in range(KC):
                    tt('dve', tA[:, :N], xt[:, kc, :], rstd[:], ALU.mult, ("xt", "rstd"), ("tA",))
                    actf(hT[:, kc, :], tA[:, :N], AF.Identity, ("tA", "mod", Acoef[1]), ("hT",),
                         scale=Acoef[0][:, kc, 0:1], bias=mod[:, shoff + kc, 0:1])
            norm_mod((A1, "A1"), 0)
            ckpt(4)

            def fm_proj(chunks, handler):
                c0 = chunks[0]; nch = len(chunks)
                wt, wr = load_w(("fm", l, c0), winfm[l][:, c0 * 128:(c0 + nch) * 128], D, nch * 128)
                for ii, ci in enumerate(chunks):
                    b = bank()
                    for kc in range(KC):
                        mm(banks[b][:, :N], wt[:, kc, ii * 128:(ii + 1) * 128], hT[:, kc, :], kc == 0, kc == KC - 1, (wr, "hT"), (f"ps{b}",))
                    handler(ci, b)

            def rope_out(out, outres, bp, bs, tab):
                tt('dve', tB[:], banks[bp][:, :N], ropet[:, tab, :], ALU.mult, (f"ps{bp}", "ropet"), ("tB",))
                tt('dve', tC[:], banks[bs][:, :N], ropet[:, tab + 1, :], ALU.mult, (f"ps{bs}", "ropet"), ("tC",))
                tt('pool', out, tB[:], tC[:], ALU.add, ("tB", "tC"), (outres,))

            def h_lru(ci, b):
                if ci < 2:
                    cp('act', xpad[:, ci, 3:3 + N], banks[b][:, :N], (f"ps{b}",), ("xpad",))
                else:
                    c = ci - 2
                    cp('act', tD[:], banks[b][:, :N], (f"ps{b}",), ("tD",))
                    tt('dve', tB[:], tD[:], tD[:], ALU.mult, ("tD",), ("tB",))
                    ts('dve', tB[:], tB[:], 0.044715, 1.0, ALU.mult, ALU.add, ("tB",), ("tB",))
                    tt('dve', tB[:], tB[:], tD[:], ALU.mult, ("tB", "tD"), ("tB",))
                    actf(tB[:], tB[:], AF.Sigmoid, ("tB",), ("tB",), scale=GC)
                    tt('dve', tD[:], tD[:], tB[:], ALU.mult, ("tD", "tB"), ("tD",))
                    lru_chunk(c)
            def lru_chunk(c):
                cw = lambda k: vecs[:, V_CW + c * 4 + k:V_CW + c * 4 + k + 1]
                ts('dve', tA[:, :N], xpad[:, c, 3:3 + N], cw(3), vecs[:, V_CB + c:V_CB + c + 1], ALU.mult, ALU.add, ("xpad", "vecs"), ("tA",))
                for k in range(3):
                    stt(tA[:, :N], xpad[:, c, k:k + N], cw(k), tA[:, :N], ALU.mult, ALU.add, ("xpad", "vecs", "tA"), ("tA",))
                if j == NB - 1:
                    dma_sp(o_lc[l][:, c, :], xpad[:, c, N:N + 3], ("xpad",), ("o_lc",))
                else:
                    cp('pool', xpad[:, c, 0:3], xpad[:, c, N:N + 3], ("xpad",), ("xpad",)) if False else None
                cp('act', xab[:], tA[:, :N], ("tA",), ("xab",))
                ba_ = bank(); bx_ = bank()
                mm(banks[ba_][:, :N], lruw[:, c, :], xab[:], True, True, ("lruw", "xab"), (f"ps{ba_}",))
                mm(banks[bx_][:, :N], lruw[:, 2 + c, :], xab[:], True, True, ("lruw", "xab"), (f"ps{bx_}",))
                actf(tB[:], banks[ba_][:, :N], AF.Sigmoid, (f"ps{ba_}", "vecs"), ("tB",), bias=vecs[:, V_BA + c:V_BA + c + 1])
                actf(tC[:], banks[bx_][:, :N], AF.Sigmoid, (f"ps{bx_}", "vecs"), ("tC",), bias=vecs[:, V_BX + c:V_BX + c + 1])
                actf(tE[:], tB[:], AF.Exp, ("tB", "cl2"), ("tE",), scale=cl2[:, c:c + 1])
                actf(tB[:], tB[:], AF.Exp, ("tB", "cl"), ("tB",), scale=cl[:, c:c + 1])
                actf(tE[:], tE[:], AF.Sqrt, ("tE",), ("tE",), scale=-1.0, bias=1.0)
                if j == 0:
                    memset('dve', tE[:, 0:1], 1.0, ("tE",))
                tt('dve', tC[:], tC[:], tA[:, :N], ALU.mult, ("tC", "tA"), ("tC",))
                tt('dve', tC[:], tC[:], tE[:], ALU.mult, ("tC", "tE"), ("tC",))
                P.op('dve', lambda e: e.tensor_tensor_scan(out=tA[:, :N], data0=tB[:], data1=tC[:], initial=hlast[:, c:c + 1],
                                                           op0=ALU.mult, op1=ALU.add), ("tB", "tC", "hlast"), ("tA",))
                cp('dve', hlast[:, c:c + 1], tA[:, N - 1:N], ("tA",), ("hlast",))
                tt('dve', catT[:, c, :], tA[:, :N], tD[:], ALU.mult, ("tA", "tD"), ("catT",))
            if j > 0:
                for c in range(2):
                    cp('dve', xpad[:, c, 0:3], xpad[:, c, N:N + 3], ("xpad",), ("xpad",))
            fm_proj([0, 1, 2, 3], h_lru)
            ckpt(5)
            if j == NB - 1:
                dma_sp(o_lh[l], hlast[:], ("hlast",), ("o_lh",))

            pend = {}
            def h_rope(ci, b):
                pend[ci] = b
            def do_rope(cbase, tab, dest_fn):
                wt, wr = load_w(("fm", l, cbase), winfm[l][:, cbase * 128:(cbase + 4) * 128], D, 512)
                bs = [bank() for _ in range(4)]
                for ii in range(4):
                    b = bs[ii]
                    for kc in range(KC):
                        mm(banks[b][:, :N], wt[:, kc, ii * 128:(ii + 1) * 128], hT[:, kc, :], kc == 0, kc == KC - 1, (wr, "hT"), (f"ps{b}",))
                for c in range(2):
                    dest_fn(c, bs[c], bs[2 + c], tab)
            def dq(c, bp, bsw, tab):
                rope_out(tE[:], "tE", bp, bsw, tab)
                for g in range(4):
                    ts('pool' if g % 2 else 'dve', QdT[:, g, c, :], tE[:], HM(g), None, ALU.mult, None, ("tE", "consts"), ("QdT",))
            def dk(c, bp, bsw, tab):
                rope_out(kdf[:], "kdf", bp, bsw, tab)
                cp('act', KdT[:, c, t0:t0 + N], kdf[:], ("kdf",), ("KdT",))
                dma_sp(o_kd[l][c * 128:(c + 1) * 128, t0:t0 + N], kdf[:], ("kdf",), ("o_kd",))
            def cq(c, bp, bsw, tab):
                rope_out(tE[:], "tE", bp, bsw, tab)
                ts('dve', QcT[:, 0, c, :], tE[:], HM(6), None, ALU.mult, None, ("tE", "consts"), ("QcT",))
                ts('pool', QcT[:, 1, c, :], tE[:], HM(7), None, ALU.mult, None, ("tE", "consts"), ("QcT",))
            def ck(c, bp, bsw, tab):
                rope_out(kdf[:], "kdf", bp, bsw, tab)
                cp('act', KcT[:, c, t0:t0 + N], kdf[:], ("kdf",), ("KcT",))
                dma_sp(o_kc[l][c * 128:(c + 1) * 128, t0:t0 + N], kdf[:], ("kdf",), ("o_kc",))
            def iq(c, bp, bsw, tab):
                rope_out(tE[:], "tE", bp, bsw, tab)
                for g in range(4):
                    ts('pool' if g % 2 else 'dve', iqm[:, c * 4 + g, :], tE[:], HM(g), None, ALU.mult, None, ("tE", "consts"), ("iqm",))
            do_rope(4, 0, dq); do_rope(8, 0, dk); do_rope(12, 2, cq); do_rope(16, 2, ck); do_rope(20, 0, iq)
            wt, wr = load_w(("fm", l, 24), winfm[l][:, 24 * 128:28 * 128], D, 512)
            bs = [bank() for _ in range(4)]
            for ii in range(4):
                b = bs[ii]
                for kc in range(KC):
                    mm(banks[b][:, :N], wt[:, kc, ii * 128:(ii + 1) * 128], hT[:, kc, :], kc == 0, kc == KC - 1, (wr, "hT"), (f"ps{b}",))
            cp('act', gqT[:], banks[bs[0]][:, :N], (f"ps{bs[0]}",), ("gqT",))
            cp('act', gkT[:], banks[bs[1]][:, :N], (f"ps{bs[1]}",), ("gkT",))
            rope_out(kdf[:], "kdf", bs[2], bs[3], 0)
            cp('act', ikT[:, t0:t0 + N], kdf[:], ("kdf",), ("ikT",))
            dma_sp(o_ik[l][:, t0:t0 + N], kdf[0:32, :], ("kdf",), ("o_ik",))
            wt, wr = load_w(("fm", l, 28), winfm[l][:, 28 * 128:29 * 128], D, 128)
            b = bank()
            for kc in range(KC):
                mm(banks[b][:16, :N], wt[:, kc, 0:16], hT[:, kc, :], kc == 0, kc == KC - 1, (wr, "hT"), (f"ps{b}",))
            cp('act', gaT[:], banks[b][:16, :N], (f"ps{b}",), ("gaT",))

            ckpt(6)
            for pi, (c0, c1) in enumerate(((0, 512), (512, 1024), (1024, NTM))):
                wt, wr = load_w(("tm", l, c0), wintm[l][:, c0:c1], D, c1 - c0)
                for q in range(QPB):
                    qt = j * QPB + q
                    tok = slice(q * 128, (q + 1) * 128)
                    b = bank()
                    for kc in range(KC):
                        mm(banks[b][:, :c1 - c0], hT[:, kc, tok], wt[:, kc, :], kc == 0, kc == KC - 1, (wr, "hT"), (f"ps{b}",))
                    if pi == 0:
                        cp('act', vout[:, 0, :], banks[b][:, 0:256], (f"ps{b}",), ("vout",))
                        dma_sp(o_vd[l][t0 + q * 128:t0 + (q + 1) * 128, :], vout[:, 0, :], ("vout",), ("o_vd",))
                        cp('dve', Vd[:, qt, :, 0:64], banks[b][:, 0:256].rearrange("p (h d) -> p h d", d=64), (f"ps{b}",), ("Vd",))
                        cp('dve', Vc[:, qt, :, 0:64], banks[b][:, 256:512].rearrange("p (h d) -> p h d", d=64), (f"ps{b}",), ("Vc",))
                        cp('act', vout[:, 0, :], banks[b][:, 256:512], (f"ps{b}",), ("vout",))
                        dma_sp(o_vc[l][t0 + q * 128:t0 + (q + 1) * 128, :], vout[:, 0, :], ("vout",), ("o_vc",))
                    elif pi == 1:
                        cp('act', gv[:, q, :], banks[b][:, 0:256], (f"ps{b}",), ("gv",))
                        actf(gg[:, q, :], banks[b][:, 256:512], AF.Silu, (f"ps{b}",), ("gg",))
                    else:
                        cp('act', gk[:, q, :], banks[b][:, 0:128], (f"ps{b}",), ("gk",))
                        cp('act', iw[:, q, :], banks[b][:, 128:136], (f"ps{b}",), ("iw",))
            ckpt(7)
            for q in range(QPB):
                qt = j * QPB + q
                tok = slice(q * 128, (q + 1) * 128)
                b = bank()
                mm(banks[b][:, 0:128], gaT[:, tok], glaw[0:16, :], True, False, ("gaT", "glaw"), (f"ps{b}",))
                mm(banks[b][:, 0:128], ones1[:, :], glab[:, :], False, True, ("ones1", "glab"), (f"ps{b}",))
                actf(g_la[:], banks[b][:, 0:128], AF.Exp, (f"ps{b}",), ("g_la",), scale=-1.0)
                actf(g_la[:], g_la[:], AF.Ln, ("g_la",), ("g_la",), bias=1.0)
                ts('dve', g_la[:], g_la[:], -1.0 / 16.0, None, ALU.mult, None, ("g_la",), ("g_la",))
                bb = bank()
                mm(banks[bb][:, 0:128], g_la[:], U_, True, True, ("g_la", "consts"), (f"ps{bb}",))
                mm(banks[bb][:, 128:256], L_, g_la[:], False, True, ("g_la", "consts"), (f"ps{bb}",))
                actf(g_e1[:], banks[bb][:, 0:128], AF.Exp, (f"ps{bb}",), ("g_e1",))
                actf(g_e2[:], banks[bb][:, 0:128], AF.Exp, (f"ps{bb}",), ("g_e2",), scale=-1.0)
                stt(g_qt[:], gqT[:, tok], 32.0 ** -0.5, g_e1[:], ALU.mult, ALU.mult, ("gqT", "g_e1"), ("g_qt",))
                for h in range(4):
                    ts('pool' if h % 2 else 'dve', g_qm[:, h, :], g_qt[:], HM(h), None, ALU.mult, None, ("g_qt", "consts"), ("g_qm",))
                tt('dve', g_kt[:], gkT[:, tok], g_e2[:], ALU.mult, ("gkT", "g_e2"), ("g_kt",))
                actf(g_e2[:], banks[bb][:, 128:256], AF.Exp, (f"ps{bb}", "g_kt"), ("g_e2",))
                tt('dve', g_kh[:], gk[:, q, :], g_e2[:], ALU.mult, ("gk", "g_e2"), ("g_kh",))
                ba_ = bank()
                for h in range(4):
                    mm(banks[ba_][:, h * 128:(h + 1) * 128], g_kt[:, :], g_qm[:, h, :], h == 0, True, ("g_kt", "g_qm"), (f"ps{ba_}",))
                tt('dve', g_at[:], banks[ba_][:, :].rearrange("p (h t) -> p h t", t=128), Ubf[:].unsqueeze(1).to_broadcast([128, 4, 128]), ALU.mult,
                   (f"ps{ba_}", "Ubf"), ("g_at",))
                bo = bank()
                for h in range(4):
                    mm(banks[bo][:, h * 64:(h + 1) * 64], g_at[:, h, :], gv[:, q, h * 64:(h + 1) * 64], h == 0, False, ("g_at", "gv"), (f"ps{bo}",))
                    mm(banks[bo][:, h * 64:(h + 1) * 64], g_qm[:, h, :], Sgb[:, :], False, True, ("g_qm", "Sgb"), (f"ps{bo}",))
                bs_ = bank()
                mm(banks[bs_][:, 0:256], g_kh[:, :], gv[:, q, :], True, True, ("g_kh", "gv"), (f"ps{bs_}",))
                ts('dve', Sg[:], Sg[:], g_e1[:, 127:128], None, ALU.mult, None, ("Sg", "g_e1"), ("Sg",))
                for h in range(4):
                    stt(Sg[:], banks[bs_][:, h * 64:(h + 1) * 64], HM(h), Sg[:], ALU.mult, ALU.add, ("Sg", "consts", f"ps{bs_}"), ("Sg",))
                cp('act', Sgb[:], Sg[:], ("Sg",), ("Sgb",))
                if qt == NT - 1:
                    dma_sp(o_gs[l], Sg[:], ("Sg",), ("o_gs",))
                o3 = banks[bo][:, 0:256].rearrange("p (h d) -> p h d", d=64)
                tt('dve', otm[:].rearrange("p (h d) -> p h d", d=64), o3, o3, ALU.mult, (f"ps{bo}",), ("otm",)) if False else None
                cp('act', otm[:], banks[bo][:, 0:256], (f"ps{bo}",), ("otm",))
                otm3 = otm[:].rearrange("p (h d) -> p h d", d=64)
                tt('dve', tmf[:, 0:4, 0:64], otm3, otm3, ALU.mult, ("otm",), ("tmf",))
                P.op('dve', lambda e: e.tensor_reduce(out=tms[:, 0:4], in_=tmf[:, 0:4, 0:64], axis=mybir.AxisListType.X, op=ALU.add), ("tmf",), ("tms",))
                actf(tms[:, 0:4], tms[:, 0:4], AF.Sqrt, ("tms",), ("tms",), scale=1.0 / 64, bias=EPS)
                recip(tms[:, 0:4], tms[:, 0:4], ("tms",), ("tms",))
                tt('dve', otm3, otm3, tms[:, 0:4].unsqueeze(2).to_broadcast([128, 4, 64]), ALU.mult, ("otm", "tms"), ("otm",))
                tt('dve', otm3, otm3, bcv[:, 64:128].unsqueeze(1).to_broadcast([128, 4, 64]), ALU.mult, ("otm", "bcv"), ("otm",))
                tt('dve', otb[:], otm[:], gg[:, q, :], ALU.mult, ("otm", "gg"), ("otb",))
                bt = bank()
                for c in range(2):
                    mm(banks[bt][:, c * 128:(c + 1) * 128], otb[:, c * 128:(c + 1) * 128], identb[:], c == 0, True, ("otb", "identb"), (f"ps{bt}",))
                cp('act', catT[:, 6:8, tok], banks[bt][:, 0:256].rearrange("p (c t) -> p c t", t=128), (f"ps{bt}",), ("catT",))

                ckpt(8)
                for kt in range(qt + 1):
                    bsA = 4 + (kt % 2)
                    for m in range(2):
                        bS = bank()
                        for h in range(4):
                            c = h // 2; g = (h % 2) * 2 + m
                            mm(banks[bS][:, h * 128:(h + 1) * 128], KdT[:, c, kt * 128:(kt + 1) * 128], QdT[:, g, c, tok], h == 0, True,
                               ("KdT", "QdT"), (f"ps{bS}",))
                        actf(PT[:, m * 512:(m + 1) * 512], banks[bS][:, :], AF.Exp, (f"ps{bS}",), ("PT",), scale=32.0 ** -0.5)
                    ckpt(81)
                    if kt == qt:
                        tt('dve', PT[:].rearrange("p (g t) -> p g t", t=128), PT[:].rearrange("p (g t) -> p g t", t=128),
                           Ubf[:].unsqueeze(1).to_broadcast([128, 8, 128]), ALU.mult, ("PT", "Ubf"), ("PT",))
                    ckpt(82)
                    for m in range(2):
                        for h in range(4):
                            mm(banks[6 + m][:, h * 65:(h + 1) * 65], PT[:, (m * 4 + h) * 128:(m * 4 + h + 1) * 128], Vd[:, kt, h, :],
                               (kt == 0 and h == 0), kt == qt, ("PT", "Vd"), (f"ps{6 + m}",))
                ckpt(83)
                for m in range(2):
                    cp('act', tmf[:, m * 4:(m + 1) * 4, :], banks[6 + m][:, 0:260].rearrange("p (h d) -> p h d", d=65), (f"ps{6 + m}",), ("tmf",))
                recip(tms[:, 0:8], tmf[:, :, 64], ("tmf",), ("tms",))
                tt('dve', tms[:, 4:8], tms[:, 4:8], lamt[:, 3:4].to_broadcast([128, 4]), ALU.mult, ("tms", "lamt"), ("tms",))
                otm3 = otm[:].rearrange("p (h d) -> p h d", d=64)
                tt('dve', tmf[:, 0:4, 0:64], tmf[:, 0:4, 0:64], tms[:, 0:4].unsqueeze(2).to_broadcast([128, 4, 64]), ALU.mult, ("tmf", "tms"), ("tmf",))
                tt('dve', tmf[:, 4:8, 0:64], tmf[:, 4:8, 0:64], tms[:, 4:8].unsqueeze(2).to_broadcast([128, 4, 64]), ALU.mult, ("tmf", "tms"), ("tmf",))
                tt('dve', otm3, tmf[:, 0:4, 0:64], tmf[:, 4:8, 0:64], ALU.add, ("tmf",), ("otm",))
                tt('dve', tmf[:, 0:4, 0:64], otm3, otm3, ALU.mult, ("otm",), ("tmf",))
                P.op('dve', lambda e: e.tensor_reduce(out=tms[:, 8:12], in_=tmf[:, 0:4, 0:64], axis=mybir.AxisListType.X, op=ALU.add), ("tmf",), ("tms",))
                actf(tms[:, 8:12], tms[:, 8:12], AF.Sqrt, ("tms",), ("tms",), scale=1.0 / 64, bias=EPS)
                recip(tms[:, 8:12], tms[:, 8:12], ("tms",), ("tms",))
                tt('dve', otm3, otm3, tms[:, 8:12].unsqueeze(2).to_broadcast([128, 4, 64]), ALU.mult, ("otm", "tms"), ("otm",))
                stt(otb[:].rearrange("p (h d) -> p h d", d=64), otm3, 1.0 - lam_init, bcv[:, 0:64].unsqueeze(1).to_broadcast([128, 4, 64]),
                    ALU.mult, ALU.mult, ("otm", "bcv"), ("otb",))
                bt = bank()
                for c in range(2):
                    mm(banks[bt][:, c * 128:(c + 1) * 128], otb[:, c * 128:(c + 1) * 128], identb[:], c == 0, True, ("otb", "identb"), (f"ps{bt}",))
                cp('act', catT[:, 2:4, tok], banks[bt][:, 0:256].rearrange("p (c t) -> p c t", t=128), (f"ps{bt}",), ("catT",))

                ckpt(9)
                Lk = (qt + 1) * 128
                nkb = (Lk + 511) // 512
                for kb in range(nkb):
                    w_ = min(512, Lk - kb * 512)
                    ks = slice(kb * 512, kb * 512 + w_)
                    for h in range(8):
                        bI = bank()
                        mm(banks[bI][:, :w_], iqm[:, h, tok], ikT[:, ks], True, True, ("iqm", "ikT"), (f"ps{bI}",))
                        actf(tA[:, :w_], banks[bI][:, :w_], AF.Relu, (f"ps{bI}",), ("tA",))
                        if h == 0:
                            ts('dve', scr[:, ks], tA[:, :w_], iw[:, q, 0:1], None, ALU.mult, None, ("tA", "iw"), ("scr",))
                        else:
                            stt(scr[:, ks], tA[:, :w_], iw[:, q, h:h + 1], scr[:, ks], ALU.mult, ALU.add, ("tA", "iw", "scr"), ("scr",))
                P.op('dve', lambda e, Lk=Lk: e.tensor_reduce(out=bis[:, 6:7], in_=scr[:, 0:Lk], axis=mybir.AxisListType.X, op=ALU.max, apply_absolute_value=True), ("scr",), ("bis",))
                tt('dve', scr[:, qt * 128:Lk], scr[:, qt * 128:Lk], NEG_, ALU.add, ("scr", "consts"), ("scr",))
                if Lk > topk:
                    P.op('dve', lambda e, Lk=Lk: e.tensor_reduce(out=bis[:, 1:2], in_=scr[:, 0:Lk], axis=mybir.AxisListType.X, op=ALU.max), ("scr",), ("bis",))
                    ts('dve', bis[:, 0:1], bis[:, 6:7], -1.0, None, ALU.mult, None, ("bis",), ("bis",))
                    for it in range(NBIS):
                        tt('dve', bis[:, 2:3], bis[:, 0:1], bis[:, 1:2], ALU.add, ("bis",), ("bis",))
                        ts('dve', bis[:, 2:3], bis[:, 2:3], 0.5, None, ALU.mult, None, ("bis",), ("bis",))
                        P.op('dve', lambda e, Lk=Lk: e.tensor_scalar(out=mbt[:, 0:Lk], in0=scr[:, 0:Lk], scalar1=bis[:, 2:3], scalar2=0.0,
                                                                     op0=ALU.is_ge, op1=ALU.add, accum_out=bis[:, 3:4]), ("scr", "bis"), ("mbt", "bis"))
                        ts('dve', bis[:, 4:5], bis[:, 3:4], float(topk) - 0.5, None, ALU.is_gt, None, ("bis",), ("bis",))
                        tt('dve', bis[:, 5:6], bis[:, 2:3], bis[:, 0:1], ALU.subtract, ("bis",), ("bis",))
                        stt(bis[:, 0:1], bis[:, 5:6], bis[:, 4:5], bis[:, 0:1], ALU.mult, ALU.add, ("bis",), ("bis",))
                        tt('dve', bis[:, 5:6], bis[:, 1:2], bis[:, 2:3], ALU.subtract, ("bis",), ("bis",))
                        stt(bis[:, 1:2], bis[:, 5:6], bis[:, 4:5], bis[:, 2:3], ALU.mult, ALU.add, ("bis",), ("bis",))
                    thr = bis[:, 0:1]
                else:
                    memset('dve', bis[:, 0:1], -1.0e29, ("bis",))
                    thr = bis[:, 0:1]
                P.op('dve', lambda e, Lk=Lk: e.tensor_scalar(out=mbt[:, 0:Lk], in0=scr[:, 0:Lk], scalar1=bis[:, 0:1], scalar2=-30000.0,
                                                             op0=ALU.is_lt, op1=ALU.mult), ("scr", "bis"), ("mbt",))
                for kt in range(qt + 1):
                    bS = bank()
                    for h in range(4):
                        c = h // 2
                        mm(banks[bS][:, h * 128:(h + 1) * 128], KcT[:, c, kt * 128:(kt + 1) * 128], QcT[:, h % 2, c, tok], h == 0, False,
                           ("KcT", "QcT"), (f"ps{bS}",))
                        mm(banks[bS][:, h * 128:(h + 1) * 128], mbt[:, kt * 128:(kt + 1) * 128], identb[:], False, True,
                           ("mbt", "identb"), (f"ps{bS}",))
                    actf(PT[:, 0:512], banks[bS][:, :], AF.Exp, (f"ps{bS}",), ("PT",), scale=0.125)
                    for h in range(4):
                        mm(banks[6][:, h * 65:(h + 1) * 65], PT[:, h * 128:(h + 1) * 128], Vc[:, kt, h, :], (kt == 0 and h == 0), kt == qt,
                           ("PT", "Vc"), ("ps6",))
                cp('act', tmf[:, 0:4, :], banks[6][:, 0:260].rearrange("p (h d) -> p h d", d=65), ("ps6",), ("tmf",))
                recip(tms[:, 0:4], tmf[:, 0:4, 64], ("tmf",), ("tms",))
                tt('dve', otb[:].rearrange("p (h d) -> p h d", d=64), tmf[:, 0:4, 0:64], tms[:, 0:4].unsqueeze(2).to_broadcast([128, 4, 64]), ALU.mult,
                   ("tmf", "tms"), ("otb",))
                bt = bank()
                for c in range(2):
                    mm(banks[bt][:, c * 128:(c + 1) * 128], otb[:, c * 128:(c + 1) * 128], identb[:], c == 0, True, ("otb", "identb"), (f"ps{bt}",))
                cp('act', catT[:, 4:6, tok], banks[bt][:, 0:256].rearrange("p (c t) -> p c t", t=128), (f"ps{bt}",), ("catT",))

            ckpt(10)
            for half in range(2):
                wt, wr = load_w(("out", l, half), wout[l][:, half * 512:(half + 1) * 512], D, 512)
                for oc in range(4):
                    o = half * 4 + oc
                    b = bank()
                    for kc in range(KC):
                        mm(banks[b][:, :N], wt[:, kc, oc * 128:(oc + 1) * 128], catT[:, kc, :], kc == 0, kc == KC - 1, (wr, "catT"), (f"ps{b}",))
                    stt(xt[:, o, :], banks[b][:, :N], mod[:, 16 + o, 0:1], xt[:, o, :], ALU.mult, ALU.add, (f"ps{b}", "mod", "xt"), ("xt",))
            ckpt(11)
            norm_mod((A2, "A2"), 24)
            for c0 in range(0, FC, 4):
                ncg = min(4, FC - c0)
                wtg, wrg = load_w(("g", l, c0), wg[l][:, c0 * 128:(c0 + ncg) * 128], D, ncg * 128)
                wtu, wru = load_w(("u", l, c0), wu[l][:, c0 * 128:(c0 + ncg) * 128], D, ncg * 128)
                for ii in range(ncg):
                    c = c0 + ii
                    bg = bank(); bu = bank()
                    for kc in range(KC):
                        mm(banks[bg][:, :N], wtg[:, kc, ii * 128:(ii + 1) * 128], hT[:, kc, :], kc == 0, kc == KC - 1, (wrg, "hT"), (f"ps{bg}",))
                    for kc in range(KC):
                        mm(banks[bu][:, :N], wtu[:, kc, ii * 128:(ii + 1) * 128], hT[:, kc, :], kc == 0, kc == KC - 1, (wru, "hT"), (f"ps{bu}",))
                    fw = lambda k: vecs[:, V_FCW + c * 3 + k:V_FCW + c * 3 + k + 1]
                    cp('act', tB[:, 2:2 + N - 2] if False else tD[:], banks[bg][:, :N], (f"ps{bg}",), ("tD",))
                    ts('dve', tA[:, :N], tD[:], fw(2), vecs[:, V_FCB + c:V_FCB + c + 1], ALU.mult, ALU.add, ("tD", "vecs"), ("tA",))
                    stt(tA[:, 1:N], tD[:, 0:N - 1], fw(1), tA[:, 1:N], ALU.mult, ALU.add, ("tD", "vecs", "tA"), ("tA",))
                    stt(tA[:, 2:N], tD[:, 0:N - 2], fw(0), tA[:, 2:N], ALU.mult, ALU.add, ("tD", "vecs", "tA"), ("tA",))
                    stt(tA[:, 0:1], fcs[:, c, 1:2], fw(1), tA[:, 0:1], ALU.mult, ALU.add, ("fcs", "vecs", "tA"), ("tA",))
                    stt(tA[:, 0:2], fcs[:, c, 0:2], fw(0), tA[:, 0:2], ALU.mult, ALU.add, ("fcs", "vecs", "tA"), ("tA",))
                    cp('dve', fcs[:, c, :], tD[:, N - 2:N], ("tD", "tA"), ("fcs",))
                    actf(tA[:, :N], tA[:, :N], AF.Silu, ("tA",), ("tA",))
                    tt('dve', actt[:, c, :], tA[:, :N], banks[bu][:, :N], ALU.mult, ("tA", f"ps{bu}"), ("actt", "scr", "mbt"))
            if j == NB - 1:
                dma_sp(o_fc[l], fcs[:], ("fcs",), ("o_fc",))
            for o in range(KC):
                wt, wr = load_w(("d", l, o), wd[l][:, o * 128:(o + 1) * 128], DFF, 128)
                b = bank()
                for kc in range(FC):
                    mm(banks[b][:, :N], wt[:, kc, :], actt[:, kc, :], kc == 0, kc == FC - 1, (wr, "actt"), (f"ps{b}",))
                stt(xt[:, o, :], banks[b][:, :N], mod[:, 40 + o, 0:1], xt[:, o, :], ALU.mult, ALU.add, (f"ps{b}", "mod", "xt"), ("xt",))
            ckpt(12)
            if xdst is not None:
                dma_sp(xdst.rearrange("(k p) t -> p k t", p=128)[:, :, t0:t0 + N], xt[:], ("xt",), (f"xs{l}",))
            else:
                b = bank()
                for kc in range(KC):
                    actf(sqb[:], xt[:, kc, :], AF.Square, ("xt",), ("sqb",))
                    mm(banks[b][:, :N], onesb[:], sqb[:], kc == 0, kc == KC - 1, ("sqb", "onesb"), (f"ps{b}",))
                actf(rstd[:], banks[b][:, :N], AF.Sqrt, (f"ps{b}",), ("rstd",), scale=1.0 / D, bias=EPS)
                recip(rstd[:], rstd[:], ("rstd",), ("rstd",))
                for kc in range(KC):
                    stt(xt[:, kc, :], xt[:, kc, :], fng[:, kc:kc + 1], rstd[:], ALU.mult, ALU.mult, ("xt", "fng", "rstd"), ("xt",))
                dma_sp(yT.rearrange("(k p) t -> p k t", p=128)[:, :, t0:t0 + N], xt[:], ("xt",), ("yT",))
        if SAMPLE:
            for g_ in range(NG):
                sample_layer(l, lam_init, g_)
    P.emit()
    stack.close()
    return nc

IN_OFF = {}
_off = 0
for _n, _w in (("lru_x", 256), ("lru_gate", 256), ("diff_q", 256), ("diff_k", 256), ("diff_v", 256), ("dsa_q", 256), ("dsa_k", 256), ("dsa_v", 256),
               ("idx_q", 256), ("idx_k", 32), ("idx_w", 8), ("gla_q", 128), ("gla_k", 128), ("gla_v", 256), ("gla_g", 256), ("gla_a", 16)):
    IN_OFF[_n] = (_off, _off + _w); _off += _w

def _swap_cols(n, hd):
    i = np.arange(n)
    return (i // hd) * hd + ((i % hd) + hd // 2) % hd

def make_consts():
    s = np.arange(128)
    I = np.eye(128, dtype=np.float32)
    U = (s[:, None] <= s[None, :]).astype(np.float32)
    UT = U.T.copy()
    Lm = 1.0 - U
    NEG = (UT - 1.0) * 1.0e30
    HMk = np.zeros((128, 128), np.float32)
    for g in range(4):
        HMk[:, g] = (s // 32 == g)
    HMk[:, 4] = ((s // 32) % 2 == 0); HMk[:, 5] = ((s // 32) % 2 == 1); HMk[:, 6] = (s < 64); HMk[:, 7] = (s >= 64)
    return np.ascontiguousarray(np.stack([I, U, UT, Lm, NEG, HMk], axis=1).astype(np.float32))

def make_rope(pos):
    out = []
    for hd in (32, 64):
        half = hd // 2
        p = np.arange(128) % hd
        inv = (10000.0 ** (-(np.arange(half, dtype=np.float32)) / np.float32(half))).astype(np.float32)
        ang = pos.astype(np.float32)[None, :] * inv[p % half][:, None]
        cos = np.cos(ang).astype(np.float32); sin = np.sin(ang).astype(np.float32)
        sgn = np.where(p < half, -1.0, 1.0).astype(np.float32)[:, None]
        out += [cos, sin * sgn]
    return np.ascontiguousarray(np.stack(out, axis=0).astype(np.float32))

def prep_weights(inp, L):
    w = {}
    w_in = inp["w_in"]
    def cols(name): a, b = IN_OFF[name]; return w_in[:, :, a:b]
    def sw(x, hd): return x[:, :, _swap_cols(x.shape[2], hd)]
    ik = cols("idx_k"); iks = sw(ik, 32)
    ga = np.concatenate([cols("gla_a"), np.zeros((L, D, 112), np.float32)], axis=2)
    w["winfm"] = np.ascontiguousarray(np.concatenate([
        cols("lru_x"), cols("lru_gate"),
        cols("diff_q"), sw(cols("diff_q"), 32), cols("diff_k"), sw(cols("diff_k"), 32),
        cols("dsa_q"), sw(cols("dsa_q"), 64), cols("dsa_k"), sw(cols("dsa_k"), 64),
        cols("idx_q"), sw(cols("idx_q"), 32),
        cols("gla_q"), cols("gla_k"),
        ik, ik, ik, ik, iks, iks, iks, iks, ga], axis=2))
    w["wintm"] = np.ascontiguousarray(np.concatenate([cols("diff_v"), cols("dsa_v"), cols("gla_v"), cols("gla_g"), cols("gla_k"), cols("idx_w")], axis=2))
    w["wada"] = inp["w_ada"]; w["wout"] = inp["w_out"]; w["wg"] = inp["ffn_w_gate"]; w["wu"] = inp["ffn_w_up"]; w["wd"] = inp["ffn_w_down"]
    vecs = np.zeros((L, 128, NV), np.float32)
    def fm(v):
        return v.reshape(L, -1, 128).transpose(0, 2, 1)
    vecs[:, :, V_N1:V_N1 + 8] = fm(inp["norm1_g"]); vecs[:, :, V_N2:V_N2 + 8] = fm(inp["norm2_g"])
    cw = inp["lru_conv_w"]
    for c in range(2):
        for k in range(4):
            vecs[:, :, V_CW + c * 4 + k] = cw[:, k, c * 128:(c + 1) * 128]
    vecs[:, :, V_CB:V_CB + 2] = fm(inp["lru_conv_b"]); vecs[:, :, V_BA:V_BA + 2] = fm(inp["lru_ba"])
    vecs[:, :, V_BX:V_BX + 2] = fm(inp["lru_bx"]); vecs[:, :, V_LAM:V_LAM + 2] = fm(inp["lru_lambda"])
    fw = inp["ffn_conv_w"]
    for k in range(3):
        vecs[:, :, V_FCW + k:V_FCW + 66:3] = fm(fw[:, k, :])
    vecs[:, :, V_FCB:V_FCB + 22] = fm(inp["ffn_conv_b"]); vecs[:, :, V_BADA:V_BADA + 48] = fm(inp["b_ada"])
    w["vecs"] = vecs
    lruw = np.zeros((L, 128, 4, 128), np.float32)
    for c in range(2):
        for nl in range(2):
            n = 2 * c + nl
            lruw[:, nl * 64:(nl + 1) * 64, c, nl * 64:(nl + 1) * 64] = inp["lru_wa"][:, n]
            lruw[:, nl * 64:(nl + 1) * 64, 2 + c, nl * 64:(nl + 1) * 64] = inp["lru_wx"][:, n]
    w["lruw"] = lruw
    w["glaw"] = np.ascontiguousarray(np.concatenate([inp["gla_wa2"], inp["gla_ba"][:, None, :]], axis=1))
    w["bc"] = np.ascontiguousarray(np.concatenate([inp["diff_subln_g"], inp["gla_norm_g"], inp["diff_lq1"], inp["diff_lk1"], inp["diff_lq2"], inp["diff_lk2"]], axis=1)[:, None, :])
    w["fng"] = np.ascontiguousarray(inp["final_norm_g"].reshape(8, 128).T)
    w["consts"] = make_consts()
    return w

def unpack_prompt(res, L, T):
    o = {}
    o["y"] = res["yT"].T
    o["diff_k"] = res["o_kd"].transpose(0, 2, 1).reshape(L, T, 4, 2, 32)
    o["diff_v"] = res["o_vd"].reshape(L, T, 4, 64)
    o["dsa_k"] = res["o_kc"].transpose(0, 2, 1).reshape(L, T, 4, 64)
    o["dsa_v"] = res["o_vc"].reshape(L, T, 4, 64)
    o["idx_k"] = res["o_ik"].transpose(0, 2, 1)
    o["lru_h"] = res["o_lh"].transpose(0, 2, 1).reshape(L, 256)
    o["lru_conv"] = res["o_lc"].transpose(0, 3, 2, 1).reshape(L, 3, 256)
    o["gla"] = res["o_gs"].reshape(L, 4, 32, 64)
    o["ffn_conv"] = res["o_fc"].transpose(0, 3, 2, 1).reshape(L, 2, 2816)
    return o

def make_constS():
    p = np.arange(128)
    cS = np.zeros((128, 144), np.float32)
    cS[:, 0:128] = (p[:, None] % 16 == p[None, :] % 16)
    i = np.arange(4)
    cS[:, 128:132] = np.where((p[:, None] < 16) & (i[None, :] <= (p[:, None] % 4)), 0.0, -1.0e30)
    cS[:, 132] = p; cS[:, 133] = p % 32
    for bb in range(4):
        cS[:, 136 + bb] = ((p % 16) // 4 == bb)
    return cS

def prep_sample(inp, L, bsls, past_len):
    if isinstance(bsls, slice):
        bsls = [bsls]
    m = {}
    pos = np.tile(past_len + np.arange(4), 4)
    m["ropeS"] = make_rope(pos)
    m["constS"] = make_constS()
    acc = {k: [] for k in ("xsT", "pt", "st_lh", "st_lc", "st_gs", "st_fc")}
    for bsl in bsls:
        xs = inp["x_sample"][bsl].reshape(16, D)
        acc["xsT"].append(xs.T.reshape(8, 128, 16).transpose(1, 0, 2))
        acc["pt"].append(inp["page_table"][bsl].reshape(1, -1).astype(np.int32))
        acc["st_lh"].append(inp["state_lru_h"][:, bsl].reshape(L, 4, 2, 128).transpose(0, 3, 2, 1))
        acc["st_lc"].append(inp["state_lru_conv"][:, bsl].reshape(L, 4, 3, 2, 128).transpose(0, 4, 3, 1, 2))
        acc["st_gs"].append(inp["state_gla"][:, bsl].reshape(L, 4, 128, 64).transpose(0, 2, 1, 3))
        acc["st_fc"].append(inp["state_ffn_conv"][:, bsl].reshape(L, 4, 2, 22, 128).transpose(0, 4, 3, 1, 2))
    for k, v in acc.items():
        m[k] = np.ascontiguousarray(np.stack(v, axis=0))
    return m

def prep_pools(inp, L):
    npool = inp["cache_diff_k"].shape[1]
    m = {}
    m["pdk"] = inp["cache_diff_k"].reshape(L * npool * 128, 256)
    m["pdv"] = inp["cache_diff_v"].reshape(L * npool * 128, 256)
    m["pck"] = inp["cache_dsa_k"].reshape(L * npool * 128, 256)
    m["pcv"] = inp["cache_dsa_v"].reshape(L * npool * 128, 256)
    m["pik"] = np.ascontiguousarray(inp["cache_idx_k"].transpose(0, 1, 3, 2)).reshape(L * npool * 32, 128)
    return m

def unpack_sample(res, L, g=0):
    res = {k: v[g] for k, v in res.items() if k in ("ysT", "s_kd", "s_vd", "s_kc", "s_vc", "s_ik", "s_lh", "s_lc", "s_gs", "s_fc")}
    o = {}
    o["y"] = res["ysT"].transpose(2, 1, 0).reshape(4, 4, D)
    o["diff_k"] = res["s_kd"].transpose(0, 2, 1).reshape(L, 4, 4, 4, 2, 32)
    o["diff_v"] = res["s_vd"].reshape(L, 4, 4, 4, 64)
    o["dsa_k"] = res["s_kc"].transpose(0, 2, 1).reshape(L, 4, 4, 4, 64)
    o["dsa_v"] = res["s_vc"].reshape(L, 4, 4, 4, 64)
    o["idx_k"] = res["s_ik"].transpose(0, 2, 1).reshape(L, 4, 4, 32)
    o["lru_h"] = res["s_lh"].transpose(0, 3, 2, 1).reshape(L, 4, 256)
    o["lru_conv"] = res["s_lc"].transpose(0, 3, 4, 2, 1).reshape(L, 4, 3, 256)
    o["gla"] = res["s_gs"].transpose(0, 2, 1, 3).reshape(L, 4, 4, 32, 64)
    o["ffn_conv"] = res["s_fc"].transpose(0, 3, 4, 2, 1).reshape(L, 4, 2, 2816)
    return o


L_ = 2; T_ = 4096; NPG_ = 64; NPOOL_ = 2560; PAST_ = NPG_ * 128

def kernel(**inputs):
    inp = {k: np.asarray(v) for k, v in inputs.items()}
    L = L_; T = T_
    lam_inits = [0.8 - 0.6 * math.exp(-0.3 * l) for l in range(L)]
    topk = min(256, T // 4); topk_s = min(256, (PAST_ + 4) // 4)
    nc = bass.Bass("TRN2", target_bir_lowering=False)
    NCORE = 4; NG = 2
    build_program(nc, T, L, lam_inits, topk, dict(npg=NPG_, npool=NPOOL_, topk=topk_s, ng=NG))
    shared = dict(prep_weights(inp, L))
    shared["rope"] = make_rope(np.arange(T))
    shared.update(prep_pools(inp, L))
    in_maps = []
    for c in range(NCORE):
        m = dict(shared)
        pb = c
        bsls = [slice(4 * (NG * c + g), 4 * (NG * c + g) + 4) for g in range(NG)]
        m["xT"] = np.ascontiguousarray(inp["x_prompt"][pb].T)
        cc = np.concatenate([inp["c_prompt"][pb:pb + 1]] + [inp["c_sample"][b_] for b_ in bsls], axis=0)
        m["cT"] = np.ascontiguousarray(cc.reshape(1 + 4 * NG, 8, 128).transpose(2, 1, 0))
        m.update(prep_sample(inp, L, bsls, PAST_))
        in_maps.append(m)
    res = run_bass_kernel_spmd(nc, in_maps, core_ids=list(range(NCORE)))
    R = res.results
    P = [unpack_prompt(R[c], L, T) for c in range(4)]
    S = [unpack_sample(R[c], L, g) for c in range(NCORE) for g in range(NG)]
    f32 = np.float32
    y_prompt = np.stack([P[c]["y"] for c in range(4)]).astype(f32)
    y_sample = np.concatenate([S[c]["y"] for c in range(8)], axis=0).astype(f32)
    keys = ("diff_k", "diff_v", "dsa_k", "dsa_v", "idx_k", "lru_h", "lru_conv", "gla", "ffn_conv")
    Pout = [np.ascontiguousarray(np.stack([P[c][k] for c in range(4)], axis=1)).astype(f32) for k in keys]
    Sout = [np.ascontiguousarray(np.concatenate([S[c][k] for c in range(8)], axis=1)).astype(f32) for k in keys]
    return (y_prompt, y_sample, *Pout, *Sout)
```

```python
import math
import numpy as np
import concourse.bass as bass
import concourse.mybir as mybir
from contextlib import ExitStack
F32 = mybir.dt.float32; BF16 = mybir.dt.bfloat16; I32 = mybir.dt.int32
AF = mybir.ActivationFunctionType; ALU = mybir.AluOpType
ENGS = ['pe', 'act', 'dve', 'pool', 'sp']
EP = 60000
ND = 20

class Prog:
    def __init__(self, nc, stack):
        self.nc = nc; self.stack = stack
        self.q = {e: [] for e in ENGS}
        self.n = {e: 0 for e in ENGS}
        self.esem = {}
        self.seen = {e: {} for e in ENGS}
        self.res = {}
        self.dsem = [stack.enter_context(nc.semaphore(f"d{i}")) for i in range(2 * ND)]
        self.dcount = [0] * (2 * ND)
        self.dnext = {'sp': 0, 'pool': 0}
        self.nbank = 0

    def _sem(self, k):
        if k[0] == 'd':
            return self.dsem[k[1]]
        if k not in self.esem:
            self.esem[k] = self.stack.enter_context(self.nc.semaphore(f"s_{k[1]}_{k[2]}"))
        return self.esem[k]

    def _deps(self, eng, reads, writes):
        writes = tuple(writes) + tuple(r for r in reads if r.startswith("ps"))
        reads = tuple(r for r in reads if not r.startswith("ps"))
        deps = []
        for r in reads:
            rr = self.res.get(r)
            if rr and rr['w']: deps.append(rr['w'])
        for w in writes:
            rr = self.res.get(w)
            if rr:
                if rr['w']: deps.append(rr['w'])
                deps.extend(rr['r'].values())
        waits = []
        for (k, v) in deps:
            if eng == 'pe' and k[0] == 'e' and k[1] == 'pe': continue
            if self.seen[eng].get(k, 0) >= v: continue
            self.seen[eng][k] = v; waits.append((k, v))
        return waits

    def _commit(self, tok, reads, writes):
        writes = tuple(writes) + tuple(r for r in reads if r.startswith("ps"))
        reads = tuple(r for r in reads if not r.startswith("ps"))
        for r in reads:
            rr = self.res.setdefault(r, {'w': None, 'r': {}})
            rr['r'][tok[0]] = tok
        for w in writes:
            self.res[w] = {'w': tok, 'r': {}}

    def op(self, eng, fn, reads=(), writes=()):
        waits = self._deps(eng, reads, writes)
        self.n[eng] += 1; idx = self.n[eng]; ep = (idx - 1) // EP
        tok = (('e', eng, ep), idx - ep * EP)
        self._sem(tok[0])
        self._commit(tok, reads, writes)
        self.q[eng].append((waits, fn, tok[0], 1))

    def dma(self, eng, fn, reads=(), writes=()):
        waits = self._deps(eng, reads, writes)
        i = (self.dnext[eng] % ND) + (ND if eng == 'pool' else 0); self.dnext[eng] += 1
        k = ('d', i); prev = self.dcount[i]
        if prev > 0 and self.seen[eng].get(k, 0) < prev:
            self.seen[eng][k] = prev; waits.append((k, prev))
        self.dcount[i] = prev + 16
        tok = (k, prev + 16)
        self._commit(tok, reads, writes)
        self.q[eng].append((waits, fn, k, 16))

    def emit(self):
        nc = self.nc
        fin = [(('d', i), c) for i, c in enumerate(self.dcount) if c > 0]
        q = self.q
        esem = self._sem
        def run(e, name):
            for (waits, fn, k, inc) in q[name]:
                for (wk, wv) in waits:
                    e.wait_ge(esem(wk), wv)
                ins = fn(e)
                ins.then_inc(esem(k), inc)
        with nc.Block() as block:
            @block.tensor
            def _(e): run(e, 'pe')
            @block.scalar
            def _(e): run(e, 'act')
            @block.vector
            def _(e): run(e, 'dve')
            @block.gpsimd
            def _(e): run(e, 'pool')
            @block.sync
            def _(e):
                run(e, 'sp')
                for (k, c) in fin:
                    e.wait_ge(esem(k), c)
                for en in ENGS:
                    if self.n[en] > 0:
                        idx = self.n[en]; ep = (idx - 1) // EP
                        e.wait_ge(esem(('e', en, ep)), idx - ep * EP)

from concourse.bass_utils import run_bass_kernel_spmd
import math
D = 1024; KC = 8; DFF = 2816; FC = 22
NFM = 29 * 128; NTM = 1160
V_N1 = 0; V_N2 = 8; V_CW = 16; V_CB = 24; V_BA = 26; V_BX = 28; V_LAM = 30; V_FCW = 32; V_FCB = 32 + 66; V_BADA = 98 + 22
NV = V_BADA + 48
EPS = 1e-6
NBIS = 20
BLK = 256
QPB = BLK // 128
GC = 1.5957691216057308

class _Stop(Exception):
    pass
def ckpt(n):
    return None

def build_program(nc, T, L, lam_inits, topk, scfg=None):
    try:
        return _build_program(nc, T, L, lam_inits, topk, scfg)
    except _Stop:
        pass
    _build_program.P.emit(); _build_program.stack.close()
    return nc

def _build_program(nc, T, L, lam_inits, topk, scfg):
    stack = ExitStack()
    P = Prog(nc, stack)
    _build_program.P = P; _build_program.stack = stack
    NB = T // BLK; NT = T // 128
    dt = nc.dram_tensor
    xT_in = dt("xT", [D, T], F32, kind="ExternalInput").ap()
    cT_in = dt("cT", [128, KC, 1 + 4 * (scfg.get("ng", 1) if scfg else 1)], F32, kind="ExternalInput").ap()
    wada = dt("wada", [L, D, 6 * D], F32, kind="ExternalInput").ap()
    winfm = dt("winfm", [L, D, NFM], F32, kind="ExternalInput").ap()
    wintm = dt("wintm", [L, D, NTM], F32, kind="ExternalInput").ap()
    wout = dt("wout", [L, D, D], F32, kind="ExternalInput").ap()
    wg = dt("wg", [L, D, DFF], F32, kind="ExternalInput").ap()
    wu = dt("wu", [L, D, DFF], F32, kind="ExternalInput").ap()
    wd = dt("wd", [L, DFF, D], F32, kind="ExternalInput").ap()
    vecs_in = dt("vecs", [L, 128, NV], F32, kind="ExternalInput").ap()
    lruw_in = dt("lruw", [L, 128, 4, 128], F32, kind="ExternalInput").ap()
    glaw_in = dt("glaw", [L, 17, 128], F32, kind="ExternalInput").ap()
    bc_in = dt("bc", [L, 1, 256], F32, kind="ExternalInput").ap()
    fng_in = dt("fng", [128, KC], F32, kind="ExternalInput").ap()
    rope_in = dt("rope", [4, 128, T], F32, kind="ExternalInput").ap()
    consts_in = dt("consts", [128, 6, 128], F32, kind="ExternalInput").ap()
    yT = dt("yT", [D, T], F32, kind="ExternalOutput").ap()
    o_kd = dt("o_kd", [L, 256, T], F32, kind="ExternalOutput").ap()
    o_vd = dt("o_vd", [L, T, 256], F32, kind="ExternalOutput").ap()
    o_kc = dt("o_kc", [L, 256, T], F32, kind="ExternalOutput").ap()
    o_vc = dt("o_vc", [L, T, 256], F32, kind="ExternalOutput").ap()
    o_ik = dt("o_ik", [L, 32, T], F32, kind="ExternalOutput").ap()
    o_lh = dt("o_lh", [L, 128, 2], F32, kind="ExternalOutput").ap()
    o_lc = dt("o_lc", [L, 128, 2, 3], F32, kind="ExternalOutput").ap()
    o_gs = dt("o_gs", [L, 128, 64], F32, kind="ExternalOutput").ap()
    o_fc = dt("o_fc", [L, 128, FC, 2], F32, kind="ExternalOutput").ap()
    xs = [dt(f"xs{i}", [D, T], F32, kind="Internal").ap() for i in range(max(1, L - 1))]
    SAMPLE = scfg is not None
    NBS = 4; NS = 16
    NG = scfg.get('ng', 1) if SAMPLE else 1
    NBC = 1 + 4 * NG
    if SAMPLE:
        NPG, NPOOL, topk_s = scfg["npg"], scfg["npool"], scfg["topk"]
        PPR = NPG // 8; W = PPR * 128
        NG_ = scfg.get('ng', 1)
        xsT_in = dt("xsT", [NG_] + [128, KC, NS], F32, kind="ExternalInput").ap()
        ropeS_in = dt("ropeS", [4, 128, NS], F32, kind="ExternalInput").ap()
        pt_in = dt("pt", [NG_, 1, NBS * NPG], I32, kind="ExternalInput").ap()
        pdkv = dt("pdkv", [L * NPOOL * 128, 512], F32, kind="ExternalInput").ap()
        pckv = dt("pckv", [L * NPOOL * 128, 512], F32, kind="ExternalInput").ap()
        pik = dt("pik", [L * NPOOL * 32, 128], F32, kind="ExternalInput").ap()
        st_lh_in = dt("st_lh", [NG_] + [L, 128, 2, NBS], F32, kind="ExternalInput").ap()
        st_lc_in = dt("st_lc", [NG_] + [L, 128, 2, NBS, 3], F32, kind="ExternalInput").ap()
        st_gs_in = dt("st_gs", [NG_] + [L, 128, NBS, 64], F32, kind="ExternalInput").ap()
        st_fc_in = dt("st_fc", [NG_] + [L, 128, FC, NBS, 2], F32, kind="ExternalInput").ap()
        cS_in = dt("constS", [128, 144], F32, kind="ExternalInput").ap()
        ysT = dt("ysT", [NG_] + [128, KC, NS], F32, kind="ExternalOutput").ap()
        s_kd = dt("s_kd", [NG_] + [L, 256, NS], F32, kind="ExternalOutput").ap()
        s_vd = dt("s_vd", [NG_] + [L, NBS, 4, 256], F32, kind="ExternalOutput").ap()
        s_kc = dt("s_kc", [NG_] + [L, 256, NS], F32, kind="ExternalOutput").ap()
        s_vc = dt("s_vc", [NG_] + [L, NBS, 4, 256], F32, kind="ExternalOutput").ap()
        s_ik = dt("s_ik", [NG_] + [L, 32, NS], F32, kind="ExternalOutput").ap()
        s_lh = dt("s_lh", [NG_] + [L, 128, 2, NBS], F32, kind="ExternalOutput").ap()
        s_lc = dt("s_lc", [NG_] + [L, 128, 2, NBS, 3], F32, kind="ExternalOutput").ap()
        s_gs = dt("s_gs", [NG_] + [L, 128, NBS, 64], F32, kind="ExternalOutput").ap()
        s_fc = dt("s_fc", [NG_] + [L, 128, FC, NBS, 2], F32, kind="ExternalOutput").ap()

    def sb(name, shape, dtype=F32):
        n = 1
        for d_ in shape[1:]: n *= d_
        _build_program.sbtot = getattr(_build_program, "sbtot", 0) + n * (2 if dtype == BF16 else 4)
        return stack.enter_context(nc.sbuf_tensor("s_" + name, shape, dtype))
    def pst(name):
        return stack.enter_context(nc.psum_tensor(name, [128, 512], F32))
    banks = [pst(f"bank{i}") for i in range(8)]
    rot = [0]
    def bank(lo=0, hi=6):
        i = lo + rot[0] % (hi - lo); rot[0] += 1
        return i

    consts = sb("consts", [128, 6, 128])
    identb = sb("identb", [128, 128], BF16)
    onesb = sb("onesb", [128, 128], BF16)
    ones1 = sb("ones1", [1, 128], BF16)
    Ubf = sb("Ubf", [128, 128], BF16)
    vecs = sb("vecs", [128, NV])
    lruw = sb("lruw", [128, 4, 128], BF16)
    glaw = sb("glaw", [17, 128], BF16)
    glab = sb("glab", [1, 128], BF16)
    bcv = sb("bcv", [128, 256])
    fng = sb("fng", [128, KC])
    csil = sb("csil", [128, KC, NBC], BF16)
    cTt = sb("cTt", [128, KC, NBC])
    mod = sb("mod", [128, 48, NBC])
    A1 = sb("A1", [128, KC, NBC]); A2 = sb("A2", [128, KC, NBC])
    lamt = sb("lamt", [128, 4]); cl = sb("cl", [128, 2]); cl2 = sb("cl2", [128, 2])
    KdT = sb("KdT", [128, 2, T], BF16); KcT = sb("KcT", [128, 2, T], BF16); ikT = sb("ikT", [128, T], BF16)
    Vd = sb("Vd", [128, NT, 4, 65], BF16); Vc = sb("Vc", [128, NT, 4, 65], BF16)
    Sg = sb("Sg", [128, 64]); Sgb = sb("Sgb", [128, 64], BF16)
    hlast = sb("hlast", [128, 2])
    fcs = sb("fcs", [128, FC, 2])
    xt = sb("xt", [128, KC, BLK]); hT = sb("hT", [128, KC, BLK], BF16); catT = sb("catT", [128, KC, BLK], BF16)
    sqb = sb("sqb", [128, BLK], BF16)
    rstd = sb("rstd", [128, BLK])
    tA = sb("tA", [128, 512]); tB = sb("tB", [128, BLK]); tC = sb("tC", [128, BLK]); tD = sb("tD", [128, BLK]); tE = sb("tE", [128, BLK])
    ropet = sb("ropet", [128, 4, BLK])
    xpad = sb("xpad", [128, 2, BLK + 3])
    xab = sb("xab", [128, BLK], BF16)
    QdT = sb("QdT", [128, 4, 2, BLK], BF16); QcT = sb("QcT", [128, 2, 2, BLK], BF16); iqm = sb("iqm", [128, 8, BLK], BF16)
    gqT = sb("gqT", [128, BLK]); gkT = sb("gkT", [128, BLK]); gaT = sb("gaT", [16, BLK], BF16)
    kdf = sb("kdf", [128, BLK]);
    gv = sb("gv", [128, QPB, 256], BF16); gg = sb("gg", [128, QPB, 256]); gk = sb("gk", [128, QPB, 128]); iw = sb("iw", [128, QPB, 8])
    vout = sb("vout", [128, 1, 256])
    NWB = 2
    wbuf = [sb(f"wb{i}", [128, 4096], BF16) for i in range(NWB)]
    wrot = [0]
    big = sb("big", [128, max(22 * BLK * 2, 6 * T, 5600 * 4) // 4])
    act = big[:].bitcast(BF16).rearrange("p (c n) -> p c n", n=512) if False else None
    PT = sb("PT", [128, 1024], BF16)
    tmf = sb("tmf", [128, 8, 65])
    tms = sb("tms", [128, 32])
    otm = sb("otm", [128, 256]); otb = sb("otb", [128, 256], BF16)
    g_la = sb("g_la", [128, 128]); g_e1 = sb("g_e1", [128, 128]); g_e2 = sb("g_e2", [128, 128]); g_kh = sb("g_kh", [128, 128], BF16)
    g_qt = sb("g_qt", [128, 128]); g_qm = sb("g_qm", [128, 4, 128], BF16); g_kt = sb("g_kt", [128, 128], BF16); g_at = sb("g_at", [128, 4, 128], BF16)
    bis = sb("bis", [128, 8])
    scr = big[:, 0:T]
    mbt = big[:, T:T + T // 2].bitcast(BF16)
    actt = big[:, 0:22 * BLK // 2].bitcast(BF16).rearrange("p (c n) -> p c n", n=BLK)

    if SAMPLE:
        cS = sb("cS", [128, 144])
        Gm = cS[:, 0:128]; negS = cS[:, 128:132]; iotaP = cS[:, 132:133]; iotaP32 = cS[:, 133:134]
        xsSg = [sb(f"xsS{g_}", [128, KC, NS]) for g_ in range(NG)]; hS = sb("hS", [128, KC, NS], BF16); catS = sb("catS", [128, KC, NS], BF16)
        ropeS = sb("ropeS", [128, 4, NS])
        sqS = sb("sqS", [128, NS], BF16); rstdS = sb("rstdS", [128, NS])
        uA = sb("uA", [128, NS]); uB = sb("uB", [128, NS]); uC = sb("uC", [128, NS]); uD = sb("uD", [128, NS]); uE = sb("uE", [128, NS]); uF = sb("uF", [128, NS])
        xabS = sb("xabS", [128, NS], BF16)
        xpS = sb("xpS", [128, 2, NBS, 7]); h0S = sb("h0S", [128, 2, NBS]); hnS = sb("hnS", [128, 2, NBS])
        QdS = sb("QdS", [128, 4, 2, NS], BF16); QcS = sb("QcS", [128, 2, 2, NS], BF16); iqmS = sb("iqmS", [128, 8, NS], BF16)
        KdN = sb("KdN", [128, 2, NS], BF16); KcN = sb("KcN", [128, 2, NS], BF16); ikN = sb("ikN", [128, NS], BF16)
        gqS = sb("gqS", [128, NS]); gkS_f = sb("gkS_f", [128, NS]); gaS = sb("gaS", [16, NS], BF16)
        voS = sb("voS", [4, 2, 256]); vdN = sb("vdN", [4, NBS, 4, 65], BF16); vcN = sb("vcN", [4, NBS, 4, 65], BF16)
        gvS = sb("gvS", [4, NBS, 256], BF16); ggS = sb("ggS", [4, NBS, 256], BF16); gkS = sb("gkS", [4, NBS, 128])
        iw16 = sb("iw16", [16, 8]); iwR = sb("iwR", [128, 8]); iwB = sb("iwB", [128, NBS, 8])
        SgS = sb("SgS", [128, NBS, 64]); SgSb = sb("SgSb", [128, NBS, 64], BF16)
        fcS = sb("fcS", [128, FC, NBS, 2]); fcSn = sb("fcSn", [128, FC, NBS, 2]); fpad = sb("fpad", [128, NBS, 6])
        actS = sb("actS", [128, FC, NS], BF16)
        ptb = sb("ptb", [128, NBS * NPG], I32); ptf1 = sb("ptf1", [128, NBS * NPG])
        ixK = sb("ixK", [128, NBS * NPG], I32); ixI = sb("ixI", [128, NBS * NPG], I32)
        vpb = [sb(f"vpb{i}", [128, 4, 65], BF16) for i in range(2)]
        PTs = sb("PTs", [128, 32], BF16); PTn = sb("PTn", [4, 32], BF16)
        bisS = sb("bisS", [128, 8])
    tAf = tA
    if SAMPLE:
        assert W + 4 <= 1040 and (W + 5) // 2 <= 560
        scrS = big[:, 0:W + 4]
        mbS = big[:, 1040:1040 + (W + 5) // 2].bitcast(BF16)[:, 0:W + 4]
        ZP = big[:, 1600:1600 + 960].bitcast(BF16).rearrange("p (h n) -> p h n", n=240)
        GD = 4
        kvf = [big[:, 2600 + 512 * i_:3112 + 512 * i_] for i_ in range(GD)]
        ipf = [big[:, 4648 + 128 * i_:4776 + 128 * i_] for i_ in range(GD)]
        kpb = [big[:, 5160 + 128 * i_:5288 + 128 * i_].bitcast(BF16).rearrange("p (c k) -> p c k", k=128) for i_ in range(2)]
        ipb = [big[:, 5416 + 64 * i_:5480 + 64 * i_].bitcast(BF16) for i_ in range(2)]
        FENCE = ("scr", "actt", "mbt", "scrS", "mbS", "ZP", "kpb0", "kpb1", "ipb0", "ipb1") + tuple(f"kvf{i_}" for i_ in range(GD)) + tuple(f"ipf{i_}" for i_ in range(GD))
    I_ = consts[:, 0, :]; U_ = consts[:, 1, :]; UT_ = consts[:, 2, :]; L_ = consts[:, 3, :]; NEG_ = consts[:, 4, :]
    HM = lambda g: consts[:, 5, g:g + 1]

    def dma_sp(out, in_, reads, writes):
        P.dma('sp', lambda e: e.dma_start(out=out, in_=in_), reads, writes)
    def dma_pool(out, in_, reads, writes):
        P.dma('pool', lambda e: e.dma_start(out=out, in_=in_), reads, writes)
    def mm(out, lhsT, rhs, start, stop, reads, writes):
        P.op('pe', lambda e: e.matmul(out, lhsT=lhsT, rhs=rhs, start=start, stop=stop, skip_group_check=True), reads, writes)
    def tr(out, in_, ident, reads, writes):
        P.op('pe', lambda e: e.transpose(out, in_, ident), reads, writes)
    def actf(out, in_, func, reads, writes, bias=None, scale=None, accum=None):
        kw = {}
        if bias is not None: kw['bias'] = bias
        if scale is not None: kw['scale'] = scale
        if accum is not None: kw['accum_out'] = accum
        P.op('act', lambda e: e.activation(out=out, in_=in_, func=func, **kw), reads, writes)
    def ts(eng, out, in0, s1, s2, op0, op1, reads, writes, accum=None):
        kw = {}
        if accum is not None: kw['accum_out'] = accum
        if op1 is None:
            P.op(eng, lambda e: e.tensor_scalar(out=out, in0=in0, scalar1=s1, scalar2=None, op0=op0, **kw), reads, writes)
        else:
            P.op(eng, lambda e: e.tensor_scalar(out=out, in0=in0, scalar1=s1, scalar2=s2, op0=op0, op1=op1, **kw), reads, writes)
    def tt(eng, out, in0, in1, op, reads, writes):
        P.op(eng, lambda e: e.tensor_tensor(out=out, in0=in0, in1=in1, op=op), reads, writes)
    def stt(out, in0, s, in1, op0, op1, reads, writes):
        P.op('dve', lambda e: e.scalar_tensor_tensor(out=out, in0=in0, scalar=s, in1=in1, op0=op0, op1=op1), reads, writes)
    def cp(eng, out, in_, reads, writes):
        if eng == 'act':
            P.op('act', lambda e: e.copy(out=out, in_=in_), reads, writes)
        else:
            P.op(eng, lambda e: e.tensor_copy(out=out, in_=in_), reads, writes)
    def memset(eng, ap, val, writes):
        P.op(eng, lambda e: e.memset(ap, val), (), writes)
    def recip(out, in_, reads, writes):
        P.op('dve', lambda e: e.reciprocal(out=out, in_=in_), reads, writes)

    wscr = {}
    def load_w(key, src2d, krows, ncols):
        i = wrot[0] % NWB; wrot[0] += 1
        kc = krows // 128
        assert kc * ncols <= 4096
        flat = wbuf[i][:, 0:kc * ncols]
        dst = flat.rearrange("p (k n) -> p k n", n=ncols)
        if key not in wscr:
            nm = "ws_" + "_".join(str(k) for k in key)
            wscr[key] = (nc.dram_tensor(nm, [128, kc * ncols], BF16, kind="Internal").ap(), nm)
            dma_pool(dst, src2d.rearrange("(k p) n -> p k n", p=128), (), (f"wb{i}",))
            dma_sp(wscr[key][0], flat, (f"wb{i}",), (wscr[key][1],))
        else:
            dma_sp(flat, wscr[key][0], (wscr[key][1],), (f"wb{i}",))
        return dst, f"wb{i}"

    def gather(dst, pool_ap, idx_col, reads, writes):
        P.dma('pool', lambda e: e.indirect_dma_start(out=dst, out_offset=None, in_=pool_ap,
                                                     in_offset=bass.IndirectOffsetOnAxis(ap=idx_col, axis=0)), reads, writes)

    def finalize_tm(PP, kind, lam_init, cat_dst_fn):
        otm3 = otm[:PP].rearrange("p (h d) -> p h d", d=64)
        otb3 = otb[:PP].rearrange("p (h d) -> p h d", d=64)
        if kind == 'diff':
            for m in range(2):
                cp('act', tmf[:PP, m * 4:(m + 1) * 4, :], banks[6 + m][:PP, 0:260].rearrange("p (h d) -> p h d", d=65), (f"ps{6 + m}",), ("tmf",))
            recip(tms[:PP, 0:8], tmf[:PP, :, 64], ("tmf",), ("tms",))
            tt('dve', tms[:PP, 4:8], tms[:PP, 4:8], lamt[:PP, 3:4].to_broadcast([PP, 4]), ALU.mult, ("tms", "lamt"), ("tms",))
            tt('dve', tmf[:PP, 0:4, 0:64], tmf[:PP, 0:4, 0:64], tms[:PP, 0:4].unsqueeze(2).to_broadcast([PP, 4, 64]), ALU.mult, ("tmf", "tms"), ("tmf",))
            tt('dve', tmf[:PP, 4:8, 0:64], tmf[:PP, 4:8, 0:64], tms[:PP, 4:8].unsqueeze(2).to_broadcast([PP, 4, 64]), ALU.mult, ("tmf", "tms"), ("tmf",))
            tt('dve', otm3, tmf[:PP, 0:4, 0:64], tmf[:PP, 4:8, 0:64], ALU.add, ("tmf",), ("otm",))
            tt('dve', tmf[:PP, 0:4, 0:64], otm3, otm3, ALU.mult, ("otm",), ("tmf",))
            P.op('dve', lambda e: e.tensor_reduce(out=tms[:PP, 8:12], in_=tmf[:PP, 0:4, 0:64], axis=mybir.AxisListType.X, op=ALU.add), ("tmf",), ("tms",))
            actf(tms[:PP, 8:12], tms[:PP, 8:12], AF.Sqrt, ("tms",), ("tms",), scale=1.0 / 64, bias=EPS)
            recip(tms[:PP, 8:12], tms[:PP, 8:12], ("tms",), ("tms",))
            tt('dve', otm3, otm3, tms[:PP, 8:12].unsqueeze(2).to_broadcast([PP, 4, 64]), ALU.mult, ("otm", "tms"), ("otm",))
            stt(otb3, otm3, 1.0 - lam_init, bcv[:PP, 0:64].unsqueeze(1).to_broadcast([PP, 4, 64]), ALU.mult, ALU.mult, ("otm", "bcv"), ("otb",))
        else:
            cp('act', tmf[:PP, 0:4, :], banks[6][:PP, 0:260].rearrange("p (h d) -> p h d", d=65), ("ps6",), ("tmf",))
            recip(tms[:PP, 0:4], tmf[:PP, 0:4, 64], ("tmf",), ("tms",))
            tt('dve', otb3, tmf[:PP, 0:4, 0:64], tms[:PP, 0:4].unsqueeze(2).to_broadcast([PP, 4, 64]), ALU.mult, ("tmf", "tms"), ("otb",))
        bt = bank()
        for c in range(2):
            mm(banks[bt][:, c * 128:c * 128 + PP], otb[:PP, c * 128:(c + 1) * 128], identb[:PP, :PP], c == 0, True, ("otb", "identb"), (f"ps{bt}",))
        for c in range(2):
            cat_dst_fn(c, bt)

    def sample_setup():
        dma_sp(cS[:], cS_in, (), ("cS",))
        for g_ in range(NG):
            dma_sp(xsSg[g_][:], xsT_in[g_], (), (f"xsS{g_}",))
        dma_sp(ropeS[:], ropeS_in.rearrange("r p t -> p r t"), (), ("ropeS",))
        memset('pool', vdN[:, :, :, 64:65], 1.0, ("vdN",)); memset('pool', vcN[:, :, :, 64:65], 1.0, ("vcN",))
        for i in range(2):
            memset('pool', vpb[i][:, :, 64:65], 1.0, (f"vpb{i}",))

    def sample_layer(l, lam_init, grp):
        xsS = xsSg[grp]; ptf = ptf1
        XS = f"xsS{grp}"; PTF = "ptf1"
        st_lh_g, st_lc_g, st_gs_g, st_fc_g = st_lh_in[grp], st_lc_in[grp], st_gs_in[grp], st_fc_in[grp]
        ysT_g = ysT[grp]; s_kd_g = s_kd[grp]; s_vd_g = s_vd[grp]; s_kc_g = s_kc[grp]; s_vc_g = s_vc[grp]; s_ik_g = s_ik[grp]
        s_lh_g = s_lh[grp]; s_lc_g = s_lc[grp]; s_gs_g = s_gs[grp]; s_fc_g = s_fc[grp]
        P.op('dve', lambda e: e.memset(bisS[:, 7:8], 0.0), (), FENCE + ("bisS",))
        memset('pool', ZP, 0.0, ("ZP",))
        dma_sp(h0S[:], st_lh_g[l], (), ("h0S",))
        dma_sp(xpS[:, :, :, 0:3], st_lc_g[l], (), ("xpS",))
        dma_sp(SgS[:], st_gs_g[l], (), ("SgS",))
        cp('act', SgSb[:], SgS[:], ("SgS",), ("SgSb",))
        dma_sp(fcS[:], st_fc_g[l], (), ("fcS",))
        dma_sp(ptb[:], pt_in[grp].partition_broadcast(128), (), ("ptb",))
        cp('dve', ptf[:], ptb[:], ("ptb",), (PTF,))
        ts('dve', uA[:, 0:1], iotaP, float(l * NPOOL * 128), None, ALU.add, None, ("cS",), ("uA",))
        ts('dve', ixK[:], ptf[:], 128.0, uA[:, 0:1], ALU.mult, ALU.add, (PTF, "uA"), ("ixK",))
        ts('dve', uA[:, 1:2], iotaP32, float(l * NPOOL * 32), None, ALU.add, None, ("cS", "ixK"), ("uA",))
        ts('dve', ixI[:], ptf[:], 32.0, uA[:, 1:2], ALU.mult, ALU.add, (PTF, "uA"), ("ixI",))

        def norm_mod_s(Acoef, shoff):
            b = bank()
            for kc in range(KC):
                actf(sqS[:], xsS[:, kc, :], AF.Square, (XS,), ("sqS",))
                mm(banks[b][:, :NS], onesb[:], sqS[:], kc == 0, kc == KC - 1, ("sqS", "onesb"), (f"ps{b}",))
            actf(rstdS[:], banks[b][:, :NS], AF.Sqrt, (f"ps{b}",), ("rstdS",), scale=1.0 / D, bias=EPS)
            recip(rstdS[:], rstdS[:], ("rstdS",), ("rstdS",))
            for kc in range(KC):
                tt('dve', uA[:], xsS[:, kc, :], rstdS[:], ALU.mult, (XS, "rstdS"), ("uA",))
                for bb in range(NBS):
                    actf(hS[:, kc, 4 * bb:4 * bb + 4], uA[:, 4 * bb:4 * bb + 4], AF.Identity, ("uA", "mod", Acoef[1]), ("hS",),
                         scale=Acoef[0][:, kc, 1 + 4 * grp + bb:2 + 4 * grp + bb], bias=mod[:, shoff + kc, 1 + 4 * grp + bb:2 + 4 * grp + bb])
        norm_mod_s((A1, "A1"), 0)

        def proj_s(key, col0, ncols, chunk_ms):
            wt, wr = load_w(key, winfm[l][:, col0:col0 + ncols], D, ncols)
            out = []
            for ii, M_ in enumerate(chunk_ms):
                b = bank(); out.append(b)
                for kc in range(KC):
                    mm(banks[b][:M_, :NS], wt[:, kc, ii * 128:ii * 128 + M_], hS[:, kc, :], kc == 0, kc == KC - 1, (wr, "hS"), (f"ps{b}",))
            return out
        def rope_s(out, outres, bp, bs, tab):
            tt('dve', uB[:], banks[bp][:, :NS], ropeS[:, tab, :], ALU.mult, (f"ps{bp}", "ropeS"), ("uB",))
            tt('dve', uC[:], banks[bs][:, :NS], ropeS[:, tab + 1, :], ALU.mult, (f"ps{bs}", "ropeS"), ("uC",))
            tt('pool', out, uB[:], uC[:], ALU.add, ("uB", "uC"), (outres,))

        bl = proj_s(("fm", l, 0), 0, 512, [128] * 4)
        for c in range(2):
            cp('act', xpS[:, c, :, 3:7], banks[bl[c]][:, :NS].rearrange("p (b i) -> p b i", i=4), (f"ps{bl[c]}",), ("xpS",))
        for c in range(2):
            b = bl[2 + c]
            cp('act', uD[:], banks[b][:, :NS], (f"ps{b}",), ("uD",))
            tt('dve', uB[:], uD[:], uD[:], ALU.mult, ("uD",), ("uB",))
            ts('dve', uB[:], uB[:], 0.044715, 1.0, ALU.mult, ALU.add, ("uB",), ("uB",))
            tt('dve', uB[:], uB[:], uD[:], ALU.mult, ("uB", "uD"), ("uB",))
            actf(uB[:], uB[:], AF.Sigmoid, ("uB",), ("uB",), scale=GC)
            tt('dve', uD[:], uD[:], uB[:], ALU.mult, ("uD", "uB"), ("uD",))
            cw = lambda k: vecs[:, V_CW + c * 4 + k:V_CW + c * 4 + k + 1]
            uA3 = uA[:].rearrange("p (b i) -> p b i", i=4)
            ts('dve', uA3, xpS[:, c, :, 3:7], cw(3), vecs[:, V_CB + c:V_CB + c + 1], ALU.mult, ALU.add, ("xpS", "vecs"), ("uA",))
            for k in range(3):
                stt(uA3, xpS[:, c, :, k:k + 4], cw(k), uA3, ALU.mult, ALU.add, ("xpS", "vecs", "uA"), ("uA",))
            dma_sp(s_lc_g[l][:, c, :, :], xpS[:, c, :, 4:7], ("xpS",), ("s_lc",))
            cp('act', xabS[:], uA[:], ("uA",), ("xabS",))
            ba_ = bank(); bx_ = bank()
            mm(banks[ba_][:, :NS], lruw[:, c, :], xabS[:], True, True, ("lruw", "xabS"), (f"ps{ba_}",))
            mm(banks[bx_][:, :NS], lruw[:, 2 + c, :], xabS[:], True, True, ("lruw", "xabS"), (f"ps{bx_}",))
            actf(uB[:], banks[ba_][:, :NS], AF.Sigmoid, (f"ps{ba_}", "vecs"), ("uB",), bias=vecs[:, V_BA + c:V_BA + c + 1])
            actf(uC[:], banks[bx_][:, :NS], AF.Sigmoid, (f"ps{bx_}", "vecs"), ("uC",), bias=vecs[:, V_BX + c:V_BX + c + 1])
            actf(uE[:], uB[:], AF.Exp, ("uB", "cl2"), ("uE",), scale=cl2[:, c:c + 1])
            actf(uB[:], uB[:], AF.Exp, ("uB", "cl"), ("uB",), scale=cl[:, c:c + 1])
            actf(uE[:], uE[:], AF.Sqrt, ("uE",), ("uE",), scale=-1.0, bias=1.0)
            tt('dve', uC[:], uC[:], uA[:], ALU.mult, ("uC", "uA"), ("uC",))
            tt('dve', uC[:], uC[:], uE[:], ALU.mult, ("uC", "uE"), ("uC",))
            for bb in range(NBS):
                sl = slice(4 * bb, 4 * bb + 4)
                P.op('dve', lambda e, sl=sl, bb=bb, c=c: e.tensor_tensor_scan(out=uF[:, sl], data0=uB[:, sl], data1=uC[:, sl], initial=h0S[:, c, bb:bb + 1],
                                                                       op0=ALU.mult, op1=ALU.add), ("uB", "uC", "h0S"), ("uF",))
            cp('dve', hnS[:, c, :], uF[:].rearrange("p (b i) -> p b i", i=4)[:, :, 3], ("uF",), ("hnS",))
            tt('dve', catS[:, c, :], uF[:], uD[:], ALU.mult, ("uF", "uD"), ("catS",))
        dma_sp(s_lh_g[l], hnS[:], ("hnS",), ("s_lh",))

        def rope_group(cbase, tab, fn):
            bs = proj_s(("fm", l, cbase), cbase * 128, 512, [128] * 4)
            for c in range(2):
                fn(c, bs[c], bs[2 + c], tab)
        def s_dq(c, bp, bsw, tab):
            rope_s(uE[:], "uE", bp, bsw, tab)
            for g in range(4):
                ts('dve', QdS[:, g, c, :], uE[:], HM(g), None, ALU.mult, None, ("uE", "consts"), ("QdS",))
        def s_dk(c, bp, bsw, tab):
            rope_s(uE[:], "uE", bp, bsw, tab)
            cp('act', KdN[:, c, :], uE[:], ("uE",), ("KdN",))
            dma_sp(s_kd_g[l][c * 128:(c + 1) * 128, :], uE[:], ("uE",), ("s_kd",))
        def s_cq(c, bp, bsw, tab):
            rope_s(uE[:], "uE", bp, bsw, tab)
            ts('dve', QcS[:, 0, c, :], uE[:], HM(6), None, ALU.mult, None, ("uE", "consts"), ("QcS",))
            ts('dve', QcS[:, 1, c, :], uE[:], HM(7), None, ALU.mult, None, ("uE", "consts"), ("QcS",))
        def s_ck(c, bp, bsw, tab):
            rope_s(uE[:], "uE", bp, bsw, tab)
            cp('act', KcN[:, c, :], uE[:], ("uE",), ("KcN",))
            dma_sp(s_kc_g[l][c * 128:(c + 1) * 128, :], uE[:], ("uE",), ("s_kc",))
        def s_iq(c, bp, bsw, tab):
            rope_s(uE[:], "uE", bp, bsw, tab)
            for g in range(4):
                ts('dve', iqmS[:, c * 4 + g, :], uE[:], HM(g), None, ALU.mult, None, ("uE", "consts"), ("iqmS",))
        rope_group(4, 0, s_dq); rope_group(8, 0, s_dk); rope_group(12, 2, s_cq); rope_group(16, 2, s_ck); rope_group(20, 0, s_iq)
        bs = proj_s(("fm", l, 24), 24 * 128, 512, [128] * 4)
        cp('act', gqS[:], banks[bs[0]][:, :NS], (f"ps{bs[0]}",), ("gqS",))
        cp('act', gkS_f[:], banks[bs[1]][:, :NS], (f"ps{bs[1]}",), ("gkS_f",))
        rope_s(uE[:], "uE", bs[2], bs[3], 0)
        cp('act', ikN[:], uE[:], ("uE",), ("ikN",))
        dma_sp(s_ik_g[l], uE[0:32, :], ("uE",), ("s_ik",))
        bs = proj_s(("fm", l, 28), 28 * 128, 128, [16])
        cp('act', gaS[:], banks[bs[0]][:16, :NS], (f"ps{bs[0]}",), ("gaS",))

        for pi, (c0, c1) in enumerate(((0, 512), (512, 1024), (1024, NTM))):
            wt, wr = load_w(("tm", l, c0), wintm[l][:, c0:c1], D, c1 - c0)
            for bb in range(NBS):
                b = bank()
                for kc in range(KC):
                    mm(banks[b][:4, :c1 - c0], hS[:, kc, 4 * bb:4 * bb + 4], wt[:, kc, :], kc == 0, kc == KC - 1, (wr, "hS"), (f"ps{b}",))
                if pi == 0:
                    cp('act', voS[:, :, :], banks[b][:4, 0:512].rearrange("p (v d) -> p v d", d=256), (f"ps{b}",), ("voS",))
                    cp('dve', vdN[:, bb, :, 0:64], banks[b][:4, 0:256].rearrange("p (h d) -> p h d", d=64), (f"ps{b}",), ("vdN",))
                    cp('dve', vcN[:, bb, :, 0:64], banks[b][:4, 256:512].rearrange("p (h d) -> p h d", d=64), (f"ps{b}",), ("vcN",))
                    dma_sp(s_vd_g[l][bb], voS[:, 0, :], ("voS",), ("s_vd",))
                    dma_sp(s_vc_g[l][bb], voS[:, 1, :], ("voS",), ("s_vc",))
                elif pi == 1:
                    cp('act', gvS[:, bb, :], banks[b][:4, 0:256], (f"ps{b}",), ("gvS",))
                    actf(ggS[:, bb, :], banks[b][:4, 256:512], AF.Silu, (f"ps{b}",), ("ggS",))
                else:
                    cp('act', gkS[:, bb, :], banks[b][:4, 0:128], (f"ps{b}",), ("gkS",))
            if pi == 2:
                b = bank()
                for kc in range(KC):
                    mm(banks[b][:16, 0:8], hS[:, kc, :], wt[:, kc, 128:136], kc == 0, kc == KC - 1, (wr, "hS"), (f"ps{b}",))
                cp('act', iw16[:], banks[b][:16, 0:8], (f"ps{b}",), ("iw16",))
        for r in range(8):
            dma_sp(iwR[16 * r:16 * r + 16, :], iw16[:], ("iw16",), ("iwR",))
        for bb in range(NBS):
            ts('dve', iwB[:, bb, :], iwR[:], cS[:, 136 + bb:137 + bb], None, ALU.mult, None, ("iwR", "cS"), ("iwB",))

        for bb in range(NBS):
            sl = slice(4 * bb, 4 * bb + 4)
            b = bank()
            mm(banks[b][:4, 0:128], gaS[:, sl], glaw[0:16, :], True, False, ("gaS", "glaw"), (f"ps{b}",))
            mm(banks[b][:4, 0:128], ones1[:, 0:4], glab[:, :], False, True, ("ones1", "glab"), (f"ps{b}",))
            actf(g_la[:4, :], banks[b][:4, 0:128], AF.Exp, (f"ps{b}",), ("g_la",), scale=-1.0)
            actf(g_la[:4, :], g_la[:4, :], AF.Ln, ("g_la",), ("g_la",), bias=1.0)
            ts('dve', g_la[:4, :], g_la[:4, :], -1.0 / 16.0, None, ALU.mult, None, ("g_la",), ("g_la",))
            bb_ = bank()
            mm(banks[bb_][:, 0:4], g_la[:4, :], consts[0:4, 1, 0:4], True, True, ("g_la", "consts"), (f"ps{bb_}",))
            mm(banks[bb_][:4, 128:256], consts[0:4, 3, 0:4], g_la[:4, :], False, True, ("g_la", "consts"), (f"ps{bb_}",))
            actf(g_e1[:, 0:4], banks[bb_][:, 0:4], AF.Exp, (f"ps{bb_}",), ("g_e1",))
            actf(g_e2[:, 0:4], banks[bb_][:, 0:4], AF.Exp, (f"ps{bb_}",), ("g_e2",), scale=-1.0)
            stt(g_qt[:, 0:4], gqS[:, sl], 32.0 ** -0.5, g_e1[:, 0:4], ALU.mult, ALU.mult, ("gqS", "g_e1"), ("g_qt",))
            for h in range(4):
                ts('dve', g_qm[:, h, 0:4], g_qt[:, 0:4], HM(h), None, ALU.mult, None, ("g_qt", "consts"), ("g_qm",))
            tt('dve', g_kt[:, 0:4], gkS_f[:, sl], g_e2[:, 0:4], ALU.mult, ("gkS_f", "g_e2"), ("g_kt",))
            actf(g_e2[:4, :], banks[bb_][:4, 128:256], AF.Exp, (f"ps{bb_}", "g_kt"), ("g_e2",))
            tt('dve', g_kh[:4, :], gkS[:, bb, :], g_e2[:4, :], ALU.mult, ("gkS", "g_e2"), ("g_kh",))
            ba_ = bank()
            for h in range(4):
                mm(banks[ba_][:4, h * 4:h * 4 + 4], g_kt[:, 0:4], g_qm[:, h, 0:4], h == 0, True, ("g_kt", "g_qm"), (f"ps{ba_}",))
            tt('dve', g_at[:4, :, 0:4], banks[ba_][:4, 0:16].rearrange("p (h t) -> p h t", t=4), Ubf[0:4, 0:4].unsqueeze(1).to_broadcast([4, 4, 4]), ALU.mult,
               (f"ps{ba_}", "Ubf"), ("g_at",))
            bo = bank()
            for h in range(4):
                mm(banks[bo][:4, h * 64:(h + 1) * 64], g_at[:4, h, 0:4], gvS[:, bb, h * 64:(h + 1) * 64], h == 0, False, ("g_at", "gvS"), (f"ps{bo}",))
                mm(banks[bo][:4, h * 64:(h + 1) * 64], g_qm[:, h, 0:4], SgSb[:, bb, :], False, True, ("g_qm", "SgSb"), (f"ps{bo}",))
            bs_ = bank()
            mm(banks[bs_][:, 0:256], g_kh[:4, :], gvS[:, bb, :], True, True, ("g_kh", "gvS"), (f"ps{bs_}",))
            ts('dve', SgS[:, bb, :], SgS[:, bb, :], g_e1[:, 3:4], None, ALU.mult, None, ("SgS", "g_e1"), ("SgS",))
            for h in range(4):
                stt(SgS[:, bb, :], banks[bs_][:, h * 64:(h + 1) * 64], HM(h), SgS[:, bb, :], ALU.mult, ALU.add, ("SgS", "consts", f"ps{bs_}"), ("SgS",))
            cp('act', otm[:4, :], banks[bo][:4, 0:256], (f"ps{bo}",), ("otm",))
            otm3 = otm[:4].rearrange("p (h d) -> p h d", d=64)
            tt('dve', tmf[:4, 0:4, 0:64], otm3, otm3, ALU.mult, ("otm",), ("tmf",))
            P.op('dve', lambda e: e.tensor_reduce(out=tms[:4, 0:4], in_=tmf[:4, 0:4, 0:64], axis=mybir.AxisListType.X, op=ALU.add), ("tmf",), ("tms",))
            actf(tms[:4, 0:4], tms[:4, 0:4], AF.Sqrt, ("tms",), ("tms",), scale=1.0 / 64, bias=EPS)
            recip(tms[:4, 0:4], tms[:4, 0:4], ("tms",), ("tms",))
            tt('dve', otm3, otm3, tms[:4, 0:4].unsqueeze(2).to_broadcast([4, 4, 64]), ALU.mult, ("otm", "tms"), ("otm",))
            tt('dve', otm3, otm3, bcv[:4, 64:128].unsqueeze(1).to_broadcast([4, 4, 64]), ALU.mult, ("otm", "bcv"), ("otm",))
            tt('dve', otb[:4, :], otm[:4, :], ggS[:, bb, :], ALU.mult, ("otm", "ggS"), ("otb",))
            bt = bank()
            for c in range(2):
                mm(banks[bt][:, c * 128:c * 128 + 4], otb[:4, c * 128:(c + 1) * 128], identb[:4, :4], c == 0, True, ("otb", "identb"), (f"ps{bt}",))
            for c in range(2):
                cp('act', catS[:, 6 + c, sl], banks[bt][:, c * 128:c * 128 + 4], (f"ps{bt}",), ("catS",))
        dma_sp(s_gs_g[l], SgS[:], ("SgS",), ("s_gs",))

        pgc = [0]
        def load_kv(kvpool, col):
            ig = pgc[0] % GD; i = pgc[0] % 2; pgc[0] += 1
            gather(kvf[ig][:], kvpool, ixK[:, col:col + 1], ("ixK",), (f"kvf{ig}",))
            bT = bank()
            for c in range(2):
                tr(banks[bT][:, c * 128:(c + 1) * 128], kvf[ig][:, c * 128:(c + 1) * 128], I_, (f"kvf{ig}", "consts"), (f"ps{bT}",))
            cp('dve', kpb[i][:], banks[bT][:, 0:256].rearrange("p (c k) -> p c k", k=128), (f"ps{bT}",), (f"kpb{i}",))
            cp('act', vpb[i][:, :, 0:64], kvf[ig][:, 256:512].rearrange("p (h d) -> p h d", d=64), (f"kvf{ig}",), (f"vpb{i}",))
            return kpb[i], f"kpb{i}", vpb[i], f"vpb{i}"

        for bb in range(NBS):
            sl = slice(4 * bb, 4 * bb + 4)
            for j in range(NPG):
                kb_, kr, vb_, vr = load_kv(pdkv, bb * NPG + j)
                bS = bank()
                for m in range(2):
                    for h in range(4):
                        c = h // 2; g = (h % 2) * 2 + m; o = (m * 4 + h) * 4
                        mm(banks[bS][:, o:o + 4], kb_[:, c, :], QdS[:, g, c, sl], (m == 0 and h == 0), True, (kr, "QdS"), (f"ps{bS}",))
                actf(PTs[:, 0:32], banks[bS][:, 0:32], AF.Exp, (f"ps{bS}",), ("PTs",), scale=32.0 ** -0.5)
                for m in range(2):
                    for h in range(4):
                        o = (m * 4 + h) * 4
                        mm(banks[6 + m][:4, h * 65:(h + 1) * 65], PTs[:, o:o + 4], vb_[:, h, :], (j == 0 and h == 0), False, ("PTs", vr), (f"ps{6 + m}",))
            bS = bank()
            for m in range(2):
                for h in range(4):
                    c = h // 2; g = (h % 2) * 2 + m; o = (m * 4 + h) * 4
                    mm(banks[bS][:4, o:o + 4], KdN[:, c, sl], QdS[:, g, c, sl], (m == 0 and h == 0), True, ("KdN", "QdS"), (f"ps{bS}",))
            actf(PTn[:, 0:32], banks[bS][:4, 0:32], AF.Exp, (f"ps{bS}",), ("PTn",), scale=32.0 ** -0.5)
            tt('dve', PTn[:, 0:32].rearrange("p (g t) -> p g t", t=4), PTn[:, 0:32].rearrange("p (g t) -> p g t", t=4),
               Ubf[0:4, 0:4].unsqueeze(1).to_broadcast([4, 8, 4]), ALU.mult, ("PTn", "Ubf"), ("PTn",))
            for m in range(2):
                for h in range(4):
                    o = (m * 4 + h) * 4
                    mm(banks[6 + m][:4, h * 65:(h + 1) * 65], PTn[:, o:o + 4], vdN[:, bb, h, :], False, True, ("PTn", "vdN"), (f"ps{6 + m}",))
            finalize_tm(4, 'diff', lam_init, lambda c, bt, sl=sl: cp('act', catS[:, 2 + c, sl], banks[bt][:, c * 128:c * 128 + 4], (f"ps{bt}",), ("catS",)))

        memset('dve', scrS[:], 0.0, ("scrS",))
        for h in range(8):
            cp('dve', ZP[:, h, 112:128], iqmS[:, h, :], ("iqmS",), ("ZP",))
        ipc = [0]
        for bb in range(NBS):
            for j in range(NPG):
                r = j // PPR; col0 = (j % PPR) * 128
                ig = ipc[0] % GD; i = ipc[0] % 2; ipc[0] += 1
                gather(ipf[ig][:], pik, ixI[:, bb * NPG + j:bb * NPG + j + 1], ("ixI",), (f"ipf{ig}",))
                cp('act', ipb[i][:], ipf[ig][:], (f"ipf{ig}",), (f"ipb{i}",))
                for hg in range(2):
                    bI = bank()
                    for hh in range(4):
                        h = hg * 4 + hh
                        mm(banks[bI][:, hh * 128:(hh + 1) * 128], ZP[:, h, 112 - 16 * r:240 - 16 * r], ipb[i][:], hh == 0, True, ("ZP", f"ipb{i}"), (f"ps{bI}",))
                    actf(tA[:, 0:512], banks[bI][:, :], AF.Relu, (f"ps{bI}",), ("tA",))
                    for hh in range(4):
                        h = hg * 4 + hh
                        stt(scrS[:, col0:col0 + 128], tA[:, hh * 128:(hh + 1) * 128], iwB[:, bb, h:h + 1], scrS[:, col0:col0 + 128], ALU.mult, ALU.add,
                            ("tA", "iwB", "scrS"), ("scrS",))
            bI = bank()
            for h in range(8):
                mm(banks[bI][:, h * 4:h * 4 + 4], ZP[:, h, 112:240], ikN[:, 4 * bb:4 * bb + 4], h == 0, True, ("ZP", "ikN"), (f"ps{bI}",))
            actf(tA[:, 0:32], banks[bI][:, 0:32], AF.Relu, (f"ps{bI}",), ("tA",))
            for h in range(8):
                stt(scrS[:, W:W + 4], tA[:, h * 4:h * 4 + 4], iwB[:, bb, h:h + 1], scrS[:, W:W + 4], ALU.mult, ALU.add, ("tA", "iwB", "scrS"), ("scrS",))
        P.op('dve', lambda e: e.tensor_reduce(out=bisS[:, 6:7], in_=scrS[:, 0:W + 4], axis=mybir.AxisListType.X, op=ALU.max, apply_absolute_value=True), ("scrS",), ("bisS",))
        bG = bank()
        mm(banks[bG][:, 0:1], Gm, bisS[:, 6:7], True, True, ("cS", "bisS"), (f"ps{bG}",))
        cp('dve', bisS[:, 1:2], banks[bG][:, 0:1], (f"ps{bG}",), ("bisS",))
        ts('dve', bisS[:, 0:1], bisS[:, 1:2], -1.0, None, ALU.mult, None, ("bisS",), ("bisS",))
        tt('dve', scrS[:, W:W + 4], scrS[:, W:W + 4], negS, ALU.add, ("scrS", "cS"), ("scrS",))
        for it in range(NBIS + 3):
            tt('dve', bisS[:, 2:3], bisS[:, 0:1], bisS[:, 1:2], ALU.add, ("bisS",), ("bisS",))
            ts('dve', bisS[:, 2:3], bisS[:, 2:3], 0.5, None, ALU.mult, None, ("bisS",), ("bisS",))
            P.op('dve', lambda e: e.tensor_scalar(out=mbS[:, :], in0=scrS[:, :], scalar1=bisS[:, 2:3], scalar2=0.0, op0=ALU.is_ge, op1=ALU.add,
                                                  accum_out=bisS[:, 3:4]), ("scrS", "bisS"), ("mbS", "bisS"))
            bG = bank()
            mm(banks[bG][:, 0:1], Gm, bisS[:, 3:4], True, True, ("cS", "bisS"), (f"ps{bG}",))
            ts('dve', bisS[:, 4:5], banks[bG][:, 0:1], float(topk_s) - 0.5, None, ALU.is_gt, None, (f"ps{bG}",), ("bisS",))
            tt('dve', bisS[:, 5:6], bisS[:, 2:3], bisS[:, 0:1], ALU.subtract, ("bisS",), ("bisS",))
            stt(bisS[:, 0:1], bisS[:, 5:6], bisS[:, 4:5], bisS[:, 0:1], ALU.mult, ALU.add, ("bisS",), ("bisS",))
            tt('dve', bisS[:, 5:6], bisS[:, 1:2], bisS[:, 2:3], ALU.subtract, ("bisS",), ("bisS",))
            stt(bisS[:, 1:2], bisS[:, 5:6], bisS[:, 4:5], bisS[:, 2:3], ALU.mult, ALU.add, ("bisS",), ("bisS",))
        P.op('dve', lambda e: e.tensor_scalar(out=mbS[:, :], in0=scrS[:, :], scalar1=bisS[:, 0:1], scalar2=-30000.0, op0=ALU.is_lt, op1=ALU.mult),
             ("scrS", "bisS"), ("mbS",))
        for bb in range(NBS):
            sl = slice(4 * bb, 4 * bb + 4)
            for j in range(NPG):
                r = j // PPR; col0 = (j % PPR) * 128
                kb_, kr, vb_, vr = load_kv(pckv, bb * NPG + j)
                bS = bank()
                for h in range(4):
                    c = h // 2
                    mm(banks[bS][:, h * 4:h * 4 + 4], kb_[:, c, :], QcS[:, h % 2, c, sl], h == 0, False, (kr, "QcS"), (f"ps{bS}",))
                    mm(banks[bS][:, h * 4:h * 4 + 4], mbS[:, col0:col0 + 128], identb[:, 16 * r + 4 * bb:16 * r + 4 * bb + 4], False, True,
                       ("mbS", "identb"), (f"ps{bS}",))
                actf(PTs[:, 0:16], banks[bS][:, 0:16], AF.Exp, (f"ps{bS}",), ("PTs",), scale=0.125)
                for h in range(4):
                    mm(banks[6][:4, h * 65:(h + 1) * 65], PTs[:, h * 4:h * 4 + 4], vb_[:, h, :], (j == 0 and h == 0), False, ("PTs", vr), ("ps6",))
            bS = bank()
            for h in range(4):
                c = h // 2
                mm(banks[bS][:4, h * 4:h * 4 + 4], KcN[:, c, sl], QcS[:, h % 2, c, sl], h == 0, False, ("KcN", "QcS"), (f"ps{bS}",))
                mm(banks[bS][:4, h * 4:h * 4 + 4], mbS[:, W:W + 4], identb[:, 4 * bb:4 * bb + 4], False, True, ("mbS", "identb"), (f"ps{bS}",))
            actf(PTn[:, 0:16], banks[bS][:4, 0:16], AF.Exp, (f"ps{bS}",), ("PTn",), scale=0.125)
            for h in range(4):
                mm(banks[6][:4, h * 65:(h + 1) * 65], PTn[:, h * 4:h * 4 + 4], vcN[:, bb, h, :], False, True, ("PTn", "vcN"), ("ps6",))
            finalize_tm(4, 'dsa', lam_init, lambda c, bt, sl=sl: cp('act', catS[:, 4 + c, sl], banks[bt][:, c * 128:c * 128 + 4], (f"ps{bt}",), ("catS",)))

        for half in range(2):
            wt, wr = load_w(("out", l, half), wout[l][:, half * 512:(half + 1) * 512], D, 512)
            for oc in range(4):
                o = half * 4 + oc
                b = bank()
                for kc in range(KC):
                    mm(banks[b][:, :NS], wt[:, kc, oc * 128:(oc + 1) * 128], catS[:, kc, :], kc == 0, kc == KC - 1, (wr, "catS"), (f"ps{b}",))
                for bb in range(NBS):
                    sl = slice(4 * bb, 4 * bb + 4)
                    stt(xsS[:, o, sl], banks[b][:, sl], mod[:, 16 + o, 1 + 4 * grp + bb:2 + 4 * grp + bb], xsS[:, o, sl], ALU.mult, ALU.add, (f"ps{b}", "mod", XS), (XS,))
        norm_mod_s((A2, "A2"), 24)
        for c0 in range(0, FC, 4):
            ncg = min(4, FC - c0)
            wtg, wrg = load_w(("g", l, c0), wg[l][:, c0 * 128:(c0 + ncg) * 128], D, ncg * 128)
            wtu, wru = load_w(("u", l, c0), wu[l][:, c0 * 128:(c0 + ncg) * 128], D, ncg * 128)
            for ii in range(ncg):
                c = c0 + ii
                bg = bank(); bu = bank()
                for kc in range(KC):
                    mm(banks[bg][:, :NS], wtg[:, kc, ii * 128:(ii + 1) * 128], hS[:, kc, :], kc == 0, kc == KC - 1, (wrg, "hS"), (f"ps{bg}",))
                for kc in range(KC):
                    mm(banks[bu][:, :NS], wtu[:, kc, ii * 128:(ii + 1) * 128], hS[:, kc, :], kc == 0, kc == KC - 1, (wru, "hS"), (f"ps{bu}",))
                fw = lambda k: vecs[:, V_FCW + c * 3 + k:V_FCW + c * 3 + k + 1]
                cp('dve', fpad[:, :, 0:2], fcS[:, c, :, :], ("fcS",), ("fpad",))
                cp('act', fpad[:, :, 2:6], banks[bg][:, :NS].rearrange("p (b i) -> p b i", i=4), (f"ps{bg}",), ("fpad",))
                uA3 = uA[:].rearrange("p (b i) -> p b i", i=4)
                ts('dve', uA3, fpad[:, :, 2:6], fw(2), vecs[:, V_FCB + c:V_FCB + c + 1], ALU.mult, ALU.add, ("fpad", "vecs"), ("uA",))
                stt(uA3, fpad[:, :, 1:5], fw(1), uA3, ALU.mult, ALU.add, ("fpad", "vecs", "uA"), ("uA",))
                stt(uA3, fpad[:, :, 0:4], fw(0), uA3, ALU.mult, ALU.add, ("fpad", "vecs", "uA"), ("uA",))
                cp('dve', fcSn[:, c, :, :], fpad[:, :, 4:6], ("fpad",), ("fcSn",))
                actf(uA[:], uA[:], AF.Silu, ("uA",), ("uA",))
                tt('dve', actS[:, c, :], uA[:], banks[bu][:, :NS], ALU.mult, ("uA", f"ps{bu}"), ("actS",))
        dma_sp(s_fc_g[l], fcSn[:], ("fcSn",), ("s_fc",))
        for o in range(KC):
            wt, wr = load_w(("d", l, o), wd[l][:, o * 128:(o + 1) * 128], DFF, 128)
            b = bank()
            for kc in range(FC):
                mm(banks[b][:, :NS], wt[:, kc, :], actS[:, kc, :], kc == 0, kc == FC - 1, (wr, "actS"), (f"ps{b}",))
            for bb in range(NBS):
                sl = slice(4 * bb, 4 * bb + 4)
                stt(xsS[:, o, sl], banks[b][:, sl], mod[:, 40 + o, 1 + 4 * grp + bb:2 + 4 * grp + bb], xsS[:, o, sl], ALU.mult, ALU.add, (f"ps{b}", "mod", XS), (XS,))
        if l == L - 1:
            b = bank()
            for kc in range(KC):
                actf(sqS[:], xsS[:, kc, :], AF.Square, (XS,), ("sqS",))
                mm(banks[b][:, :NS], onesb[:], sqS[:], kc == 0, kc == KC - 1, ("sqS", "onesb"), (f"ps{b}",))
            actf(rstdS[:], banks[b][:, :NS], AF.Sqrt, (f"ps{b}",), ("rstdS",), scale=1.0 / D, bias=EPS)
            recip(rstdS[:], rstdS[:], ("rstdS",), ("rstdS",))
            for kc in range(KC):
                stt(xsS[:, kc, :], xsS[:, kc, :], fng[:, kc:kc + 1], rstdS[:], ALU.mult, ALU.mult, (XS, "fng", "rstdS"), (XS,))
            dma_sp(ysT_g, xsS[:], (XS,), ("ysT",))
        P.op('dve', lambda e: e.memset(bisS[:, 7:8], 0.0), (), FENCE + ("bisS",))

    dma_sp(consts[:], consts_in, (), ("consts",))
    cp('dve', identb[:], I_, ("consts",), ("identb",))
    cp('dve', Ubf[:], U_, ("consts",), ("Ubf",))
    memset('pool', onesb[:], 1.0, ("onesb",))
    memset('pool', ones1[:], 1.0, ("ones1",))
    dma_sp(fng[:], fng_in, (), ("fng",))
    dma_sp(cTt[:], cT_in, (), ("cTt",))
    actf(csil[:], cTt[:], AF.Silu, ("cTt",), ("csil",))
    memset('pool', Vd[:, :, :, 64:65], 1.0, ("Vd",))
    memset('pool', Vc[:, :, :, 64:65], 1.0, ("Vc",))

    if SAMPLE:
        sample_setup()
    ckpt(1)
    for l in range(L):
        lam_init = lam_inits[l]
        dma_sp(vecs[:], vecs_in[l], (), ("vecs",))
        dma_pool(lruw[:], lruw_in[l], (), ("lruw",))
        dma_pool(glaw[:], glaw_in[l], (), ("glaw",))
        dma_pool(glab[:], glaw_in[l, 16:17, :], (), ("glab",))
        dma_sp(bcv[:], bc_in[l].partition_broadcast(128), (), ("bcv",))
        for pc in range(12):
            wt, wr = load_w(("ada", l, pc), wada[l][:, pc * 512:(pc + 1) * 512], D, 512)
            b = bank()
            for j in range(4):
                for kc in range(KC):
                    mm(banks[b][:, j * 16:j * 16 + NBC], wt[:, kc, j * 128:(j + 1) * 128], csil[:, kc, :], kc == 0, kc == KC - 1,
                       (wr, "csil"), (f"ps{b}",))
            for j in range(4):
                ch = pc * 4 + j
                actf(mod[:, ch, :], banks[b][:, j * 16:j * 16 + NBC], AF.Identity, (f"ps{b}", "vecs"), ("mod",),
                     bias=vecs[:, V_BADA + ch:V_BADA + ch + 1])
        ckpt(2)
        stt(A1[:], mod[:, 8:16, :], 1.0, vecs[:, V_N1:V_N1 + 8].unsqueeze(2).to_broadcast([128, 8, NBC]), ALU.add, ALU.mult, ("mod", "vecs"), ("A1",))
        stt(A2[:], mod[:, 32:40, :], 1.0, vecs[:, V_N2:V_N2 + 8].unsqueeze(2).to_broadcast([128, 8, NBC]), ALU.add, ALU.mult, ("mod", "vecs"), ("A2",))
        actf(cl[:], vecs[:, V_LAM:V_LAM + 2], AF.Exp, ("vecs",), ("cl",), scale=-1.0)
        actf(cl[:], cl[:], AF.Ln, ("cl",), ("cl",), bias=1.0)
        ts('dve', cl2[:], cl[:], -16.0, None, ALU.mult, None, ("cl",), ("cl2",))
        ts('dve', cl[:], cl[:], -8.0, None, ALU.mult, None, ("cl", "cl2"), ("cl",))
        tt('dve', tms[:, 0:32], bcv[:, 128:160], bcv[:, 160:192], ALU.mult, ("bcv",), ("tms",))
        P.op('dve', lambda e: e.reduce_sum(out=lamt[:, 0:1], in_=tms[:, 0:32], axis=mybir.AxisListType.X), ("tms",), ("lamt",))
        tt('dve', tms[:, 0:32], bcv[:, 192:224], bcv[:, 224:256], ALU.mult, ("bcv", "lamt"), ("tms",))
        P.op('dve', lambda e: e.reduce_sum(out=lamt[:, 1:2], in_=tms[:, 0:32], axis=mybir.AxisListType.X), ("tms",), ("lamt",))
        actf(lamt[:, 0:2], lamt[:, 0:2], AF.Exp, ("lamt",), ("lamt",))
        tt('dve', lamt[:, 2:3], lamt[:, 0:1], lamt[:, 1:2], ALU.subtract, ("lamt",), ("lamt",))
        ts('dve', lamt[:, 3:4], lamt[:, 2:3], lam_init, -1.0, ALU.add, ALU.mult, ("lamt",), ("lamt",))
        memset('pool', Sg[:], 0.0, ("Sg",)); memset('pool', Sgb[:], 0.0, ("Sgb",))
        memset('pool', xpad[:, :, 0:3], 0.0, ("xpad",))
        memset('pool', fcs[:], 0.0, ("fcs",))
        memset('pool', hlast[:], 0.0, ("hlast",))

        ckpt(3)
        xsrc = xT_in if l == 0 else xs[l - 1]
        xdst = None if l == L - 1 else xs[l]
        for j in range(NB):
            t0 = j * BLK
            N = BLK
            dma_sp(xt[:], xsrc.rearrange("(k p) t -> p k t", p=128)[:, :, t0:t0 + N], (f"xs{l-1}",) if l > 0 else (), ("xt",))
            dma_sp(ropet[:], rope_in.rearrange("r p t -> p r t")[:, :, t0:t0 + N], (), ("ropet",))

            def norm_mod(Acoef, shoff):
                b = bank()
                for kc in range(KC):
                    actf(sqb[:], xt[:, kc, :], AF.Square, ("xt",), ("sqb",))
                    mm(banks[b][:, :N], onesb[:], sqb[:], kc == 0, kc == KC - 1, ("sqb", "onesb"), (f"ps{b}",))
                actf(rstd[:], banks[b][:, :N], AF.Sqrt, (f"ps{b}",), ("rstd",), scale=1.0 / D, bias=EPS)
                recip(rstd[:], rstd[:], ("rstd",), ("rstd",))
                for kc in range(KC):
                    tt('dve', tA[:, :N], xt[:, kc, :], rstd[:], ALU.mult, ("xt", "rstd"), ("tA",))
                    actf(hT[:, kc, :], tA[:, :N], AF.Identity, ("tA", "mod", Acoef[1]), ("hT",),
                         scale=Acoef[0][:, kc, 0:1], bias=mod[:, shoff + kc, 0:1])
            norm_mod((A1, "A1"), 0)
            ckpt(4)

            def fm_proj(chunks, handler):
                c0 = chunks[0]; nch = len(chunks)
                wt, wr = load_w(("fm", l, c0), winfm[l][:, c0 * 128:(c0 + nch) * 128], D, nch * 128)
                for ii, ci in enumerate(chunks):
                    b = bank()
                    for kc in range(KC):
                        mm(banks[b][:, :N], wt[:, kc, ii * 128:(ii + 1) * 128], hT[:, kc, :], kc == 0, kc == KC - 1, (wr, "hT"), (f"ps{b}",))
                    handler(ci, b)

            def rope_out(out, outres, bp, bs, tab):
                tt('dve', tB[:], banks[bp][:, :N], ropet[:, tab, :], ALU.mult, (f"ps{bp}", "ropet"), ("tB",))
                tt('dve', tC[:], banks[bs][:, :N], ropet[:, tab + 1, :], ALU.mult, (f"ps{bs}", "ropet"), ("tC",))
                tt('pool', out, tB[:], tC[:], ALU.add, ("tB", "tC"), (outres,))

            def h_lru(ci, b):
                if ci < 2:
                    cp('act', xpad[:, ci, 3:3 + N], banks[b][:, :N], (f"ps{b}",), ("xpad",))
                else:
                    c = ci - 2
                    cp('act', tD[:], banks[b][:, :N], (f"ps{b}",), ("tD",))
                    tt('dve', tB[:], tD[:], tD[:], ALU.mult, ("tD",), ("tB",))
                    ts('dve', tB[:], tB[:], 0.044715, 1.0, ALU.mult, ALU.add, ("tB",), ("tB",))
                    tt('dve', tB[:], tB[:], tD[:], ALU.mult, ("tB", "tD"), ("tB",))
                    actf(tB[:], tB[:], AF.Sigmoid, ("tB",), ("tB",), scale=GC)
                    tt('dve', tD[:], tD[:], tB[:], ALU.mult, ("tD", "tB"), ("tD",))
                    lru_chunk(c)
            def lru_chunk(c):
                cw = lambda k: vecs[:, V_CW + c * 4 + k:V_CW + c * 4 + k + 1]
                ts('dve', tA[:, :N], xpad[:, c, 3:3 + N], cw(3), vecs[:, V_CB + c:V_CB + c + 1], ALU.mult, ALU.add, ("xpad", "vecs"), ("tA",))
                for k in range(3):
                    stt(tA[:, :N], xpad[:, c, k:k + N], cw(k), tA[:, :N], ALU.mult, ALU.add, ("xpad", "vecs", "tA"), ("tA",))
                if j == NB - 1:
                    dma_sp(o_lc[l][:, c, :], xpad[:, c, N:N + 3], ("xpad",), ("o_lc",))
                else:
                    cp('pool', xpad[:, c, 0:3], xpad[:, c, N:N + 3], ("xpad",), ("xpad",)) if False else None
                cp('act', xab[:], tA[:, :N], ("tA",), ("xab",))
                ba_ = bank(); bx_ = bank()
                mm(banks[ba_][:, :N], lruw[:, c, :], xab[:], True, True, ("lruw", "xab"), (f"ps{ba_}",))
                mm(banks[bx_][:, :N], lruw[:, 2 + c, :], xab[:], True, True, ("lruw", "xab"), (f"ps{bx_}",))
                actf(tB[:], banks[ba_][:, :N], AF.Sigmoid, (f"ps{ba_}", "vecs"), ("tB",), bias=vecs[:, V_BA + c:V_BA + c + 1])
                actf(tC[:], banks[bx_][:, :N], AF.Sigmoid, (f"ps{bx_}", "vecs"), ("tC",), bias=vecs[:, V_BX + c:V_BX + c + 1])
                actf(tE[:], tB[:], AF.Exp, ("tB", "cl2"), ("tE",), scale=cl2[:, c:c + 1])
                actf(tB[:], tB[:], AF.Exp, ("tB", "cl"), ("tB",), scale=cl[:, c:c + 1])
                actf(tE[:], tE[:], AF.Sqrt, ("tE",), ("tE",), scale=-1.0, bias=1.0)
                if j == 0:
                    memset('dve', tE[:, 0:1], 1.0, ("tE",))
                tt('dve', tC[:], tC[:], tA[:, :N], ALU.mult, ("tC", "tA"), ("tC",))
                tt('dve', tC[:], tC[:], tE[:], ALU.mult, ("tC", "tE"), ("tC",))
                P.op('dve', lambda e: e.tensor_tensor_scan(out=tA[:, :N], data0=tB[:], data1=tC[:], initial=hlast[:, c:c + 1],
                                                           op0=ALU.mult, op1=ALU.add), ("tB", "tC", "hlast"), ("tA",))
                cp('dve', hlast[:, c:c + 1], tA[:, N - 1:N], ("tA",), ("hlast",))
                tt('dve', catT[:, c, :], tA[:, :N], tD[:], ALU.mult, ("tA", "tD"), ("catT",))
            if j > 0:
                for c in range(2):
                    cp('dve', xpad[:, c, 0:3], xpad[:, c, N:N + 3], ("xpad",), ("xpad",))
            fm_proj([0, 1, 2, 3], h_lru)
            ckpt(5)
            if j == NB - 1:
                dma_sp(o_lh[l], hlast[:], ("hlast",), ("o_lh",))

            pend = {}
            def h_rope(ci, b):
                pend[ci] = b
            def do_rope(cbase, tab, dest_fn):
                wt, wr = load_w(("fm", l, cbase), winfm[l][:, cbase * 128:(cbase + 4) * 128], D, 512)
                bs = [bank() for _ in range(4)]
                for ii in range(4):
                    b = bs[ii]
                    for kc in range(KC):
                        mm(banks[b][:, :N], wt[:, kc, ii * 128:(ii + 1) * 128], hT[:, kc, :], kc == 0, kc == KC - 1, (wr, "hT"), (f"ps{b}",))
                for c in range(2):
                    dest_fn(c, bs[c], bs[2 + c], tab)
            def dq(c, bp, bsw, tab):
                rope_out(tE[:], "tE", bp, bsw, tab)
                for g in range(4):
                    ts('pool' if g % 2 else 'dve', QdT[:, g, c, :], tE[:], HM(g), None, ALU.mult, None, ("tE", "consts"), ("QdT",))
            def dk(c, bp, bsw, tab):
                rope_out(kdf[:], "kdf", bp, bsw, tab)
                cp('act', KdT[:, c, t0:t0 + N], kdf[:], ("kdf",), ("KdT",))
                dma_sp(o_kd[l][c * 128:(c + 1) * 128, t0:t0 + N], kdf[:], ("kdf",), ("o_kd",))
            def cq(c, bp, bsw, tab):
                rope_out(tE[:], "tE", bp, bsw, tab)
                ts('dve', QcT[:, 0, c, :], tE[:], HM(6), None, ALU.mult, None, ("tE", "consts"), ("QcT",))
                ts('pool', QcT[:, 1, c, :], tE[:], HM(7), None, ALU.mult, None, ("tE", "consts"), ("QcT",))
            def ck(c, bp, bsw, tab):
                rope_out(kdf[:], "kdf", bp, bsw, tab)
                cp('act', KcT[:, c, t0:t0 + N], kdf[:], ("kdf",), ("KcT",))
                dma_sp(o_kc[l][c * 128:(c + 1) * 128, t0:t0 + N], kdf[:], ("kdf",), ("o_kc",))
            def iq(c, bp, bsw, tab):
                rope_out(tE[:], "tE", bp, bsw, tab)
                for g in range(4):
                    ts('pool' if g % 2 else 'dve', iqm[:, c * 4 + g, :], tE[:], HM(g), None, ALU.mult, None, ("tE", "consts"), ("iqm",))
            do_rope(4, 0, dq); do_rope(8, 0, dk); do_rope(12, 2, cq); do_rope(16, 2, ck); do_rope(20, 0, iq)
            wt, wr = load_w(("fm", l, 24), winfm[l][:, 24 * 128:28 * 128], D, 512)
            bs = [bank() for _ in range(4)]
            for ii in range(4):
                b = bs[ii]
                for kc in range(KC):
                    mm(banks[b][:, :N], wt[:, kc, ii * 128:(ii + 1) * 128], hT[:, kc, :], kc == 0, kc == KC - 1, (wr, "hT"), (f"ps{b}",))
            cp('act', gqT[:], banks[bs[0]][:, :N], (f"ps{bs[0]}",), ("gqT",))
            cp('act', gkT[:], banks[bs[1]][:, :N], (f"ps{bs[1]}",), ("gkT",))
            rope_out(kdf[:], "kdf", bs[2], bs[3], 0)
            cp('act', ikT[:, t0:t0 + N], kdf[:], ("kdf",), ("ikT",))
            dma_sp(o_ik[l][:, t0:t0 + N], kdf[0:32, :], ("kdf",), ("o_ik",))
            wt, wr = load_w(("fm", l, 28), winfm[l][:, 28 * 128:29 * 128], D, 128)
            b = bank()
            for kc in range(KC):
                mm(banks[b][:16, :N], wt[:, kc, 0:16], hT[:, kc, :], kc == 0, kc == KC - 1, (wr, "hT"), (f"ps{b}",))
            cp('act', gaT[:], banks[b][:16, :N], (f"ps{b}",), ("gaT",))

            ckpt(6)
            for pi, (c0, c1) in enumerate(((0, 512), (512, 1024), (1024, NTM))):
                wt, wr = load_w(("tm", l, c0), wintm[l][:, c0:c1], D, c1 - c0)
                for q in range(QPB):
                    qt = j * QPB + q
                    tok = slice(q * 128, (q + 1) * 128)
                    b = bank()
                    for kc in range(KC):
                        mm(banks[b][:, :c1 - c0], hT[:, kc, tok], wt[:, kc, :], kc == 0, kc == KC - 1, (wr, "hT"), (f"ps{b}",))
                    if pi == 0:
                        cp('act', vout[:, 0, :], banks[b][:, 0:256], (f"ps{b}",), ("vout",))
                        dma_sp(o_vd[l][t0 + q * 128:t0 + (q + 1) * 128, :], vout[:, 0, :], ("vout",), ("o_vd",))
                        cp('dve', Vd[:, qt, :, 0:64], banks[b][:, 0:256].rearrange("p (h d) -> p h d", d=64), (f"ps{b}",), ("Vd",))
                        cp('dve', Vc[:, qt, :, 0:64], banks[b][:, 256:512].rearrange("p (h d) -> p h d", d=64), (f"ps{b}",), ("Vc",))
                        cp('act', vout[:, 0, :], banks[b][:, 256:512], (f"ps{b}",), ("vout",))
                        dma_sp(o_vc[l][t0 + q * 128:t0 + (q + 1) * 128, :], vout[:, 0, :], ("vout",), ("o_vc",))
                    elif pi == 1:
                        cp('act', gv[:, q, :], banks[b][:, 0:256], (f"ps{b}",), ("gv",))
                        actf(gg[:, q, :], banks[b][:, 256:512], AF.Silu, (f"ps{b}",), ("gg",))
                    else:
                        cp('act', gk[:, q, :], banks[b][:, 0:128], (f"ps{b}",), ("gk",))
                        cp('act', iw[:, q, :], banks[b][:, 128:136], (f"ps{b}",), ("iw",))
            ckpt(7)
            for q in range(QPB):
                qt = j * QPB + q
                tok = slice(q * 128, (q + 1) * 128)
                b = bank()
                mm(banks[b][:, 0:128], gaT[:, tok], glaw[0:16, :], True, False, ("gaT", "glaw"), (f"ps{b}",))
                mm(banks[b][:, 0:128], ones1[:, :], glab[:, :], False, True, ("ones1", "glab"), (f"ps{b}",))
                actf(g_la[:], banks[b][:, 0:128], AF.Exp, (f"ps{b}",), ("g_la",), scale=-1.0)
                actf(g_la[:], g_la[:], AF.Ln, ("g_la",), ("g_la",), bias=1.0)
                ts('dve', g_la[:], g_la[:], -1.0 / 16.0, None, ALU.mult, None, ("g_la",), ("g_la",))
                bb = bank()
                mm(banks[bb][:, 0:128], g_la[:], U_, True, True, ("g_la", "consts"), (f"ps{bb}",))
                mm(banks[bb][:, 128:256], L_, g_la[:], False, True, ("g_la", "consts"), (f"ps{bb}",))
                actf(g_e1[:], banks[bb][:, 0:128], AF.Exp, (f"ps{bb}",), ("g_e1",))
                actf(g_e2[:], banks[bb][:, 0:128], AF.Exp, (f"ps{bb}",), ("g_e2",), scale=-1.0)
                stt(g_qt[:], gqT[:, tok], 32.0 ** -0.5, g_e1[:], ALU.mult, ALU.mult, ("gqT", "g_e1"), ("g_qt",))
                for h in range(4):
                    ts('pool' if h % 2 else 'dve', g_qm[:, h, :], g_qt[:], HM(h), None, ALU.mult, None, ("g_qt", "consts"), ("g_qm",))
                tt('dve', g_kt[:], gkT[:, tok], g_e2[:], ALU.mult, ("gkT", "g_e2"), ("g_kt",))
                actf(g_e2[:], banks[bb][:, 128:256], AF.Exp, (f"ps{bb}", "g_kt"), ("g_e2",))
                tt('dve', g_kh[:], gk[:, q, :], g_e2[:], ALU.mult, ("gk", "g_e2"), ("g_kh",))
                ba_ = bank()
                for h in range(4):
                    mm(banks[ba_][:, h * 128:(h + 1) * 128], g_kt[:, :], g_qm[:, h, :], h == 0, True, ("g_kt", "g_qm"), (f"ps{ba_}",))
                tt('dve', g_at[:], banks[ba_][:, :].rearrange("p (h t) -> p h t", t=128), Ubf[:].unsqueeze(1).to_broadcast([128, 4, 128]), ALU.mult,
                   (f"ps{ba_}", "Ubf"), ("g_at",))
                bo = bank()
                for h in range(4):
                    mm(banks[bo][:, h * 64:(h + 1) * 64], g_at[:, h, :], gv[:, q, h * 64:(h + 1) * 64], h == 0, False, ("g_at", "gv"), (f"ps{bo}",))
                    mm(banks[bo][:, h * 64:(h + 1) * 64], g_qm[:, h, :], Sgb[:, :], False, True, ("g_qm", "Sgb"), (f"ps{bo}",))
                bs_ = bank()
                mm(banks[bs_][:, 0:256], g_kh[:, :], gv[:, q, :], True, True, ("g_kh", "gv"), (f"ps{bs_}",))
                ts('dve', Sg[:], Sg[:], g_e1[:, 127:128], None, ALU.mult, None, ("Sg", "g_e1"), ("Sg",))
                for h in range(4):
                    stt(Sg[:], banks[bs_][:, h * 64:(h + 1) * 64], HM(h), Sg[:], ALU.mult, ALU.add, ("Sg", "consts", f"ps{bs_}"), ("Sg",))
                cp('act', Sgb[:], Sg[:], ("Sg",), ("Sgb",))
                if qt == NT - 1:
                    dma_sp(o_gs[l], Sg[:], ("Sg",), ("o_gs",))
                o3 = banks[bo][:, 0:256].rearrange("p (h d) -> p h d", d=64)
                tt('dve', otm[:].rearrange("p (h d) -> p h d", d=64), o3, o3, ALU.mult, (f"ps{bo}",), ("otm",)) if False else None
                cp('act', otm[:], banks[bo][:, 0:256], (f"ps{bo}",), ("otm",))
                otm3 = otm[:].rearrange("p (h d) -> p h d", d=64)
                tt('dve', tmf[:, 0:4, 0:64], otm3, otm3, ALU.mult, ("otm",), ("tmf",))
                P.op('dve', lambda e: e.tensor_reduce(out=tms[:, 0:4], in_=tmf[:, 0:4, 0:64], axis=mybir.AxisListType.X, op=ALU.add), ("tmf",), ("tms",))
                actf(tms[:, 0:4], tms[:, 0:4], AF.Sqrt, ("tms",), ("tms",), scale=1.0 / 64, bias=EPS)
                recip(tms[:, 0:4], tms[:, 0:4], ("tms",), ("tms",))
                tt('dve', otm3, otm3, tms[:, 0:4].unsqueeze(2).to_broadcast([128, 4, 64]), ALU.mult, ("otm", "tms"), ("otm",))
                tt('dve', otm3, otm3, bcv[:, 64:128].unsqueeze(1).to_broadcast([128, 4, 64]), ALU.mult, ("otm", "bcv"), ("otm",))
                tt('dve', otb[:], otm[:], gg[:, q, :], ALU.mult, ("otm", "gg"), ("otb",))
                bt = bank()
                for c in range(2):
                    mm(banks[bt][:, c * 128:(c + 1) * 128], otb[:, c * 128:(c + 1) * 128], identb[:], c == 0, True, ("otb", "identb"), (f"ps{bt}",))
                cp('act', catT[:, 6:8, tok], banks[bt][:, 0:256].rearrange("p (c t) -> p c t", t=128), (f"ps{bt}",), ("catT",))

                ckpt(8)
                for kt in range(qt + 1):
                    bsA = 4 + (kt % 2)
                    for m in range(2):
                        bS = bank()
                        for h in range(4):
                            c = h // 2; g = (h % 2) * 2 + m
                            mm(banks[bS][:, h * 128:(h + 1) * 128], KdT[:, c, kt * 128:(kt + 1) * 128], QdT[:, g, c, tok], h == 0, True,
                               ("KdT", "QdT"), (f"ps{bS}",))
                        actf(PT[:, m * 512:(m + 1) * 512], banks[bS][:, :], AF.Exp, (f"ps{bS}",), ("PT",), scale=32.0 ** -0.5)
                    ckpt(81)
                    if kt == qt:
                        tt('dve', PT[:].rearrange("p (g t) -> p g t", t=128), PT[:].rearrange("p (g t) -> p g t", t=128),
                           Ubf[:].unsqueeze(1).to_broadcast([128, 8, 128]), ALU.mult, ("PT", "Ubf"), ("PT",))
                    ckpt(82)
                    for m in range(2):
                        for h in range(4):
                            mm(banks[6 + m][:, h * 65:(h + 1) * 65], PT[:, (m * 4 + h) * 128:(m * 4 + h + 1) * 128], Vd[:, kt, h, :],
                               (kt == 0 and h == 0), kt == qt, ("PT", "Vd"), (f"ps{6 + m}",))
                ckpt(83)
                for m in range(2):
                    cp('act', tmf[:, m * 4:(m + 1) * 4, :], banks[6 + m][:, 0:260].rearrange("p (h d) -> p h d", d=65), (f"ps{6 + m}",), ("tmf",))
                recip(tms[:, 0:8], tmf[:, :, 64], ("tmf",), ("tms",))
                tt('dve', tms[:, 4:8], tms[:, 4:8], lamt[:, 3:4].to_broadcast([128, 4]), ALU.mult, ("tms", "lamt"), ("tms",))
                otm3 = otm[:].rearrange("p (h d) -> p h d", d=64)
                tt('dve', tmf[:, 0:4, 0:64], tmf[:, 0:4, 0:64], tms[:, 0:4].unsqueeze(2).to_broadcast([128, 4, 64]), ALU.mult, ("tmf", "tms"), ("tmf",))
                tt('dve', tmf[:, 4:8, 0:64], tmf[:, 4:8, 0:64], tms[:, 4:8].unsqueeze(2).to_broadcast([128, 4, 64]), ALU.mult, ("tmf", "tms"), ("tmf",))
                tt('dve', otm3, tmf[:, 0:4, 0:64], tmf[:, 4:8, 0:64], ALU.add, ("tmf",), ("otm",))
                tt('dve', tmf[:, 0:4, 0:64], otm3, otm3, ALU.mult, ("otm",), ("tmf",))
                P.op('dve', lambda e: e.tensor_reduce(out=tms[:, 8:12], in_=tmf[:, 0:4, 0:64], axis=mybir.AxisListType.X, op=ALU.add), ("tmf",), ("tms",))
                actf(tms[:, 8:12], tms[:, 8:12], AF.Sqrt, ("tms",), ("tms",), scale=1.0 / 64, bias=EPS)
                recip(tms[:, 8:12], tms[:, 8:12], ("tms",), ("tms",))
                tt('dve', otm3, otm3, tms[:, 8:12].unsqueeze(2).to_broadcast([128, 4, 64]), ALU.mult, ("otm", "tms"), ("otm",))
                stt(otb[:].rearrange("p (h d) -> p h d", d=64), otm3, 1.0 - lam_init, bcv[:, 0:64].unsqueeze(1).to_broadcast([128, 4, 64]),
                    ALU.mult, ALU.mult, ("otm", "bcv"), ("otb",))
                bt = bank()
                for c in range(2):
                    mm(banks[bt][:, c * 128:(c + 1) * 128], otb[:, c * 128:(c + 1) * 128], identb[:], c == 0, True, ("otb", "identb"), (f"ps{bt}",))
                cp('act', catT[:, 2:4, tok], banks[bt][:, 0:256].rearrange("p (c t) -> p c t", t=128), (f"ps{bt}",), ("catT",))

                ckpt(9)
                Lk = (qt + 1) * 128
                nkb = (Lk + 511) // 512
                for kb in range(nkb):
                    w_ = min(512, Lk - kb * 512)
                    ks = slice(kb * 512, kb * 512 + w_)
                    for h in range(8):
                        bI = bank()
                        mm(banks[bI][:, :w_], iqm[:, h, tok], ikT[:, ks], True, True, ("iqm", "ikT"), (f"ps{bI}",))
                        actf(tA[:, :w_], banks[bI][:, :w_], AF.Relu, (f"ps{bI}",), ("tA",))
                        if h == 0:
                            ts('dve', scr[:, ks], tA[:, :w_], iw[:, q, 0:1], None, ALU.mult, None, ("tA", "iw"), ("scr",))
                        else:
                            stt(scr[:, ks], tA[:, :w_], iw[:, q, h:h + 1], scr[:, ks], ALU.mult, ALU.add, ("tA", "iw", "scr"), ("scr",))
                P.op('dve', lambda e, Lk=Lk: e.tensor_reduce(out=bis[:, 6:7], in_=scr[:, 0:Lk], axis=mybir.AxisListType.X, op=ALU.max, apply_absolute_value=True), ("scr",), ("bis",))
                tt('dve', scr[:, qt * 128:Lk], scr[:, qt * 128:Lk], NEG_, ALU.add, ("scr", "consts"), ("scr",))
                if Lk > topk:
                    P.op('dve', lambda e, Lk=Lk: e.tensor_reduce(out=bis[:, 1:2], in_=scr[:, 0:Lk], axis=mybir.AxisListType.X, op=ALU.max), ("scr",), ("bis",))
                    ts('dve', bis[:, 0:1], bis[:, 6:7], -1.0, None, ALU.mult, None, ("bis",), ("bis",))
                    for it in range(NBIS):
                        tt('dve', bis[:, 2:3], bis[:, 0:1], bis[:, 1:2], ALU.add, ("bis",), ("bis",))
                        ts('dve', bis[:, 2:3], bis[:, 2:3], 0.5, None, ALU.mult, None, ("bis",), ("bis",))
                        P.op('dve', lambda e, Lk=Lk: e.tensor_scalar(out=mbt[:, 0:Lk], in0=scr[:, 0:Lk], scalar1=bis[:, 2:3], scalar2=0.0,
                                                                     op0=ALU.is_ge, op1=ALU.add, accum_out=bis[:, 3:4]), ("scr", "bis"), ("mbt", "bis"))
                        ts('dve', bis[:, 4:5], bis[:, 3:4], float(topk) - 0.5, None, ALU.is_gt, None, ("bis",), ("bis",))
                        tt('dve', bis[:, 5:6], bis[:, 2:3], bis[:, 0:1], ALU.subtract, ("bis",), ("bis",))
                        stt(bis[:, 0:1], bis[:, 5:6], bis[:, 4:5], bis[:, 0:1], ALU.mult, ALU.add, ("bis",), ("bis",))
                        tt('dve', bis[:, 5:6], bis[:, 1:2], bis[:, 2:3], ALU.subtract, ("bis",), ("bis",))
                        stt(bis[:, 1:2], bis[:, 5:6], bis[:, 4:5], bis[:, 2:3], ALU.mult, ALU.add, ("bis",), ("bis",))
                    thr = bis[:, 0:1]
                else:
                    memset('dve', bis[:, 0:1], -1.0e29, ("bis",))
                    thr = bis[:, 0:1]
                P.op('dve', lambda e, Lk=Lk: e.tensor_scalar(out=mbt[:, 0:Lk], in0=scr[:, 0:Lk], scalar1=bis[:, 0:1], scalar2=-30000.0,
                                                             op0=ALU.is_lt, op1=ALU.mult), ("scr", "bis"), ("mbt",))
                for kt in range(qt + 1):
                    bS = bank()
                    for h in range(4):
                        c = h // 2
                        mm(banks[bS][:, h * 128:(h + 1) * 128], KcT[:, c, kt * 128:(kt + 1) * 128], QcT[:, h % 2, c, tok], h == 0, False,
                           ("KcT", "QcT"), (f"ps{bS}",))
                        mm(banks[bS][:, h * 128:(h + 1) * 128], mbt[:, kt * 128:(kt + 1) * 128], identb[:], False, True,
                           ("mbt", "identb"), (f"ps{bS}",))
                    actf(PT[:, 0:512], banks[bS][:, :], AF.Exp, (f"ps{bS}",), ("PT",), scale=0.125)
                    for h in range(4):
                        mm(banks[6][:, h * 65:(h + 1) * 65], PT[:, h * 128:(h + 1) * 128], Vc[:, kt, h, :], (kt == 0 and h == 0), kt == qt,
                           ("PT", "Vc"), ("ps6",))
                cp('act', tmf[:, 0:4, :], banks[6][:, 0:260].rearrange("p (h d) -> p h d", d=65), ("ps6",), ("tmf",))
                recip(tms[:, 0:4], tmf[:, 0:4, 64], ("tmf",), ("tms",))
                tt('dve', otb[:].rearrange("p (h d) -> p h d", d=64), tmf[:, 0:4, 0:64], tms[:, 0:4].unsqueeze(2).to_broadcast([128, 4, 64]), ALU.mult,
                   ("tmf", "tms"), ("otb",))
                bt = bank()
                for c in range(2):
                    mm(banks[bt][:, c * 128:(c + 1) * 128], otb[:, c * 128:(c + 1) * 128], identb[:], c == 0, True, ("otb", "identb"), (f"ps{bt}",))
                cp('act', catT[:, 4:6, tok], banks[bt][:, 0:256].rearrange("p (c t) -> p c t", t=128), (f"ps{bt}",), ("catT",))

            ckpt(10)
            for half in range(2):
                wt, wr = load_w(("out", l, half), wout[l][:, half * 512:(half + 1) * 512], D, 512)
                for oc in range(4):
                    o = half * 4 + oc
                    b = bank()
                    for kc in range(KC):
                        mm(banks[b][:, :N], wt[:, kc, oc * 128:(oc + 1) * 128], catT[:, kc, :], kc == 0, kc == KC - 1, (wr, "catT"), (f"ps{b}",))
                    stt(xt[:, o, :], banks[b][:, :N], mod[:, 16 + o, 0:1], xt[:, o, :], ALU.mult, ALU.add, (f"ps{b}", "mod", "xt"), ("xt",))
            ckpt(11)
            norm_mod((A2, "A2"), 24)
            for c0 in range(0, FC, 4):
                ncg = min(4, FC - c0)
                wtg, wrg = load_w(("g", l, c0), wg[l][:, c0 * 128:(c0 + ncg) * 128], D, ncg * 128)
                wtu, wru = load_w(("u", l, c0), wu[l][:, c0 * 128:(c0 + ncg) * 128], D, ncg * 128)
                for ii in range(ncg):
                    c = c0 + ii
                    bg = bank(); bu = bank()
                    for kc in range(KC):
                        mm(banks[bg][:, :N], wtg[:, kc, ii * 128:(ii + 1) * 128], hT[:, kc, :], kc == 0, kc == KC - 1, (wrg, "hT"), (f"ps{bg}",))
                    for kc in range(KC):
                        mm(banks[bu][:, :N], wtu[:, kc, ii * 128:(ii + 1) * 128], hT[:, kc, :], kc == 0, kc == KC - 1, (wru, "hT"), (f"ps{bu}",))
                    fw = lambda k: vecs[:, V_FCW + c * 3 + k:V_FCW + c * 3 + k + 1]
                    cp('act', tB[:, 2:2 + N - 2] if False else tD[:], banks[bg][:, :N], (f"ps{bg}",), ("tD",))
                    ts('dve', tA[:, :N], tD[:], fw(2), vecs[:, V_FCB + c:V_FCB + c + 1], ALU.mult, ALU.add, ("tD", "vecs"), ("tA",))
                    stt(tA[:, 1:N], tD[:, 0:N - 1], fw(1), tA[:, 1:N], ALU.mult, ALU.add, ("tD", "vecs", "tA"), ("tA",))
                    stt(tA[:, 2:N], tD[:, 0:N - 2], fw(0), tA[:, 2:N], ALU.mult, ALU.add, ("tD", "vecs", "tA"), ("tA",))
                    stt(tA[:, 0:1], fcs[:, c, 1:2], fw(1), tA[:, 0:1], ALU.mult, ALU.add, ("fcs", "vecs", "tA"), ("tA",))
                    stt(tA[:, 0:2], fcs[:, c, 0:2], fw(0), tA[:, 0:2], ALU.mult, ALU.add, ("fcs", "vecs", "tA"), ("tA",))
                    cp('dve', fcs[:, c, :], tD[:, N - 2:N], ("tD", "tA"), ("fcs",))
                    actf(tA[:, :N], tA[:, :N], AF.Silu, ("tA",), ("tA",))
                    tt('dve', actt[:, c, :], tA[:, :N], banks[bu][:, :N], ALU.mult, ("tA", f"ps{bu}"), ("actt", "scr", "mbt"))
            if j == NB - 1:
                dma_sp(o_fc[l], fcs[:], ("fcs",), ("o_fc",))
            for o in range(KC):
                wt, wr = load_w(("d", l, o), wd[l][:, o * 128:(o + 1) * 128], DFF, 128)
                b = bank()
                for kc in range(FC):
                    mm(banks[b][:, :N], wt[:, kc, :], actt[:, kc, :], kc == 0, kc == FC - 1, (wr, "actt"), (f"ps{b}",))
                stt(xt[:, o, :], banks[b][:, :N], mod[:, 40 + o, 0:1], xt[:, o, :], ALU.mult, ALU.add, (f"ps{b}", "mod", "xt"), ("xt",))
            ckpt(12)
            if xdst is not None:
                dma_sp(xdst.rearrange("(k p) t -> p k t", p=128)[:, :, t0:t0 + N], xt[:], ("xt",), (f"xs{l}",))
            else:
                b = bank()
                for kc in range(KC):
                    actf(sqb[:], xt[:, kc, :], AF.Square, ("xt",), ("sqb",))
                    mm(banks[b][:, :N], onesb[:], sqb[:], kc == 0, kc == KC - 1, ("sqb", "onesb"), (f"ps{b}",))
                actf(rstd[:], banks[b][:, :N], AF.Sqrt, (f"ps{b}",), ("rstd",), scale=1.0 / D, bias=EPS)
                recip(rstd[:], rstd[:], ("rstd",), ("rstd",))
                for kc in range(KC):
                    stt(xt[:, kc, :], xt[:, kc, :], fng[:, kc:kc + 1], rstd[:], ALU.mult, ALU.mult, ("xt", "fng", "rstd"), ("xt",))
                dma_sp(yT.rearrange("(k p) t -> p k t", p=128)[:, :, t0:t0 + N], xt[:], ("xt",), ("yT",))
        if SAMPLE:
            for g_ in range(NG):
                sample_layer(l, lam_init, g_)
    P.emit()
    stack.close()
    return nc

IN_OFF = {}
_off = 0
for _n, _w in (("lru_x", 256), ("lru_gate", 256), ("diff_q", 256), ("diff_k", 256), ("diff_v", 256), ("dsa_q", 256), ("dsa_k", 256), ("dsa_v", 256),
               ("idx_q", 256), ("idx_k", 32), ("idx_w", 8), ("gla_q", 128), ("gla_k", 128), ("gla_v", 256), ("gla_g", 256), ("gla_a", 16)):
    IN_OFF[_n] = (_off, _off + _w); _off += _w

def _swap_cols(n, hd):
    i = np.arange(n)
    return (i // hd) * hd + ((i % hd) + hd // 2) % hd

def make_consts():
    s = np.arange(128)
    I = np.eye(128, dtype=np.float32)
    U = (s[:, None] <= s[None, :]).astype(np.float32)
    UT = U.T.copy()
    Lm = 1.0 - U
    NEG = (UT - 1.0) * 1.0e30
    HMk = np.zeros((128, 128), np.float32)
    for g in range(4):
        HMk[:, g] = (s // 32 == g)
    HMk[:, 4] = ((s // 32) % 2 == 0); HMk[:, 5] = ((s // 32) % 2 == 1); HMk[:, 6] = (s < 64); HMk[:, 7] = (s >= 64)
    return np.ascontiguousarray(np.stack([I, U, UT, Lm, NEG, HMk], axis=1).astype(np.float32))

def make_rope(pos):
    out = []
    for hd in (32, 64):
        half = hd // 2
        p = np.arange(128) % hd
        inv = (10000.0 ** (-(np.arange(half, dtype=np.float32)) / np.float32(half))).astype(np.float32)
        ang = pos.astype(np.float32)[None, :] * inv[p % half][:, None]
        cos = np.cos(ang).astype(np.float32); sin = np.sin(ang).astype(np.float32)
        sgn = np.where(p < half, -1.0, 1.0).astype(np.float32)[:, None]
        out += [cos, sin * sgn]
    return np.ascontiguousarray(np.stack(out, axis=0).astype(np.float32))

def prep_weights(inp, L):
    w = {}
    w_in = inp["w_in"]
    def cols(name): a, b = IN_OFF[name]; return w_in[:, :, a:b]
    def sw(x, hd): return x[:, :, _swap_cols(x.shape[2], hd)]
    ik = cols("idx_k"); iks = sw(ik, 32)
    ga = np.concatenate([cols("gla_a"), np.zeros((L, D, 112), np.float32)], axis=2)
    w["winfm"] = np.ascontiguousarray(np.concatenate([
        cols("lru_x"), cols("lru_gate"),
        cols("diff_q"), sw(cols("diff_q"), 32), cols("diff_k"), sw(cols("diff_k"), 32),
        cols("dsa_q"), sw(cols("dsa_q"), 64), cols("dsa_k"), sw(cols("dsa_k"), 64),
        cols("idx_q"), sw(cols("idx_q"), 32),
        cols("gla_q"), cols("gla_k"),
        ik, ik, ik, ik, iks, iks, iks, iks, ga], axis=2))
    w["wintm"] = np.ascontiguousarray(np.concatenate([cols("diff_v"), cols("dsa_v"), cols("gla_v"), cols("gla_g"), cols("gla_k"), cols("idx_w")], axis=2))
    w["wada"] = inp["w_ada"]; w["wout"] = inp["w_out"]; w["wg"] = inp["ffn_w_gate"]; w["wu"] = inp["ffn_w_up"]; w["wd"] = inp["ffn_w_down"]
    vecs = np.zeros((L, 128, NV), np.float32)
    def fm(v):
        return v.reshape(L, -1, 128).transpose(0, 2, 1)
    vecs[:, :, V_N1:V_N1 + 8] = fm(inp["norm1_g"]); vecs[:, :, V_N2:V_N2 + 8] = fm(inp["norm2_g"])
    cw = inp["lru_conv_w"]
    for c in range(2):
        for k in range(4):
            vecs[:, :, V_CW + c * 4 + k] = cw[:, k, c * 128:(c + 1) * 128]
    vecs[:, :, V_CB:V_CB + 2] = fm(inp["lru_conv_b"]); vecs[:, :, V_BA:V_BA + 2] = fm(inp["lru_ba"])
    vecs[:, :, V_BX:V_BX + 2] = fm(inp["lru_bx"]); vecs[:, :, V_LAM:V_LAM + 2] = fm(inp["lru_lambda"])
    fw = inp["ffn_conv_w"]
    for k in range(3):
        vecs[:, :, V_FCW + k:V_FCW + 66:3] = fm(fw[:, k, :])
    vecs[:, :, V_FCB:V_FCB + 22] = fm(inp["ffn_conv_b"]); vecs[:, :, V_BADA:V_BADA + 48] = fm(inp["b_ada"])
    w["vecs"] = vecs
    lruw = np.zeros((L, 128, 4, 128), np.float32)
    for c in range(2):
        for nl in range(2):
            n = 2 * c + nl
            lruw[:, nl * 64:(nl + 1) * 64, c, nl * 64:(nl + 1) * 64] = inp["lru_wa"][:, n]
            lruw[:, nl * 64:(nl + 1) * 64, 2 + c, nl * 64:(nl + 1) * 64] = inp["lru_wx"][:, n]
    w["lruw"] = lruw
    w["glaw"] = np.ascontiguousarray(np.concatenate([inp["gla_wa2"], inp["gla_ba"][:, None, :]], axis=1))
    w["bc"] = np.ascontiguousarray(np.concatenate([inp["diff_subln_g"], inp["gla_norm_g"], inp["diff_lq1"], inp["diff_lk1"], inp["diff_lq2"], inp["diff_lk2"]], axis=1)[:, None, :])
    w["fng"] = np.ascontiguousarray(inp["final_norm_g"].reshape(8, 128).T)
    w["consts"] = make_consts()
    return w

def unpack_prompt(res, L, T):
    o = {}
    o["y"] = res["yT"].T
    o["diff_k"] = res["o_kd"].transpose(0, 2, 1).reshape(L, T, 4, 2, 32)
    o["diff_v"] = res["o_vd"].reshape(L, T, 4, 64)
    o["dsa_k"] = res["o_kc"].transpose(0, 2, 1).reshape(L, T, 4, 64)
    o["dsa_v"] = res["o_vc"].reshape(L, T, 4, 64)
    o["idx_k"] = res["o_ik"].transpose(0, 2, 1)
    o["lru_h"] = res["o_lh"].transpose(0, 2, 1).reshape(L, 256)
    o["lru_conv"] = res["o_lc"].transpose(0, 3, 2, 1).reshape(L, 3, 256)
    o["gla"] = res["o_gs"].reshape(L, 4, 32, 64)
    o["ffn_conv"] = res["o_fc"].transpose(0, 3, 2, 1).reshape(L, 2, 2816)
    return o

def make_constS():
    p = np.arange(128)
    cS = np.zeros((128, 144), np.float32)
    cS[:, 0:128] = (p[:, None] % 16 == p[None, :] % 16)
    i = np.arange(4)
    cS[:, 128:132] = np.where((p[:, None] < 16) & (i[None, :] <= (p[:, None] % 4)), 0.0, -1.0e30)
    cS[:, 132] = p; cS[:, 133] = p % 32
    for bb in range(4):
        cS[:, 136 + bb] = ((p % 16) // 4 == bb)
    return cS

def prep_sample(inp, L, bsls, past_len):
    if isinstance(bsls, slice):
        bsls = [bsls]
    m = {}
    pos = np.tile(past_len + np.arange(4), 4)
    m["ropeS"] = make_rope(pos)
    m["constS"] = make_constS()
    acc = {k: [] for k in ("xsT", "pt", "st_lh", "st_lc", "st_gs", "st_fc")}
    for bsl in bsls:
        xs = inp["x_sample"][bsl].reshape(16, D)
        acc["xsT"].append(xs.T.reshape(8, 128, 16).transpose(1, 0, 2))
        acc["pt"].append(inp["page_table"][bsl].reshape(1, -1).astype(np.int32))
        acc["st_lh"].append(inp["state_lru_h"][:, bsl].reshape(L, 4, 2, 128).transpose(0, 3, 2, 1))
        acc["st_lc"].append(inp["state_lru_conv"][:, bsl].reshape(L, 4, 3, 2, 128).transpose(0, 4, 3, 1, 2))
        acc["st_gs"].append(inp["state_gla"][:, bsl].reshape(L, 4, 128, 64).transpose(0, 2, 1, 3))
        acc["st_fc"].append(inp["state_ffn_conv"][:, bsl].reshape(L, 4, 2, 22, 128).transpose(0, 4, 3, 1, 2))
    for k, v in acc.items():
        m[k] = np.ascontiguousarray(np.stack(v, axis=0))
    return m

def prep_pools(inp, L):
    npool = inp["cache_diff_k"].shape[1]
    m = {}
    m["pdkv"] = np.concatenate([inp["cache_diff_k"].reshape(L * npool * 128, 256), inp["cache_diff_v"].reshape(L * npool * 128, 256)], axis=1)
    m["pckv"] = np.concatenate([inp["cache_dsa_k"].reshape(L * npool * 128, 256), inp["cache_dsa_v"].reshape(L * npool * 128, 256)], axis=1)
    m["pik"] = np.ascontiguousarray(inp["cache_idx_k"].transpose(0, 1, 3, 2)).reshape(L * npool * 32, 128)
    return m

def unpack_sample(res, L, g=0):
    res = {k: v[g] for k, v in res.items() if k in ("ysT", "s_kd", "s_vd", "s_kc", "s_vc", "s_ik", "s_lh", "s_lc", "s_gs", "s_fc")}
    o = {}
    o["y"] = res["ysT"].transpose(2, 1, 0).reshape(4, 4, D)
    o["diff_k"] = res["s_kd"].transpose(0, 2, 1).reshape(L, 4, 4, 4, 2, 32)
    o["diff_v"] = res["s_vd"].reshape(L, 4, 4, 4, 64)
    o["dsa_k"] = res["s_kc"].transpose(0, 2, 1).reshape(L, 4, 4, 4, 64)
    o["dsa_v"] = res["s_vc"].reshape(L, 4, 4, 4, 64)
    o["idx_k"] = res["s_ik"].transpose(0, 2, 1).reshape(L, 4, 4, 32)
    o["lru_h"] = res["s_lh"].transpose(0, 3, 2, 1).reshape(L, 4, 256)
    o["lru_conv"] = res["s_lc"].transpose(0, 3, 4, 2, 1).reshape(L, 4, 3, 256)
    o["gla"] = res["s_gs"].transpose(0, 2, 1, 3).reshape(L, 4, 4, 32, 64)
    o["ffn_conv"] = res["s_fc"].transpose(0, 3, 4, 2, 1).reshape(L, 4, 2, 2816)
    return o


L_ = 2; T_ = 4096; NPG_ = 64; NPOOL_ = 2560; PAST_ = NPG_ * 128

def kernel(**inputs):
    inp = {k: np.asarray(v) for k, v in inputs.items()}
    L = L_; T = T_
    lam_inits = [0.8 - 0.6 * math.exp(-0.3 * l) for l in range(L)]
    topk = min(256, T // 4); topk_s = min(256, (PAST_ + 4) // 4)
    nc = bass.Bass("TRN2", target_bir_lowering=False)
    NCORE = 4; NG = 2
    build_program(nc, T, L, lam_inits, topk, dict(npg=NPG_, npool=NPOOL_, topk=topk_s, ng=NG))
    shared = dict(prep_weights(inp, L))
    shared["rope"] = make_rope(np.arange(T))
    shared.update(prep_pools(inp, L))
    in_maps = []
    for c in range(NCORE):
        m = dict(shared)
        pb = c
        bsls = [slice(4 * (NG * c + g), 4 * (NG * c + g) + 4) for g in range(NG)]
        m["xT"] = np.ascontiguousarray(inp["x_prompt"][pb].T)
        cc = np.concatenate([inp["c_prompt"][pb:pb + 1]] + [inp["c_sample"][b_] for b_ in bsls], axis=0)
        m["cT"] = np.ascontiguousarray(cc.reshape(1 + 4 * NG, 8, 128).transpose(2, 1, 0))
        m.update(prep_sample(inp, L, bsls, PAST_))
        in_maps.append(m)
    res = run_bass_kernel_spmd(nc, in_maps, core_ids=list(range(NCORE)))
    R = res.results
    P = [unpack_prompt(R[c], L, T) for c in range(4)]
    S = [unpack_sample(R[c], L, g) for c in range(NCORE) for g in range(NG)]
    f32 = np.float32
    y_prompt = np.stack([P[c]["y"] for c in range(4)]).astype(f32)
    y_sample = np.concatenate([S[c]["y"] for c in range(8)], axis=0).astype(f32)
    keys = ("diff_k", "diff_v", "dsa_k", "dsa_v", "idx_k", "lru_h", "lru_conv", "gla", "ffn_conv")
    Pout = [np.ascontiguousarray(np.stack([P[c][k] for c in range(4)], axis=1)).astype(f32) for k in keys]
    Sout = [np.ascontiguousarray(np.concatenate([S[c][k] for c in range(8)], axis=1)).astype(f32) for k in keys]
    return (y_prompt, y_sample, *Pout, *Sout)
```
